# Optimizing a Trainium2 kernel written in Bass

```python
import math
import jax, jax.numpy as jnp
from jax import lax
import numpy as np

D_MODEL = 1024
BATCH = 8
SEQ = 8192
DEPTH = 2
DEC_BATCH = 16
DEC_SEQ = 32
PAST_LEN = 4096

CHUNK = 64
EPS = 1e-6
N_EVEN = (DEPTH + 1) // 2
N_ODD = DEPTH // 2
MIX_WIDTH = D_MODEL
A_HEADS = 4
A_DK = D_MODEL // 8
A_DV = D_MODEL // 8
A_WIDTH = A_HEADS * A_DV
B_HEADS = 4
B_DK = D_MODEL // 8
B_DV = D_MODEL // 8
B_WIDTH = B_HEADS * B_DV
C_WIDTH = 3 * D_MODEL // 4
C_BLOCK = 64
C_BLOCKS = C_WIDTH // C_BLOCK
CONV_W = 4
RG_C = 8.0
S5_CH = 16
S5_STATE = 64
D_WIDTH = D_MODEL // 4
S5_GROUPS = D_WIDTH // S5_CH
FFN_DIM = ((8 * D_MODEL // 3 + 255) // 256) * 256
AB_SIZES = (A_WIDTH, A_WIDTH, A_WIDTH, A_WIDTH, A_HEADS, A_HEADS,
            B_HEADS * B_DK, B_HEADS * B_DK, B_WIDTH, B_WIDTH)
AB_IN = sum(AB_SIZES)
CD_SIZES = (C_WIDTH, C_WIDTH, D_WIDTH)
CD_IN = sum(CD_SIZES)

kernel_name = 'hybrid_streaming_mlstm_hgrn2_rglru_s5_step'


def rmsnorm(x, g):
    xf = x.astype(jnp.float32)
    y = xf * lax.rsqrt(jnp.mean(xf * xf, axis=-1, keepdims=True) + EPS)
    return (y * g.astype(jnp.float32)).astype(x.dtype)


def head_rmsnorm(h, g):
    bn, L, H, d = h.shape
    hf = h.astype(jnp.float32)
    y = hf * lax.rsqrt(jnp.mean(hf * hf, axis=-1, keepdims=True) + EPS)
    return y.reshape(bn, L, H * d) * g.astype(jnp.float32)


def swiglu(x, wg, wu, wd):
    return (jax.nn.silu(x @ wg) * (x @ wu)) @ wd


def _split(z, sizes):
    cuts = [int(c) for c in np.cumsum(sizes)[:-1]]
    return jnp.split(z, cuts, axis=-1)


def _chunk_len(L):
    return CHUNK if L % CHUNK == 0 else L


def _to_chunks(t, c):
    bn, L = t.shape[:2]
    return jnp.moveaxis(t.reshape((bn, L // c, c) + t.shape[2:]), 1, 0)


def _from_chunks(t):
    nc, bn, c = t.shape[:3]
    return jnp.moveaxis(t, 0, 1).reshape((bn, nc * c) + t.shape[3:])


def mlstm_chunkwise(q, k, v, ig, lf, C0, n0, m0):
    L = q.shape[1]
    c = _chunk_len(L)
    tril = jnp.tril(jnp.ones((c, c), dtype=bool))

    def step(carry, blk):
        C, n, m = carry
        qb, kb, vb, ib, fb = blk
        F = jnp.swapaxes(jnp.cumsum(fb, axis=1), 1, 2)
        it = jnp.swapaxes(ib, 1, 2)
        log_src = jnp.where(tril, F[..., :, None] - F[..., None, :] + it[..., None, :], -jnp.inf)
        log_prev = F + m[..., None]
        m_t = jnp.maximum(log_prev, jnp.max(log_src, axis=-1))
        w_src = jnp.exp(log_src - m_t[..., None])
        w_prev = jnp.exp(log_prev - m_t)
        s = jnp.einsum('bthd,bshd->bhts', qb, kb) * w_src
        num = (jnp.einsum('bhts,bshv->bthv', s, vb)
               + jnp.swapaxes(w_prev, 1, 2)[..., None] * jnp.einsum('bthd,bhdv->bthv', qb, C))
        den = jnp.sum(s, axis=-1) + w_prev * jnp.einsum('bthd,bhd->bht', qb, n)
        den = jnp.maximum(jnp.abs(den), jnp.exp(-m_t))
        h = num / jnp.swapaxes(den, 1, 2)[..., None]
        m_new = m_t[..., -1]
        decay_prev = jnp.exp(F[..., -1] + m - m_new)
        w_end = jnp.exp(F[..., -1:] - F + it - m_new[..., None])
        C = decay_prev[..., None, None] * C + jnp.einsum('bhs,bshd,bshv->bhdv', w_end, kb, vb)
        n = decay_prev[..., None] * n + jnp.einsum('bhs,bshd->bhd', w_end, kb)
        return (C, n, m_new), h

    blocks = (_to_chunks(q, c), _to_chunks(k, c), _to_chunks(v, c), _to_chunks(ig, c), _to_chunks(lf, c))
    (C, n, m), h = lax.scan(step, (C0, n0, m0), blocks)
    return _from_chunks(h), C, n, m


def mlstm_mixer(aq, ak, av, ao, ai, af, C0, n0, m0, norm_g):
    bn, L, _ = aq.shape
    f32 = jnp.float32
    q = aq.astype(f32).reshape(bn, L, A_HEADS, A_DK) * (A_DK ** -0.5)
    k = ak.astype(f32).reshape(bn, L, A_HEADS, A_DK)
    v = av.astype(f32).reshape(bn, L, A_HEADS, A_DV)
    ig = ai.astype(f32)
    lf = jax.nn.log_sigmoid(af.astype(f32))
    h, C, n, m = mlstm_chunkwise(q, k, v, ig, lf, C0.astype(f32), n0.astype(f32), m0.astype(f32))
    out = head_rmsnorm(h, norm_g) * jax.nn.sigmoid(ao.astype(f32))
    return out, C, n, m


def hgrn2_chunkwise(q, kk, lf, iv, S0):
    L = q.shape[1]
    c = _chunk_len(L)
    tril = jnp.tril(jnp.ones((c, c), dtype=bool))

    def step(S, blk):
        qb, kb, fb, ib = blk
        G = jnp.cumsum(fb, axis=1)
        q_dec = qb * jnp.exp(G)
        k_inv = kb * jnp.exp(-G)
        att = jnp.where(tril, jnp.einsum('bthd,bshd->bhts', q_dec, k_inv), 0.0)
        o = jnp.einsum('bhts,bshv->bthv', att, ib) + jnp.einsum('bthd,bhdv->bthv', q_dec, S)
        G_end = G[:, -1]
        k_end = kb * jnp.exp(G_end[:, None] - G)
        S = jnp.exp(G_end)[..., None] * S + jnp.einsum('bshd,bshv->bhdv', k_end, ib)
        return S, o

    blocks = (_to_chunks(q, c), _to_chunks(kk, c), _to_chunks(lf, c), _to_chunks(iv, c))
    S, o = lax.scan(step, S0, blocks)
    return _from_chunks(o), S


def hgrn2_mixer(bq, bf, bi, bg, S0, lb, norm_g):
    bn, L, _ = bq.shape
    f32 = jnp.float32
    q = jax.nn.silu(bq.astype(f32)).reshape(bn, L, B_HEADS, B_DK)
    zf = bf.astype(f32).reshape(bn, L, B_HEADS, B_DK)
    lbh = lb.reshape(B_HEADS, B_DK)
    lf = jnp.log(lbh + (1.0 - lbh) * jax.nn.sigmoid(zf))
    kk = (1.0 - lbh) * jax.nn.sigmoid(-zf)
    iv = bi.astype(f32).reshape(bn, L, B_HEADS, B_DV)
    o, S = hgrn2_chunkwise(q, kk, lf, iv, S0.astype(f32))
    out = head_rmsnorm(o, norm_g) * jax.nn.silu(bg.astype(f32))
    return out, S


def causal_dwconv(u, buf, w, b):
    L = u.shape[1]
    ext = jnp.concatenate([buf.astype(u.dtype), u], axis=1)
    y = b + sum(ext[:, j:j + L] * w[j] for j in range(CONV_W))
    return y, ext[:, ext.shape[1] - (CONV_W - 1):]


def _real_combine(l, r):
    return (l[0] * r[0], r[0] * l[1] + r[1])


def _complex_combine(l, r):
    lar, lai, lbr, lbi = l
    rar, rai, rbr, rbi = r
    return (rar * lar - rai * lai, rar * lai + rai * lar,
            rar * lbr - rai * lbi + rbr, rar * lbi + rai * lbr + rbi)


def rglru(u, h0, w_a, b_a, w_x, b_x, lam):
    bn, L, W = u.shape
    f32 = jnp.float32
    uf = u.astype(f32)
    ub = uf.reshape(bn, L, C_BLOCKS, C_BLOCK)
    r = jax.nn.sigmoid(jnp.einsum('blnd,nde->blne', ub, w_a.astype(f32)).reshape(bn, L, W) + b_a.astype(f32))
    i = jax.nn.sigmoid(jnp.einsum('blnd,nde->blne', ub, w_x.astype(f32)).reshape(bn, L, W) + b_x.astype(f32))
    log_a = RG_C * r * jax.nn.log_sigmoid(lam.astype(f32))
    a = jnp.exp(log_a)
    bterm = jnp.sqrt(-jnp.expm1(2.0 * log_a)) * (i * uf)
    bterm = bterm.at[:, 0].add(a[:, 0] * h0.astype(f32))
    _, h = lax.associative_scan(_real_combine, (a, bterm), axis=1)
    return h, h[:, -1]


def s5_mixer(u, h_re, h_im, A_re, A_im, log_dt, B_re, B_im, C_re, C_im, d_skip, w_glu, b_glu):
    bn, L, _ = u.shape
    f32 = jnp.float32
    uf = u.astype(f32).reshape(bn, L, S5_GROUPS, S5_CH)
    A_re = A_re.astype(f32)
    A_im = A_im.astype(f32)
    dt = jnp.exp(log_dt.astype(f32))[:, None]
    mag = jnp.exp(dt * A_re)
    ab_re = mag * jnp.cos(dt * A_im)
    ab_im = mag * jnp.sin(dt * A_im)
    inv = 1.0 / (A_re * A_re + A_im * A_im)
    z_re = ((ab_re - 1.0) * A_re + ab_im * A_im) * inv
    z_im = (ab_im * A_re - (ab_re - 1.0) * A_im) * inv
    B_re = B_re.astype(f32)
    B_im = B_im.astype(f32)
    bb_re = z_re[..., None] * B_re - z_im[..., None] * B_im
    bb_im = z_re[..., None] * B_im + z_im[..., None] * B_re
    bu_re = jnp.einsum('blgc,gpc->blgp', uf, bb_re)
    bu_im = jnp.einsum('blgc,gpc->blgp', uf, bb_im)
    h_re = h_re.astype(f32)
    h_im = h_im.astype(f32)
    bu_re = bu_re.at[:, 0].add(ab_re * h_re - ab_im * h_im)
    bu_im = bu_im.at[:, 0].add(ab_re * h_im + ab_im * h_re)
    shp = bu_re.shape
    _, _, s_re, s_im = lax.associative_scan(
        _complex_combine, (jnp.broadcast_to(ab_re, shp), jnp.broadcast_to(ab_im, shp), bu_re, bu_im), axis=1)
    y = (jnp.einsum('gcp,blgp->blgc', C_re.astype(f32), s_re)
         - jnp.einsum('gcp,blgp->blgc', C_im.astype(f32), s_im)
         + d_skip.astype(f32).reshape(S5_GROUPS, S5_CH) * uf)
    y = jax.nn.gelu(y.reshape(bn, L, D_WIDTH))
    y = y * jax.nn.sigmoid(y @ w_glu.astype(f32) + b_glu.astype(f32))
    return y, s_re[:, -1], s_im[:, -1]


def run_trunk(x, states, W):
    mC, mn, mm, hS, rh, rc, sre, sim = states
    f32 = jnp.float32
    lb_all = jnp.cumsum(jax.nn.softmax(W['hgrn_lb_logits'].astype(f32), axis=0), axis=0)
    outs = ([], [], [], [], [], [], [], [])
    for l in range(DEPTH):
        x = x + 0.5 * swiglu(rmsnorm(x, W['norm_g'][l, 0]), W['ffn_w_gate'][l, 0], W['ffn_w_up'][l, 0], W['ffn_w_down'][l, 0])
        xn = rmsnorm(x, W['norm_g'][l, 1])
        j = l // 2
        if l % 2 == 0:
            z = xn @ W['ab_w_in'][j] + W['ab_b_in'][j]
            aq, ak, av, ao, ai, af, bq, bf, bi, bg = _split(z, AB_SIZES)
            a_out, C1, n1, m1 = mlstm_mixer(aq, ak, av, ao, ai, af, mC[j], mn[j], mm[j], W['mlstm_norm_g'][j])
            b_out, S1 = hgrn2_mixer(bq, bf, bi, bg, hS[j], lb_all[l], W['hgrn_norm_g'][j])
            mix = jnp.concatenate([a_out, b_out], axis=-1) @ W['ab_w_out'][j]
            for lst, val in zip(outs[:4], (C1, n1, m1, S1)):
                lst.append(val)
        else:
            z = xn @ W['cd_w_in'][j] + W['cd_b_in'][j]
            cg, cr, du = _split(z, CD_SIZES)
            u, buf1 = causal_dwconv(cr, rc[j], W['conv_w'][j], W['conv_b'][j])
            h, h1 = rglru(u, rh[j], W['rg_w_a'][j], W['rg_b_a'][j], W['rg_w_x'][j], W['rg_b_x'][j], W['rg_lambda'][j])
            c_out = jax.nn.gelu(cg.astype(f32)) * h
            d_out, s1re, s1im = s5_mixer(du, sre[j], sim[j], W['s5_A_re'][j], W['s5_A_im'][j], W['s5_log_dt'][j],
                                         W['s5_B_re'][j], W['s5_B_im'][j], W['s5_C_re'][j], W['s5_C_im'][j],
                                         W['s5_D'][j], W['s5_w_glu'][j], W['s5_b_glu'][j])
            mix = jnp.concatenate([c_out, d_out], axis=-1) @ W['cd_w_out'][j]
            for lst, val in zip(outs[4:], (h1, buf1, s1re, s1im)):
                lst.append(val)
        x = x + mix.astype(x.dtype)
        x = x + 0.5 * swiglu(rmsnorm(x, W['norm_g'][l, 2]), W['ffn_w_gate'][l, 1], W['ffn_w_up'][l, 1], W['ffn_w_down'][l, 1])
    y = rmsnorm(x, W['final_norm_g'])
    return y, [jnp.stack(lst) for lst in outs]


def setup_inputs(seed: int = 0) -> dict:
    key = jax.random.key(seed)
    kit = iter(jax.random.split(key, 64))
    f32 = jnp.float32

    def nrm(shape, scale=1.0):
        return jax.random.normal(next(kit), shape, f32) * scale

    def unif(shape, lo, hi):
        return jax.random.uniform(next(kit), shape, f32, minval=lo, maxval=hi)

    x_prompt = nrm((BATCH, SEQ, D_MODEL))
    x_sample = nrm((DEC_BATCH, DEC_SEQ, D_MODEL))
    state_mlstm_C = nrm((N_EVEN, DEC_BATCH, A_HEADS, A_DK, A_DV), 0.5)
    state_mlstm_n = jnp.abs(nrm((N_EVEN, DEC_BATCH, A_HEADS, A_DK), 0.5))
    state_mlstm_m = nrm((N_EVEN, DEC_BATCH, A_HEADS))
    state_hgrn_S = nrm((N_EVEN, DEC_BATCH, B_HEADS, B_DK, B_DV), 0.5)
    state_rglru_h = nrm((N_ODD, DEC_BATCH, C_WIDTH), 0.5)
    state_rglru_conv = nrm((N_ODD, DEC_BATCH, CONV_W - 1, C_WIDTH))
    state_s5_re = nrm((N_ODD, DEC_BATCH, S5_GROUPS, S5_STATE), 0.1)
    state_s5_im = nrm((N_ODD, DEC_BATCH, S5_GROUPS, S5_STATE), 0.1)

    norm_g = 1.0 + nrm((DEPTH, 3, D_MODEL), 0.02)
    final_norm_g = 1.0 + nrm((D_MODEL,), 0.02)
    ffn_w_gate = nrm((DEPTH, 2, D_MODEL, FFN_DIM), D_MODEL ** -0.5)
    ffn_w_up = nrm((DEPTH, 2, D_MODEL, FFN_DIM), D_MODEL ** -0.5)
    ffn_w_down = nrm((DEPTH, 2, FFN_DIM, D_MODEL), FFN_DIM ** -0.5)

    ab_w_in = nrm((N_EVEN, D_MODEL, AB_IN), D_MODEL ** -0.5)
    f_off = 4 * A_WIDTH + A_HEADS
    ab_b_in = nrm((N_EVEN, AB_IN), 0.02).at[:, f_off:f_off + A_HEADS].add(jnp.linspace(3.0, 6.0, A_HEADS))
    mlstm_norm_g = 1.0 + nrm((N_EVEN, A_WIDTH), 0.02)
    hgrn_norm_g = 1.0 + nrm((N_EVEN, B_WIDTH), 0.02)
    hgrn_lb_logits = nrm((DEPTH + 1, B_HEADS * B_DK), 0.1)
    ab_w_out = nrm((N_EVEN, MIX_WIDTH, D_MODEL), MIX_WIDTH ** -0.5)

    cd_w_in = nrm((N_ODD, D_MODEL, CD_IN), D_MODEL ** -0.5)
    cd_b_in = nrm((N_ODD, CD_IN), 0.02)
    conv_w = nrm((N_ODD, CONV_W, C_WIDTH), CONV_W ** -0.5)
    conv_b = nrm((N_ODD, C_WIDTH), 0.02)
    rg_w_a = nrm((N_ODD, C_BLOCKS, C_BLOCK, C_BLOCK), C_BLOCK ** -0.5)
    rg_b_a = nrm((N_ODD, C_WIDTH), 0.02)
    rg_w_x = nrm((N_ODD, C_BLOCKS, C_BLOCK, C_BLOCK), C_BLOCK ** -0.5)
    rg_b_x = nrm((N_ODD, C_WIDTH), 0.02)
    a0 = unif((N_ODD, C_WIDTH), 0.9, 0.999)
    s0 = a0 ** (1.0 / RG_C)
    rg_lambda = jnp.log(s0) - jnp.log1p(-s0)

    s5_A_re = -0.5 + nrm((N_ODD, S5_GROUPS, S5_STATE), 0.01)
    s5_A_im = math.pi * jnp.arange(S5_STATE, dtype=f32) + nrm((N_ODD, S5_GROUPS, S5_STATE), 0.01)
    s5_log_dt = unif((N_ODD, S5_GROUPS), math.log(1e-3), math.log(1e-1))
    s5_B_re = nrm((N_ODD, S5_GROUPS, S5_STATE, S5_CH), (2 * S5_CH) ** -0.5)
    s5_B_im = nrm((N_ODD, S5_GROUPS, S5_STATE, S5_CH), (2 * S5_CH) ** -0.5)
    s5_C_re = nrm((N_ODD, S5_GROUPS, S5_CH, S5_STATE), (2 * S5_STATE) ** -0.5)
    s5_C_im = nrm((N_ODD, S5_GROUPS, S5_CH, S5_STATE), (2 * S5_STATE) ** -0.5)
    s5_D = nrm((N_ODD, D_WIDTH))
    s5_w_glu = nrm((N_ODD, D_WIDTH, D_WIDTH), D_WIDTH ** -0.5)
    s5_b_glu = nrm((N_ODD, D_WIDTH), 0.02)
    cd_w_out = nrm((N_ODD, MIX_WIDTH, D_MODEL), MIX_WIDTH ** -0.5)

    return {'x_prompt': x_prompt, 'x_sample': x_sample,
            'state_mlstm_C': state_mlstm_C, 'state_mlstm_n': state_mlstm_n, 'state_mlstm_m': state_mlstm_m,
            'state_hgrn_S': state_hgrn_S, 'state_rglru_h': state_rglru_h, 'state_rglru_conv': state_rglru_conv,
            'state_s5_re': state_s5_re, 'state_s5_im': state_s5_im,
            'norm_g': norm_g, 'final_norm_g': final_norm_g,
            'ffn_w_gate': ffn_w_gate, 'ffn_w_up': ffn_w_up, 'ffn_w_down': ffn_w_down,
            'ab_w_in': ab_w_in, 'ab_b_in': ab_b_in, 'mlstm_norm_g': mlstm_norm_g, 'hgrn_norm_g': hgrn_norm_g,
            'hgrn_lb_logits': hgrn_lb_logits, 'ab_w_out': ab_w_out,
            'cd_w_in': cd_w_in, 'cd_b_in': cd_b_in, 'conv_w': conv_w, 'conv_b': conv_b,
            'rg_w_a': rg_w_a, 'rg_b_a': rg_b_a, 'rg_w_x': rg_w_x, 'rg_b_x': rg_b_x, 'rg_lambda': rg_lambda,
            's5_A_re': s5_A_re, 's5_A_im': s5_A_im, 's5_log_dt': s5_log_dt,
            's5_B_re': s5_B_re, 's5_B_im': s5_B_im, 's5_C_re': s5_C_re, 's5_C_im': s5_C_im,
            's5_D': s5_D, 's5_w_glu': s5_w_glu, 's5_b_glu': s5_b_glu, 'cd_w_out': cd_w_out}


def reference(x_prompt, x_sample, state_mlstm_C, state_mlstm_n, state_mlstm_m, state_hgrn_S,
              state_rglru_h, state_rglru_conv, state_s5_re, state_s5_im,
              norm_g, final_norm_g, ffn_w_gate, ffn_w_up, ffn_w_down,
              ab_w_in, ab_b_in, mlstm_norm_g, hgrn_norm_g, hgrn_lb_logits, ab_w_out,
              cd_w_in, cd_b_in, conv_w, conv_b, rg_w_a, rg_b_a, rg_w_x, rg_b_x, rg_lambda,
              s5_A_re, s5_A_im, s5_log_dt, s5_B_re, s5_B_im, s5_C_re, s5_C_im, s5_D, s5_w_glu, s5_b_glu,
              cd_w_out):
    W = dict(norm_g=norm_g, final_norm_g=final_norm_g, ffn_w_gate=ffn_w_gate, ffn_w_up=ffn_w_up,
             ffn_w_down=ffn_w_down, ab_w_in=ab_w_in, ab_b_in=ab_b_in, mlstm_norm_g=mlstm_norm_g,
             hgrn_norm_g=hgrn_norm_g, hgrn_lb_logits=hgrn_lb_logits, ab_w_out=ab_w_out,
             cd_w_in=cd_w_in, cd_b_in=cd_b_in, conv_w=conv_w, conv_b=conv_b,
             rg_w_a=rg_w_a, rg_b_a=rg_b_a, rg_w_x=rg_w_x, rg_b_x=rg_b_x, rg_lambda=rg_lambda,
             s5_A_re=s5_A_re, s5_A_im=s5_A_im, s5_log_dt=s5_log_dt, s5_B_re=s5_B_re, s5_B_im=s5_B_im,
             s5_C_re=s5_C_re, s5_C_im=s5_C_im, s5_D=s5_D, s5_w_glu=s5_w_glu, s5_b_glu=s5_b_glu,
             cd_w_out=cd_w_out)
    f32 = jnp.float32
    bp = x_prompt.shape[0]
    zero_states = (jnp.zeros((N_EVEN, bp, A_HEADS, A_DK, A_DV), f32),
                   jnp.zeros((N_EVEN, bp, A_HEADS, A_DK), f32),
                   jnp.zeros((N_EVEN, bp, A_HEADS), f32),
                   jnp.zeros((N_EVEN, bp, B_HEADS, B_DK, B_DV), f32),
                   jnp.zeros((N_ODD, bp, C_WIDTH), f32),
                   jnp.zeros((N_ODD, bp, CONV_W - 1, C_WIDTH), x_prompt.dtype),
                   jnp.zeros((N_ODD, bp, S5_GROUPS, S5_STATE), f32),
                   jnp.zeros((N_ODD, bp, S5_GROUPS, S5_STATE), f32))
    y_prompt, p_st = run_trunk(x_prompt, zero_states, W)
    sample_states = (state_mlstm_C, state_mlstm_n, state_mlstm_m, state_hgrn_S,
                     state_rglru_h, state_rglru_conv, state_s5_re, state_s5_im)
    y_sample, s_st = run_trunk(x_sample, sample_states, W)
    p_mC, p_mn, p_mm, p_hS, p_rh, p_rc, p_sre, p_sim = p_st
    s_mC, s_mn, s_mm, s_hS, s_rh, s_rc, s_sre, s_sim = s_st
    return (y_prompt, y_sample, p_mC, p_mn, p_mm, p_hS, p_rh, p_rc, p_sre, p_sim,
            s_mC, s_mn, s_mm, s_hS, s_rh, s_rc, s_sre, s_sim)
```

```python
import numpy as np
import concourse.bass as bass
import concourse.mybir as mybir

F32 = mybir.dt.float32
BF16 = mybir.dt.bfloat16
AF = mybir.ActivationFunctionType
ALU = mybir.AluOpType
PAGE = 64
LAT_X = 1.2
LAT_S = 0.25
NDMASEM = 24
DT_SIZE = {F32: 4, BF16: 2}


class Buf:
    def __init__(self, name, th, space, addr, nbytes, shape, dtype):
        self.name, self.th, self.space, self.addr, self.nbytes = name, th, space, addr, nbytes
        self.shape, self.dtype = shape, dtype

    def __getitem__(self, k):
        return self.th[k]

    def res(self, lo=None, hi=None):
        if self.space == 'ps':
            return [('ps', self.addr)]
        if self.space == 'dram':
            return [('dram', self.name)]
        lo = 0 if lo is None else lo
        hi = self.nbytes if hi is None else hi
        return [('sb', p) for p in range((self.addr + lo) // PAGE, (self.addr + hi - 1) // PAGE + 1)]

    def d(self, key):
        return DR(self.name, key)

    def ch(self, j, n=1):
        per = self.nbytes // self.shape[1]
        return R(self, j * per, (j + n) * per)


class BV:
    def __init__(self, buf, off):
        self.buf, self.off = buf, off

    def __getitem__(self, key):
        rows, cols = key
        a = (cols.start or 0) + self.off
        b = cols.stop + self.off
        return self.buf[rows, a:b]

    def res(self):
        return self.buf.res()


class R:
    def __init__(self, buf, lo, hi):
        self.buf, self.lo, self.hi = buf, lo, hi

    def res(self):
        return self.buf.res(self.lo, self.hi)


class DR:
    def __init__(self, name, key):
        self.k = ('dram', name, key)

    def res(self):
        return [self.k]


def _res(x):
    return x.res()


class _Rec:
    def __init__(self):
        self.call = None

    def __getattr__(self, name):
        def f(*a, **kw):
            self.call = (name, a, kw)
            return None
        return f


def _dur(call, eng, is_dma):
    try:
        name, a, kw = call
        out = kw.get('out', a[0] if a else None)
        n = 1
        for d in out.shape[1:]:
            n *= d
        if is_dma:
            return 128 * n * 3 / 250e3
        if eng == 'pe':
            rhs = a[2] if len(a) > 2 else kw.get('rhs')
            m = 1
            for d in rhs.shape[1:]:
                m *= d
            return max(0.07, m * 0.00042 * (4 if rhs.dtype == F32 else 1))
        if name == 'tensor_tensor_scan':
            return 0.07 + 2 * n * 0.00105
        if eng == 'act':
            return 0.15 + n * 0.00085
        if eng == 'pool':
            return 0.1 + n * 0.01
        return 0.07 + n * 0.00105
    except Exception:
        return 0.3


def _capture(fn):
    if fn is None or isinstance(fn, tuple):
        return fn
    p = _Rec()
    fn(p)
    assert p.call is not None
    return p.call


class Op:
    __slots__ = ('eng', 'fn', 'is_dma', 'seq', 'waits', 'signal', 'semval', 'dsem', 'gid', 't_end')

    def __init__(self, eng, fn, is_dma):
        self.eng, self.fn, self.is_dma = eng, fn, is_dma
        self.waits = []
        self.signal = False
        self.semval = None
        self.dsem = None


class K:
    ENGS = ['pe', 'act', 'dve', 'pool', 'sp']

    def __init__(self, nc):
        self.nc = nc
        self.ops = {e: [] for e in self.ENGS}
        self.writers = {}
        self.readers = {}
        self.seen = {e: {} for e in self.ENGS}
        self.seen_dma = {e: set() for e in self.ENGS}
        self.sb_off = 16640
        self.ps_banks = 0
        self.ndma = 0
        self.dma_ring = [None] * NDMASEM
        self.out_dmas = []
        self.nops = 0
        self.rec = None
        self.eng_free = {e: 0.0 for e in self.ENGS}

    def sb(self, name, shape, dtype, at=None):
        nbytes = int(np.prod(shape[1:])) * DT_SIZE[dtype]
        if at is None:
            addr = (self.sb_off + 63) // 64 * 64
            self.sb_off = addr + nbytes
        else:
            addr = at
        assert addr + nbytes <= 221000, (name, addr, nbytes)
        th = self.nc.alloc_sbuf_tensor_at(name, list(shape), dtype, offset=addr)
        return Buf(name, th, 'sb', addr, nbytes, shape, dtype)

    def ps(self, name, shape=(128, 512), dtype=F32):
        th = self.nc.alloc_psum_tensor(name, list(shape), dtype)
        b = Buf(name, th, 'ps', self.ps_banks, 2048, shape, dtype)
        self.ps_banks += 1
        return b

    def dram(self, name, shape, dtype, kind="Internal"):
        th = self.nc.dram_tensor(name, list(shape), dtype, kind=kind)
        return Buf(name, th, 'dram', 0, 0, shape, dtype)

    def _dep(self, op, d, selfsync):
        if d is op:
            return
        if d.is_dma:
            if id(d) in self.seen_dma[op.eng]:
                return
            self.seen_dma[op.eng].add(id(d))
            op.waits.append(d)
            return
        if d.eng == op.eng and not op.is_dma:
            if not selfsync or op.eng == 'pe':
                return
        s = self.seen[op.eng].get(d.eng, -1)
        if d.seq <= s:
            return
        self.seen[op.eng][d.eng] = d.seq
        d.signal = True
        op.waits.append(d)

    def begin_thread(self):
        self.rec = []

    def end_thread(self):
        r = self.rec
        self.rec = None
        return r

    def replay(self, item):
        eng, fn, reads, writes, is_dma, selfsync, acc, is_out = item
        o = self.op(eng, fn, reads, writes, is_dma=is_dma, selfsync=selfsync, acc=acc)
        if is_out:
            self.out_dmas.append(o)

    def _peek(self, item):
        eng, fn, reads, writes, is_dma, selfsync, acc, is_out = item
        t = 0.0
        W, Rd = self.writers, self.readers

        def upd(d, t):
            tt = d.t_end + (LAT_S if (d.eng == eng and not d.is_dma) else LAT_X)
            return tt if tt > t else t
        for r in reads:
            for kx in r.res():
                for d in W.get(kx, {}).values():
                    t = upd(d, t)
        if not acc:
            for w in writes:
                for kx in w.res():
                    for d in W.get(kx, {}).values():
                        t = upd(d, t)
                    for d in Rd.get(kx, {}).values():
                        t = upd(d, t)
        return max(self.eng_free[eng], t)

    def merge(self, *threads):
        threads = [t for t in threads if t]
        if len(threads) == 1:
            for it in threads[0]:
                self.replay(it)
            return
        ptr = [0] * len(threads)
        n = [len(t) for t in threads]
        total = sum(n)
        for _ in range(total):
            best, bt = None, None
            for i, t in enumerate(threads):
                if ptr[i] >= n[i]:
                    continue
                est = self._peek(t[ptr[i]])
                if bt is None or est < bt:
                    best, bt = i, est
            self.replay(threads[best][ptr[best]])
            ptr[best] += 1

    def op(self, eng, fn, reads=(), writes=(), is_dma=False, selfsync=True, acc=False, _is_out=False):
        fn = _capture(fn)
        if getattr(self, 'rec', None) is not None:
            self.rec.append((eng, fn, list(reads), list(writes), is_dma, selfsync, acc, _is_out))
            return None
        o = Op(eng, fn, is_dma)
        o.seq = len(self.ops[eng])
        o.gid = self.nops
        self.nops += 1
        deps = []
        rkeys = [k for r in reads for k in _res(r)]
        wkeys = [k for w in writes for k in _res(w)]
        for k in rkeys:
            deps.extend(self.writers.get(k, {}).values())
        raw_ids = set(id(d) for d in deps)
        for k in wkeys:
            if acc:
                continue
            deps.extend(self.writers.get(k, {}).values())
            deps.extend(self.readers.get(k, {}).values())
        if not is_dma:
            deps = [d for d in deps if id(d) in raw_ids or d.is_dma or d.eng != eng]
        t_ready = 0.0
        for d in deps:
            tt = d.t_end + (LAT_S if (d.eng == eng and not d.is_dma) else LAT_X)
            if tt > t_ready:
                t_ready = tt
        start = max(self.eng_free[eng], t_ready)
        du = _dur(fn, eng, is_dma)
        if is_dma:
            self.eng_free[eng] = start + 0.06
            o.t_end = start + 2.0 + du
        else:
            self.eng_free[eng] = start + du
            o.t_end = start + du
        seen = set()
        for d in sorted(deps, key=lambda d: -d.gid):
            if id(d) in seen:
                continue
            seen.add(id(d))
            self._dep(o, d, selfsync)
        key = ('dma', o.gid) if is_dma else eng
        for k in rkeys:
            self.readers.setdefault(k, {})[key] = o
        for k in wkeys:
            if acc:
                self.writers.setdefault(k, {})[key] = o
            else:
                self.writers[k] = {key: o}
                self.readers[k] = {}
        if is_dma:
            slot = self.ndma % NDMASEM
            prev = self.dma_ring[slot]
            if prev is not None:
                self._dep(o, prev, False)
            self.dma_ring[slot] = o
            o.dsem = (slot, 16 * (self.ndma // NDMASEM + 1))
            self.ndma += 1
        self.ops[eng].append(o)
        return o

    def dma(self, out, in_, reads, writes, eng='sp', is_out=False, acc=False, **kw):
        o = self.op(eng, lambda e: e.dma_start(out=out, in_=in_, **kw), reads, writes, is_dma=True, acc=acc, _is_out=is_out)
        if is_out and o is not None:
            self.out_dmas.append(o)
        return o

    def emit(self):
        nc = self.nc
        fin = Op('sp', None, False)
        fin.t_end = 0.0
        fin.seq = len(self.ops['sp'])
        for d in self.out_dmas:
            if id(d) not in self.seen_dma['sp']:
                fin.waits.append(d)
        self.ops['sp'].append(fin)
        sems = {e: nc.alloc_semaphore('s_' + e) for e in self.ENGS}
        dsems = [nc.alloc_semaphore('d%d' % i) for i in range(NDMASEM)]
        for e in self.ENGS:
            c = 0
            for o in self.ops[e]:
                if o.is_dma:
                    o.semval = (dsems[o.dsem[0]], o.dsem[1])
                elif o.signal:
                    c += 1
                    o.semval = (sems[e], c)
        ops = self.ops

        def run(eng, e):
            for o in ops[e]:
                for d in o.waits:
                    eng.wait_ge(d.semval[0], d.semval[1])
                if o.fn is None:
                    continue
                name, a_, kw_ = o.fn
                ins = getattr(eng, name)(*a_, **kw_)
                if o.is_dma:
                    ins.then_inc(o.semval[0], 16)
                elif o.signal:
                    ins.then_inc(o.semval[0], 1)

        with nc.Block() as block:
            @block.tensor
            def _(t):
                run(t, 'pe')

            @block.scalar
            def _(t):
                run(t, 'act')

            @block.vector
            def _(t):
                run(t, 'dve')

            @block.gpsimd
            def _(t):
                run(t, 'pool')

            @block.sync
            def _(t):
                run(t, 'sp')
        return {e: len(self.ops[e]) for e in self.ENGS}


DKS = 128 ** -0.5


def l0_setup(self):
    k = self.k
    A = self.arena0
    off = [A]

    def al(name, shape, dt):
        b = k.sb(name, shape, dt, at=off[0])
        off[0] = (off[0] + b.nbytes + 63) // 64 * 64
        return b
    self.qT = al("qT", [128, 4, 512], BF16)
    self.kT = al("kT", [128, 4, 512], BF16)
    self.Ktok = al("Ktok", [128, 4, 512], BF16)
    self.Vtok = al("Vtok", [128, 4, 512], BF16)
    self.rows = al("rows", [128, 4, 512], F32)
    self.Rr = al("Rr", [128, 3, 512], F32)
    self.hT = al("hT", [128, 4, 512], F32)
    self.qdec = al("qdec", [128, 4, 512], BF16)
    self.kinv = al("kinv", [128, 4, 512], BF16)
    self.tG = al("tG", [128, 512], F32)
    self.tK = al("tK", [128, 512], F32)
    self.eG = al("eG", [128, 512], F32)
    self.bcol = al("bcol", [128, 4, 4], F32)
    self.eGend = al("eGend", [128, 4, 8], F32)
    self.Em = [al("Em%d" % i, [128, 128], F32) for i in range(4)]
    self.Dm = [al("Dm%d" % i, [128, 128], F32) for i in range(4)]
    self.PT = [al("PT%d" % i, [128, 128], BF16) for i in range(4)]
    self.qw = [al("qw%d" % i, [128, 128], BF16) for i in range(4)]
    self.aden = [al("aden%d" % i, [128, 128], F32) for i in range(4)]
    self.Kw = [al("Kw%d" % i, [128, 128], BF16) for i in range(4)]
    self.attm = [al("attm%d" % i, [128, 64], BF16) for i in range(4)]
    self.ktokb = [al("ktokb%d" % i, [128, 128], BF16) for i in range(4)]
    self.tmpS = [al("tmpS%d" % i, [128, 128], F32) for i in range(4)]
    self.dcol = [al("dcol%d" % i, [128, 2], F32) for i in range(4)]
    self.sqh = al("sqh", [128, 512], BF16)
    self.rstdh = al("rstdh", [128, 512], F32)
    self.arena0_end = off[0]
    self.og = k.sb("og", [128, 4, 512], F32, at=self.h.addr)
    self.sig = k.sb("sig", [128, 4, 512], F32, at=self.h.addr)
    self.qh = k.sb("qh", [128, 4, 512], F32, at=self.h.addr + 8192)
    self.mixT = k.sb("mixT", [128, 8, 512], BF16, at=self.h.addr + 16384)
    self.gate_b = k.sb("gate_b", [128, 4, 512], F32, at=self.qT.addr)
    self.IVtok = k.sb("IVtok", [128, 8, 512], BF16, at=self.Ktok.addr)
    assert self.kT.addr == self.qT.addr + 4096 and self.Vtok.addr == self.Ktok.addr + 4096
    self.CN = k.sb("CN", [128, 4, 256], F32)
    self.CNb = k.sb("CNb", [128, 4, 256], BF16)
    self.S = k.sb("S", [128, 4, 128], F32)
    self.Sb = k.sb("Sb", [128, 4, 128], BF16)
    self.mcol = k.sb("mcol", [128, 2], F32)
    self.ncol = k.sb("ncol", [128, 4], F32)
    self.bA = k.sb("bA", [128, 16], F32)
    self.bB = k.sb("bB", [128, 16], F32)
    self.bqs = k.sb("bqs", [128, 4], F32)
    self.bg4 = k.sb("bg4", [128, 4], F32)
    self.brow_f = k.sb("brow_f", [1, 1536], F32, at=self.hT.addr)
    self.brow_b = k.sb("brow_b", [1, 1536], BF16)
    self.gm = k.sb("gm", [128, 4], F32)
    self.gh = k.sb("gh", [128, 4], F32)
    self.lg = k.sb("lg", [128, 3, 4], F32)
    self.lb = k.sb("lb", [128, 4], F32)
    self.oml = k.sb("oml", [128, 4], F32)
    self.noml = k.sb("noml", [128, 4], F32)
    self.lsum = k.sb("lsum", [128, 4], F32)
    bi = self.ab_b_in
    k.dma(self.bA[:], bi[0:2048].rearrange("(c p) -> p c", p=128), [], [self.bA], allow_slow_non_contiguous=True)
    k.dma(self.bB[:], bi[2056:4104].rearrange("(c p) -> p c", p=128), [], [self.bB], allow_slow_non_contiguous=True)
    k.dma(self.bg4[0:4, 0:1], bi[2048:2052].rearrange("(p o) -> p o", o=1), [], [self.bg4])
    k.dma(self.bg4[0:4, 1:2], bi[2052:2056].rearrange("(p o) -> p o", o=1), [], [self.bg4])
    k.dma(self.brow_f[0:1, 0:1024], bi[512:1536].rearrange("(o c) -> o c", o=1), [], [self.brow_f])
    k.dma(self.brow_f[0:1, 1024:1536], bi[3080:3592].rearrange("(o c) -> o c", o=1), [], [self.brow_f])
    k.dma(self.gm[:], self.mlstm_norm_g[:].rearrange("(c p) -> p c", p=128), [], [self.gm], allow_slow_non_contiguous=True)
    k.dma(self.gh[:], self.hgrn_norm_g[:].rearrange("(c p) -> p c", p=128), [], [self.gh], allow_slow_non_contiguous=True)
    k.dma(self.lg[:], self.hgrn_lb_logits[:].rearrange("s (h p) -> p s h", p=128), [], [self.lg], allow_slow_non_contiguous=True)
    k.op('dve', lambda e: e.tensor_copy(self.brow_b[:], self.brow_f[:]), [self.brow_f], [self.brow_b])
    k.op('dve', lambda e: e.tensor_scalar(self.bqs[:], self.bA[:, 0:4], DKS, None, ALU.mult), [self.bA], [self.bqs])
    k.op('dve', lambda e: e.tensor_scalar(self.bg4[0:4, 1:2], self.bg4[0:4, 1:2], -1.0, None, ALU.mult), [self.bg4], [self.bg4])
    k.op('act', lambda e: e.activation(self.lg[:], self.lg[:], AF.Exp), [self.lg], [self.lg])
    k.op('dve', lambda e: e.tensor_tensor(self.lsum[:], self.lg[:, 0, :], self.lg[:, 1, :], ALU.add), [self.lg], [self.lsum])
    k.op('dve', lambda e: e.tensor_tensor(self.lsum[:], self.lsum[:], self.lg[:, 2, :], ALU.add), [self.lg, self.lsum], [self.lsum])
    k.op('dve', lambda e: e.reciprocal(self.lsum[:], self.lsum[:]), [self.lsum], [self.lsum])
    k.op('dve', lambda e: e.tensor_tensor(self.lb[:], self.lg[:, 0, :], self.lsum[:], ALU.mult), [self.lg, self.lsum], [self.lb])
    k.op('dve', lambda e: e.tensor_scalar(self.oml[:], self.lb[:], -1.0, 1.0, ALU.mult, ALU.add), [self.lb], [self.oml])
    k.op('dve', lambda e: e.tensor_scalar(self.noml[:], self.oml[:], -1.0, None, ALU.mult), [self.oml], [self.noml])
    k.op('pool', lambda e: e.memset(self.rows[:], 0.0), [], [self.rows])
    k.op('pool', lambda e: e.memset(self.Rr[:], 0.0), [], [self.Rr])
    self.onec = k.sb("onec", [128, 2], F32)
    k.op('pool', lambda e: e.memset(self.onec[:], 1.0), [], [self.onec])


def l0_zero_state(self):
    k = self.k
    k.op('pool', lambda e: e.memset(self.CN[:], 0.0), [], [self.CN])
    k.op('pool', lambda e: e.memset(self.CNb[:], 0.0), [], [self.CNb])
    k.op('pool', lambda e: e.memset(self.S[:], 0.0), [], [self.S])
    k.op('pool', lambda e: e.memset(self.Sb[:], 0.0), [], [self.Sb])
    k.op('pool', lambda e: e.memset(self.mcol[:], 0.0), [], [self.mcol])


def l0_load_m(self, sq_):
    k = self.k
    for h in range(4):
        k.dma(self.CN[:, h, 0:128], self.st_mC[sq_, h], [], [self.CN])
    k.dma(self.ncol[:], self.st_mn[sq_].rearrange("h d -> d h"), [], [self.ncol], allow_slow_non_contiguous=True)
    k.dma(self.mcol[0:4, 0:1], self.st_mm[sq_].rearrange("(h o) -> h o", o=1), [], [self.mcol])
    ones = self.cst[:, self.C_ONES:self.C_ONES + 128]
    for h in range(4):
        k.op('dve', lambda e, h=h: e.tensor_scalar(self.CN[:, h, 128:256], ones, self.ncol[:, h:h + 1], None, ALU.mult),
             [self.cst, self.ncol], [self.CN])
    k.op('act', lambda e: e.activation(self.CNb[:], self.CN[:], AF.Copy), [self.CN], [self.CNb])


def l0_load_h(self, sq_):
    k = self.k
    for h in range(4):
        k.dma(self.S[:, h, :], self.st_hS[sq_, h], [], [self.S])
    k.op('act', lambda e: e.activation(self.Sb[:], self.S[:], AF.Copy), [self.S], [self.Sb])


def l0_store_m(self, dC, dn, dm):
    k = self.k
    for h in range(4):
        k.dma(dC[h], self.CN[:, h, 0:128], [self.CN], [self.yp.d(1)], is_out=True, acc=True)
        k.dma(dn[h].rearrange("(d o) -> d o", o=1), self.CN[:, h, 128:129], [self.CN], [self.yp.d(1)], is_out=True, acc=True)
    k.dma(dm.rearrange("(h o) -> h o", o=1), self.mcol[0:4, 0:1], [self.mcol], [self.yp.d(1)], is_out=True, acc=True)


def l0_store_h(self, dS):
    k = self.k
    for h in range(4):
        k.dma(dS[h], self.S[:, h, :], [self.S], [self.yp.d(1)], is_out=True, acc=True)


def l0_mixer(self, kind, t0, T, segs, last):
    k = self.k
    x, xn = self.x, self.xn
    k.op('act', lambda e: e.activation(self.rows[:], self.x[:, 0:4, :], AF.Copy, scale=0.0), [self.x], [self.rows])
    k.op('act', lambda e: e.activation(self.Rr[:], self.x[:, 4:7, :], AF.Copy, scale=0.0), [self.x], [self.Rr])
    self.rmsnorm(T, 1)
    mch = []
    hch = []
    for si, (c0, ln, _) in enumerate(segs):
        for a in range(0, ln, 128):
            mch.append((c0 + a, min(128, ln - a), si))
        for a in range(0, ln, 64):
            hch.append((c0 + a, min(64, ln - a), si))
    ones1 = self.cstb[0:1, self.CB_ONES:self.CB_ONES + 128]

    PIDX = {0: 0, 1024: 1, 2048: 2, 2568: 3, 3592: 4}

    def piece(c0, c1):
        w = c1 - c0
        slot = self.wload(("ab", PIDX[c0]))
        return slot, slot[:, 0:NJW * w].rearrange("p (j c) -> p j c", j=NJW)
    NJW = 8

    def fm(slot, wv, col, M=128):
        ps = self.bank()
        self.mm(ps, ps[:M, :T], [(wv[:, j, col:col + M], xn[:, j, :T]) for j in range(8)], [slot, xn])
        return ps

    slot, wv = piece(0, 1024)
    for h in range(4):
        ps = fm(slot, wv, h * 128)
        k.op('act', lambda e, ps=ps, h=h: e.activation(self.qT[:, h, :T], ps[:, :T], AF.Identity, bias=self.bqs[:, h:h + 1], scale=DKS),
             [ps, self.bqs], [self.qT.ch(h)])
        ps = fm(slot, wv, 512 + h * 128)
        k.op('dve', lambda e, ps=ps, h=h: e.tensor_scalar(self.kT[:, h, :T], ps[:, :T], self.bA[:, 4 + h:5 + h], None, ALU.add),
             [ps, self.bA], [self.kT.ch(h)])

    def tokproj(slot, wv, wcol, brcol, chunks, dst, eng):
        for ci, (c0, L, si) in enumerate(chunks):
            ps = self.bank()
            pairs = [(xn[:, j, c0:c0 + L], wv[:, j, wcol:wcol + 512]) for j in range(8)]
            pairs.append((ones1[:, 0:L], self.brow_b[0:1, brcol:brcol + 512]))
            self.mm(ps, ps[:L, :], pairs, [slot, xn, self.brow_b, self.cstb])
            if eng == 'act':
                k.op('act', lambda e, ps=ps, ci=ci, L=L: e.activation(dst[:L, ci, :], ps[:L, :], AF.Copy), [ps], [dst.ch(ci)])
            else:
                k.op('dve', lambda e, ps=ps, ci=ci, L=L: e.tensor_copy(dst[:L, ci, :], ps[:L, :]), [ps], [dst.ch(ci)])
    tokproj(slot, wv, 512, 0, mch, self.Ktok, 'act')
    slot, wv = piece(1024, 2048)
    tokproj(slot, wv, 0, 512, mch, self.Vtok, 'dve')
    for h in range(4):
        ps = fm(slot, wv, 512 + h * 128)
        k.op('act', lambda e, ps=ps, h=h: e.activation(self.og[:, h, :T], ps[:, :T], AF.Sigmoid, bias=self.bA[:, 12 + h:13 + h]),
             [ps, self.bA], [self.og.ch(h)])
    slot, wv = piece(2048, 2568)
    rows, Rr = self.rows, self.Rr
    ps_i = fm(slot, wv, 0, M=4)
    k.op('act', lambda e: e.activation(rows[0:4, 0, :T], ps_i[0:4, :T], AF.Identity, bias=self.bg4[0:4, 0:1]),
         [ps_i, self.bg4], [rows.ch(0)])
    ps_f = fm(slot, wv, 4, M=4)
    k.op('act', lambda e: e.activation(rows[0:4, 1, :T], ps_f[0:4, :T], AF.Exp, bias=self.bg4[0:4, 1:2], scale=-1.0),
         [ps_f, self.bg4], [rows.ch(1)])
    k.op('act', lambda e: e.activation(rows[0:4, 1, :T], rows[0:4, 1, :T], AF.Ln, bias=self.onec[0:4, 0:1]), [rows.ch(1), self.onec], [rows.ch(1)])
    for h in range(4):
        ps = fm(slot, wv, 8 + h * 128)
        k.op('act', lambda e, ps=ps, h=h: e.activation(self.qh[:, h, :T], ps[:, :T], AF.Silu, bias=self.bB[:, h:h + 1]),
             [ps, self.bB], [self.qh.ch(h)])

    def hgrn_proj():
        slot, wv = piece(2568, 3592)
        for h in range(4):
            ps = fm(slot, wv, h * 128)
            k.op('act', lambda e, ps=ps, h=h: e.activation(self.sig[:, h, :T], ps[:, :T], AF.Sigmoid, bias=self.bB[:, 4 + h:5 + h]),
                 [ps, self.bB], [self.sig.ch(h)])
        tokproj(slot, wv, 512, 1024, hch, self.IVtok, 'dve')
        slot, wv = piece(3592, 4104)
        for h in range(4):
            ps = fm(slot, wv, h * 128)
            k.op('act', lambda e, ps=ps, h=h: e.activation(self.gate_b[:, h, :T], ps[:, :T], AF.Silu, bias=self.bB[:, 12 + h:13 + h]),
                 [ps, self.bB], [self.gate_b.ch(h)])

    ones4 = self.cst[0:4, self.C_ONES:self.C_ONES + 512]
    for si, (s0, ln, sq_) in enumerate(segs):
        if sq_ is not None:
            l0_load_m(self, sq_)
        sl = slice(s0, s0 + ln)
        k.op('dve', lambda e, sl=sl, ln=ln: e.tensor_tensor_scan(rows[0:4, 2, sl], ones4[:, 0:ln], rows[0:4, 1, sl], 0.0, ALU.mult, ALU.add),
             [rows.ch(1), self.cst], [rows.ch(2)])
        k.op('dve', lambda e, sl=sl: e.tensor_tensor(rows[0:4, 0, sl], rows[0:4, 0, sl], rows[0:4, 2, sl], ALU.add),
             [rows.ch(0), rows.ch(2)], [rows.ch(0)])
        k.op('dve', lambda e, sl=sl, ln=ln: e.tensor_tensor_scan(rows[0:4, 3, sl], ones4[:, 0:ln], rows[0:4, 0, sl], self.mcol[0:4, 0:1], ALU.mult, ALU.max),
             [rows.ch(0), self.cst, self.mcol], [rows.ch(3)])
        k.op('dve', lambda e, sl=sl: e.tensor_scalar(Rr[0:4, 0, sl], rows[0:4, 3, sl], -1.0, None, ALU.mult), [rows.ch(3)], [Rr.ch(0)])
        k.op('dve', lambda e, sl=sl: e.tensor_tensor(rows[0:4, 1, sl], rows[0:4, 2, sl], rows[0:4, 3, sl], ALU.subtract),
             [rows.ch(2), rows.ch(3)], [rows.ch(1)])
        k.op('act', lambda e, sl=sl: e.activation(Rr[0:4, 2, sl], rows[0:4, 1, sl], AF.Exp), [rows.ch(1)], [Rr.ch(2)])
        my_m = [(ci, c) for ci, c in enumerate(mch) if c[2] == si]
        for (ci, (c0, L, _)) in my_m:
            if c0 == s0:
                k.op('act', lambda e, c0=c0, L=L: e.activation(Rr[0:4, 1, c0:c0 + L], Rr[0:4, 0, c0:c0 + L], AF.Exp, bias=self.mcol[0:4, 0:1]),
                     [Rr.ch(0), self.mcol], [Rr.ch(1)])
            else:
                k.op('act', lambda e, c0=c0, L=L: e.activation(Rr[0:4, 1, c0:c0 + L], Rr[0:4, 0, c0:c0 + L], AF.Exp, bias=rows[0:4, 3, c0 - 1:c0]),
                     [Rr.ch(0), rows.ch(3)], [Rr.ch(1)])
        k.op('dve', lambda e, s0=s0, ln=ln: e.tensor_tensor(self.mcol[0:4, 0:1], rows[0:4, 3, s0 + ln - 1:s0 + ln], rows[0:4, 2, s0 + ln - 1:s0 + ln], ALU.subtract),
             [rows.ch(3), rows.ch(2), Rr.ch(1)], [self.mcol])
        for (ci, (c0, L, _)) in my_m:
            cs = slice(c0, c0 + L)
            pb = self.bank()
            k.op('pe', lambda e, pb=pb, cs=cs, L=L: e.matmul(pb[:L, 0:4], rows[:, 0, cs], self.cst[:, self.C_ID:self.C_ID + 4], start=True, stop=True),
                 [rows.ch(0), self.cst], [pb])
            k.op('dve', lambda e, pb=pb, L=L, ci=ci: e.tensor_copy(self.bcol[:L, ci % 4, :], pb[:L, 0:4]), [pb], [self.bcol])
            allb = list(self.banks)
            thr = []
            for h in range(4):
                k.begin_thread()
                par = h
                Em, Dm, PT, qw, aden, Kw, dcol = self.Em[par], self.Dm[par], self.PT[par], self.qw[par], self.aden[par], self.Kw[par], self.dcol[par]
                pS, pB, pC, pD = BV(allb[2 * h], 0), BV(allb[2 * h], 128), BV(allb[2 * h + 1], 0), BV(allb[2 * h + 1], 256)
                k.op('pe', lambda e, pS=pS, h=h, cs=cs, L=L: e.matmul(pS[:L, :L], self.kT[:, h, cs], self.qT[:, h, cs], start=True, stop=True),
                     [self.kT.ch(h), self.qT.ch(h)], [pS])
                k.op('pe', lambda e, pB=pB, h=h, cs=cs, L=L: e.matmul(pB[:, 0:3 * L].rearrange("p (r l) -> p r l", r=3),
                                                                      self.cst[:, self.C_SEL + h * 128:self.C_SEL + (h + 1) * 128],
                                                                      Rr[:, :, cs], start=True, stop=True),
                     [Rr, self.cst], [pB])
                k.op('dve', lambda e, pB=pB, Em=Em, h=h, L=L, ci=ci: e.scalar_tensor_tensor(Em[:L, :L], pB[:L, 0:L], self.bcol[:L, ci % 4, h:h + 1],
                                                                                             self.cst[:L, self.C_MNEG:self.C_MNEG + L], ALU.add, ALU.add),
                     [pB, self.bcol, self.cst], [Em])
                k.op('act', lambda e, Em=Em, Dm=Dm, L=L: e.activation(Dm[:L, :L], Em[:L, :L], AF.Exp), [Em], [Dm])
                k.op('dve', lambda e, pS=pS, Dm=Dm, PT=PT, L=L: e.tensor_tensor(PT[:L, :L], pS[:L, :L], Dm[:L, :L], ALU.mult), [pS, Dm], [PT])
                k.op('dve', lambda e, pB=pB, qw=qw, h=h, cs=cs, L=L: e.tensor_tensor(qw[:, :L], self.qT[:, h, cs], pB[:, L:2 * L], ALU.mult),
                     [pB, self.qT.ch(h)], [qw])
                k.op('dve', lambda e, pB=pB, dcol=dcol, L=L: e.tensor_copy(dcol[:, 0:1], pB[:, 2 * L - 1:2 * L]), [pB], [dcol])
                self.mm(pC, pC[:, 0:L], [(self.Vtok[:L, ci, h * 128:(h + 1) * 128], PT[:L, :L]), (self.CNb[:, h, 0:128], qw[:, :L])],
                        [self.Vtok.ch(ci), PT, self.CNb, qw])
                self.mm(pC, pC[:, 128:128 + L], [(self.cstb[:L, self.CB_ONES:self.CB_ONES + 128], PT[:L, :L]), (self.CNb[:, h, 128:256], qw[:, :L])],
                        [PT, self.CNb, qw, self.cstb])
                k.op('act', lambda e, pC=pC, aden=aden, L=L: e.activation(aden[:, :L], pC[:, 128:128 + L], AF.Abs), [pC], [aden])
                k.op('dve', lambda e, pB=pB, aden=aden, L=L: e.tensor_tensor(aden[:, :L], aden[:, :L], pB[:, 2 * L:3 * L], ALU.max), [aden, pB], [aden])
                k.op('dve', lambda e, aden=aden, L=L: e.reciprocal(aden[:, :L], aden[:, :L]), [aden], [aden])
                k.op('dve', lambda e, pC=pC, aden=aden, h=h, cs=cs, L=L: e.tensor_tensor(self.hT[:, h, cs], pC[:, 0:L], aden[:, :L], ALU.mult),
                     [pC, aden], [self.hT.ch(h)])
                k.op('dve', lambda e, Kw=Kw, Dm=Dm, h=h, L=L, ci=ci: e.tensor_scalar(Kw[:L, :], self.Ktok[:L, ci, h * 128:(h + 1) * 128], Dm[:L, L - 1:L], None, ALU.mult),
                     [self.Ktok.ch(ci), Dm], [Kw])
                k.op('pe', lambda e, pD=pD, Kw=Kw, h=h, L=L, ci=ci: e.matmul(pD[:, 0:128], Kw[:L, :], self.Vtok[:L, ci, h * 128:(h + 1) * 128], start=True, stop=True),
                     [Kw, self.Vtok.ch(ci)], [pD])
                k.op('pe', lambda e, pD=pD, Kw=Kw, L=L: e.matmul(pD[:, 128:256], Kw[:L, :], self.cstb[:L, self.CB_ONES:self.CB_ONES + 128], start=True, stop=True),
                     [Kw, self.cstb], [pD])
                k.op('dve', lambda e, pD=pD, dcol=dcol, h=h: e.scalar_tensor_tensor(self.CN[:, h, :], self.CN[:, h, :], dcol[:, 0:1], pD[:, 0:256], ALU.mult, ALU.add),
                     [pD, dcol, self.CN.ch(h)], [self.CN.ch(h)])
                k.op('act', lambda e, h=h: e.activation(self.CNb[:, h, :], self.CN[:, h, :], AF.Copy), [self.CN.ch(h)], [self.CNb.ch(h)])
                thr.append(k.end_thread())
            k.merge(*thr)
        if sq_ is not None:
            l0_store_m(self, self.s_mC[sq_], self.s_mn[sq_], self.s_mm[sq_])
        elif last:
            l0_store_m(self, self.p_mC[:], self.p_mn[:], self.p_mm[:])
    head_epilogue(self, T, self.gm, self.og, 0)
    hgrn_proj()
    for si, (s0, ln, sq_) in enumerate(segs):
        sl = slice(s0, s0 + ln)
        if sq_ is not None:
            l0_load_h(self, sq_)
        my_h = [(ci, c) for ci, c in enumerate(hch) if c[2] == si]
        tG, tK, eG = self.tG, self.tK, self.eG
        rst = self.cst[:, self.C_RST:self.C_RST + 512]
        for h in range(4):
            k.op('dve', lambda e, h=h, sl=sl: e.tensor_scalar(tG[:, sl], self.sig[:, h, sl], self.oml[:, h:h + 1], self.lb[:, h:h + 1], ALU.mult, ALU.add),
                 [self.sig.ch(h), self.oml, self.lb], [tG])
            k.op('act', lambda e, sl=sl: e.activation(tG[:, sl], tG[:, sl], AF.Ln), [tG], [tG])
            k.op('dve', lambda e, h=h, sl=sl: e.tensor_scalar(tK[:, sl], self.sig[:, h, sl], self.noml[:, h:h + 1], self.oml[:, h:h + 1], ALU.mult, ALU.add),
                 [self.sig.ch(h), self.oml, self.noml], [tK])
            k.op('dve', lambda e, sl=sl, ln=ln: e.tensor_tensor_scan(eG[:, sl], rst[:, 0:ln], tG[:, sl], 0.0, ALU.mult, ALU.add), [tG, self.cst], [eG])
            k.op('act', lambda e, sl=sl: e.activation(tG[:, sl], eG[:, sl], AF.Exp, scale=-1.0), [eG], [tG])
            k.op('act', lambda e, sl=sl: e.activation(eG[:, sl], eG[:, sl], AF.Exp), [eG], [eG])
            k.op('dve', lambda e, h=h, sl=sl: e.tensor_tensor(self.kinv[:, h, sl], tK[:, sl], tG[:, sl], ALU.mult), [tK, tG], [self.kinv.ch(h)])
            k.op('dve', lambda e, h=h, sl=sl: e.tensor_tensor(self.qdec[:, h, sl], self.qh[:, h, sl], eG[:, sl], ALU.mult), [self.qh.ch(h), eG], [self.qdec.ch(h)])
            for (ci, (c0, L, _)) in my_h:
                k.op('dve', lambda e, h=h, ci=ci, c0=c0, L=L: e.tensor_copy(self.eGend[:, h, ci:ci + 1], eG[:, c0 + L - 1:c0 + L]), [eG], [self.eGend])
        for (ci, (c0, L, _)) in my_h:
            cs = slice(c0, c0 + L)
            allb = list(self.banks)
            thr = []
            for h in range(4):
                k.begin_thread()
                par = h
                attm, ktokb, tmpS = self.attm[par], self.ktokb[par], self.tmpS[par]
                pA, pB, pC, pD = BV(allb[2 * h], 0), BV(allb[2 * h], 128), BV(allb[2 * h + 1], 0), BV(allb[2 * h + 1], 128)
                k.op('pe', lambda e, pA=pA, h=h, cs=cs, L=L: e.matmul(pA[:L, :L], self.kinv[:, h, cs], self.qdec[:, h, cs], start=True, stop=True),
                     [self.kinv.ch(h), self.qdec.ch(h)], [pA])
                k.op('pe', lambda e, pB=pB, h=h, cs=cs, L=L: e.matmul(pB[:L, 0:128], self.kinv[:, h, cs], self.cstb[:, self.CB_ID:self.CB_ID + 128], start=True, stop=True),
                     [self.kinv.ch(h), self.cstb], [pB])
                k.op('dve', lambda e, pA=pA, attm=attm, L=L: e.tensor_tensor(attm[:L, :L], pA[:L, :L], self.cst[:L, self.C_TRI:self.C_TRI + L], ALU.mult),
                     [pA, self.cst], [attm])
                k.op('act', lambda e, pB=pB, ktokb=ktokb, L=L: e.activation(ktokb[:L, :], pB[:L, 0:128], AF.Copy), [pB], [ktokb])
                self.mm(pC, pC[:, 0:L], [(self.IVtok[:L, ci, h * 128:(h + 1) * 128], attm[:L, :L]), (self.Sb[:, h, :], self.qdec[:, h, cs])],
                        [self.IVtok.ch(ci), attm, self.Sb.ch(h), self.qdec.ch(h)])
                k.op('dve', lambda e, pC=pC, h=h, cs=cs, L=L: e.tensor_copy(self.hT[:, h, cs], pC[:, 0:L]), [pC], [self.hT.ch(h)])
                k.op('pe', lambda e, pD=pD, ktokb=ktokb, h=h, L=L, ci=ci: e.matmul(pD[:, 0:128], ktokb[:L, :], self.IVtok[:L, ci, h * 128:(h + 1) * 128], start=True, stop=True),
                     [ktokb, self.IVtok.ch(ci)], [pD])
                k.op('dve', lambda e, pD=pD, tmpS=tmpS, h=h: e.tensor_tensor(tmpS[:, :], self.S[:, h, :], pD[:, 0:128], ALU.add), [pD, self.S.ch(h)], [tmpS])
                k.op('act', lambda e, tmpS=tmpS, h=h, ci=ci: e.activation(self.S[:, h, :], tmpS[:, :], AF.Identity, scale=self.eGend[:, h, ci:ci + 1]),
                     [tmpS, self.eGend], [self.S.ch(h)])
                k.op('act', lambda e, tmpS=tmpS, h=h, ci=ci: e.activation(self.Sb[:, h, :], tmpS[:, :], AF.Identity, scale=self.eGend[:, h, ci:ci + 1]),
                     [tmpS, self.eGend], [self.Sb.ch(h)])
                thr.append(k.end_thread())
            for t_ in thr:
                k.merge(t_)
        if sq_ is not None:
            l0_store_h(self, self.s_hS[sq_])
        elif last:
            l0_store_h(self, self.p_hS[:])
    head_epilogue(self, T, self.gh, self.gate_b, 4)
    out_proj(self, T, ("abo",))


def head_epilogue(self, T, gcols, gate, mix_off):
    k = self.k
    for h in range(4):
        k.op('act', lambda e, h=h: e.activation(self.sqh[:, :T], self.hT[:, h, :T], AF.Square), [self.hT.ch(h)], [self.sqh])
        ps = self.bank()
        self.mm(ps, ps[:, :T], [(self.cstb[:, self.CB_ONES:self.CB_ONES + 128], self.sqh[:, :T])], [self.sqh, self.cstb])
        self.rsqrt(self.rstdh, ps, T, 1.0 / 128)
        k.op('dve', lambda e, h=h: e.scalar_tensor_tensor(self.hT[:, h, :T], self.hT[:, h, :T], gcols[:, h:h + 1], self.rstdh[:, :T], ALU.mult, ALU.mult),
             [self.hT.ch(h), self.rstdh], [self.hT.ch(h)])
        k.op('dve', lambda e, h=h: e.tensor_tensor(self.mixT[:, mix_off + h, :T], self.hT[:, h, :T], gate[:, h, :T], ALU.mult),
             [self.hT.ch(h), gate.ch(h)], [self.mixT.ch(mix_off + h)])


def out_proj(self, T, Wb):
    k = self.k
    slot = self.wload(Wb)
    wv = slot[:, 0:8 * 1024].rearrange("p (j c) -> p j c", j=8)
    for d in range(8):
        po = self.bank()
        self.mm(po, po[:, :T], [(wv[:, j, d * 128:(d + 1) * 128], self.mixT[:, j, :T]) for j in range(8)], [slot, self.mixT])
        k.op('dve', lambda e, po=po, d=d: e.tensor_tensor(self.x[:, d, :T], self.x[:, d, :T], po[:, :T], ALU.add), [po, self.x.ch(d)], [self.x.ch(d)])

import math


def l1_setup(self):
    k = self.k
    P = lambda name, shape, dt: k.sb(name, shape, dt)
    self.bC = P("bC", [128, 14], F32)
    self.convw = P("convw", [128, 4, 6], F32)
    self.convb = P("convb", [128, 6], F32)
    self.rgA = P("rgA", [128, 6, 128], BF16)
    self.rgX = P("rgX", [128, 6, 128], BF16)
    self.rgba = P("rgba", [128, 6], F32)
    self.rgbx = P("rgbx", [128, 6], F32)
    self.ccol = P("ccol", [128, 6], F32)
    self.c2col = P("c2col", [128, 6], F32)
    self.Are = P("Are", [128, 8], F32)
    self.Aim = P("Aim", [128, 8], F32)
    self.dtc = P("dtc", [128, 8], F32)
    self.rho = P("rho", [128, 8], F32)
    self.Ec = P("Ec", [128, 8, 128], F32)
    self.Es = P("Es", [128, 8, 128], F32)
    self.BBre = P("BBre", [128, 8, 128], BF16)
    self.BBim = P("BBim", [128, 8, 128], BF16)
    self.CTre = P("CTre", [128, 8, 128], BF16)
    self.CTimn = P("CTimn", [128, 8, 128], BF16)
    self.dS5 = P("dS5", [128, 2], F32)
    self.wglu = P("wglu", [128, 2, 256], BF16)
    self.bglu = P("bglu", [128, 2], F32)
    self.hst = P("hst", [128, 6], F32)
    self.crprev = P("crprev", [128, 6, 3], F32)
    self.sre = P("sre", [128, 8], F32)
    self.sim = P("sim", [128, 8], F32)
    self.hpi = P("hpi", [128, 2], F32)
    self.cc4 = P("cc4", [128, 8, 2], F32)
    self.sreS = [P("sreS%d" % i, [128, 8], F32) for i in range(2)]
    self.simS = [P("simS%d" % i, [128, 8], F32) for i in range(2)]
    self.tz = [P("tz%d" % i, [128, 8], F32) for i in range(8)]
    A = self.arena0
    stA = k.sb("stA", [128, 8, 128], F32, at=A)
    stB = k.sb("stB", [128, 8, 128], F32, at=A + 4096)
    stC = k.sb("stC", [128, 8, 128], F32, at=A + 8192)
    stD = k.sb("stD", [128, 8, 128], F32, at=A + 12288)
    stW = k.sb("stW", [128, 2, 256], F32, at=A + 16384)
    nz = lambda ap: ap
    k.dma(self.bC[:], self.cd_b_in[:].rearrange("(c p) -> p c", p=128), [], [self.bC], allow_slow_non_contiguous=True)
    k.dma(self.convw[:], self.conv_w[:].rearrange("j (c p) -> p j c", p=128), [], [self.convw], allow_slow_non_contiguous=True)
    k.dma(self.convb[:], self.conv_b[:].rearrange("(c p) -> p c", p=128), [], [self.convb], allow_slow_non_contiguous=True)
    k.dma(self.rgba[:], self.rg_b_a[:].rearrange("(c p) -> p c", p=128), [], [self.rgba], allow_slow_non_contiguous=True)
    k.dma(self.rgbx[:], self.rg_b_x[:].rearrange("(c p) -> p c", p=128), [], [self.rgbx], allow_slow_non_contiguous=True)
    k.dma(self.ccol[:], self.rg_lambda[:].rearrange("(c p) -> p c", p=128), [], [self.ccol], allow_slow_non_contiguous=True)
    k.dma(self.dS5[:], self.s5_D[:].rearrange("(c p) -> p c", p=128), [], [self.dS5], allow_slow_non_contiguous=True)
    k.dma(self.bglu[:], self.s5_b_glu[:].rearrange("(c p) -> p c", p=128), [], [self.bglu], allow_slow_non_contiguous=True)
    k.dma(self.Are[:], self.s5_A_re[:].rearrange("(k g) p -> (g p) k", g=2), [], [self.Are], allow_slow_non_contiguous=True)
    k.dma(self.Aim[:], self.s5_A_im[:].rearrange("(k g) p -> (g p) k", g=2), [], [self.Aim], allow_slow_non_contiguous=True)
    ld = self.s5_log_dt[:]
    for g2 in range(2):
        src = bass.AP(ld.tensor, g2, [[0, 64], [2, 8]])
        k.dma(self.dtc[g2 * 64:(g2 + 1) * 64, :], src, [], [self.dtc], allow_slow_non_contiguous=True)
    k.op('pool', lambda e: e.memset(self.hpi[:], math.pi / 2), [], [self.hpi])
    k.op('act', lambda e: e.activation(self.ccol[:], self.ccol[:], AF.Exp, scale=-1.0), [self.ccol], [self.ccol])
    k.op('act', lambda e: e.activation(self.ccol[:], self.ccol[:], AF.Ln, bias=self.onec[:, 0:1]), [self.ccol, self.onec], [self.ccol])
    k.op('dve', lambda e: e.tensor_scalar(self.c2col[:], self.ccol[:], -16.0, None, ALU.mult), [self.ccol], [self.c2col])
    k.op('dve', lambda e: e.tensor_scalar(self.ccol[:], self.ccol[:], -8.0, None, ALU.mult), [self.ccol], [self.ccol])
    for (src, dst) in ((self.rg_w_a, self.rgA), (self.rg_w_x, self.rgX)):
        k.op('pool', lambda e: e.memset(stA[:, 0:6, :], 0.0), [], [stA])
        for n in range(12):
            c, hf = n // 2, n % 2
            k.dma(stA[hf * 64:(hf + 1) * 64, c, hf * 64:(hf + 1) * 64], src[n], [], [stA])
        k.op('dve', lambda e, dst=dst: e.tensor_copy(dst[:], stA[:, 0:6, :]), [stA], [dst])
    k.dma(stW[:], self.s5_w_glu[:].rearrange("(j p) c -> p j c", p=128), [], [stW])
    k.op('dve', lambda e: e.tensor_copy(self.wglu[:], stW[:]), [stW], [self.wglu])
    t = self.tz
    TT = lambda out, a, b, op: k.op('dve', lambda e: e.tensor_tensor(out[:], a[:], b[:], op), [a, b], [out])
    TS = lambda out, a, s1, s2, op0, op1=None: k.op('dve', (lambda e: e.tensor_scalar(out[:], a[:], s1, s2, op0, op1)) if op1 is not None else
                                                    (lambda e: e.tensor_scalar(out[:], a[:], s1, None, op0)), [a], [out])
    dt = self.dtc
    k.op('act', lambda e: e.activation(dt[:], dt[:], AF.Exp, scale=0.125), [dt], [dt])
    for _ in range(3):
        TT(dt, dt, dt, ALU.mult)
    TT(t[0], dt, self.Are, ALU.mult)
    TT(t[1], dt, self.Aim, ALU.mult)
    k.op('act', lambda e: e.activation(self.rho[:], t[0][:], AF.Exp), [t[0]], [self.rho])
    k.op('act', lambda e: e.activation(t[2][:], t[1][:], AF.Sin, bias=self.hpi[:, 0:1], scale=1.0 / 16), [t[1], self.hpi], [t[2]])
    k.op('act', lambda e: e.activation(t[3][:], t[1][:], AF.Sin, scale=1.0 / 16), [t[1]], [t[3]])
    for _ in range(4):
        TT(t[4], t[2], t[2], ALU.mult)
        TT(t[5], t[3], t[3], ALU.mult)
        TT(t[6], t[2], t[3], ALU.mult)
        TT(t[2], t[4], t[5], ALU.subtract)
        TS(t[3], t[6], 2.0, None, ALU.mult)
    cth, sth = t[2], t[3]
    TT(t[4], self.rho, cth, ALU.mult)
    TT(t[5], self.rho, sth, ALU.mult)
    TT(t[0], self.Are, self.Are, ALU.mult)
    TT(t[1], self.Aim, self.Aim, ALU.mult)
    TT(t[0], t[0], t[1], ALU.add)
    k.op('dve', lambda e: e.reciprocal(t[0][:], t[0][:]), [t[0]], [t[0]])
    TS(t[6], t[4], -1.0, None, ALU.add)
    TT(t[1], t[6], self.Are, ALU.mult)
    TT(t[7], t[5], self.Aim, ALU.mult)
    TT(t[1], t[1], t[7], ALU.add)
    TT(t[1], t[1], t[0], ALU.mult)
    TT(t[7], t[5], self.Are, ALU.mult)
    TT(t[6], t[6], self.Aim, ALU.mult)
    TT(t[7], t[7], t[6], ALU.subtract)
    TT(t[7], t[7], t[0], ALU.mult)
    TS(t[6], t[7], -1.0, None, ALU.mult)
    zre, zim, nzim = t[1], t[7], t[6]
    Ec, Es = self.Ec, self.Es
    k.op('dve', lambda e: e.tensor_copy(Ec[:, :, 0], cth[:]), [cth], [Ec])
    k.op('dve', lambda e: e.tensor_copy(Es[:, :, 0], sth[:]), [sth], [Es])
    tmpT = stD
    m = 1
    while m < 128:
        k.op('dve', lambda e, m=m: e.tensor_scalar(t[0][:], Es[:, :, m - 1], -1.0, None, ALU.mult), [Es], [t[0]])
        for kk in range(8):
            cm = Ec[:, kk, m - 1:m]
            sm = Es[:, kk, m - 1:m]
            nsm = t[0][:, kk:kk + 1]
            k.op('dve', lambda e, kk=kk, m=m, cm=cm: e.tensor_scalar(tmpT[:, kk, 0:m], Ec[:, kk, 0:m], cm, None, ALU.mult), [Ec], [tmpT])
            k.op('dve', lambda e, kk=kk, m=m, sm=sm: e.tensor_scalar(tmpT[:, kk, 64:64 + m], Ec[:, kk, 0:m], sm, None, ALU.mult), [Ec], [tmpT])
            k.op('dve', lambda e, kk=kk, m=m, nsm=nsm: e.scalar_tensor_tensor(Ec[:, kk, m:2 * m], Es[:, kk, 0:m], nsm, tmpT[:, kk, 0:m], ALU.mult, ALU.add),
                 [Es, tmpT, t[0]], [Ec])
            k.op('dve', lambda e, kk=kk, m=m, cm=cm: e.scalar_tensor_tensor(Es[:, kk, m:2 * m], Es[:, kk, 0:m], cm, tmpT[:, kk, 64:64 + m], ALU.mult, ALU.add),
                 [Es, tmpT, Ec], [Es])
        m *= 2
    k.op('pool', lambda e: e.memset(stA[:], 0.0), [], [stA])
    k.op('pool', lambda e: e.memset(stB[:], 0.0), [], [stB])
    for g in range(16):
        kk, g2 = g // 2, g % 2
        off = 32 * (kk % 4) + g2 * 16
        k.dma(stA[g2 * 64:(g2 + 1) * 64, kk, off:off + 16], self.s5_B_re[g], [], [stA])
        k.dma(stB[g2 * 64:(g2 + 1) * 64, kk, off:off + 16], self.s5_B_im[g], [], [stB])
    for kk in range(8):
        k.op('dve', lambda e, kk=kk: e.tensor_scalar(stC[:, kk, :], stA[:, kk, :], zre[:, kk:kk + 1], None, ALU.mult), [stA, zre], [stC])
        k.op('dve', lambda e, kk=kk: e.scalar_tensor_tensor(stC[:, kk, :], stB[:, kk, :], nzim[:, kk:kk + 1], stC[:, kk, :], ALU.mult, ALU.add),
             [stB, nzim, stC], [stC])
        k.op('dve', lambda e, kk=kk: e.tensor_scalar(stD[:, kk, :], stB[:, kk, :], zre[:, kk:kk + 1], None, ALU.mult), [stB, zre], [stD])
        k.op('dve', lambda e, kk=kk: e.scalar_tensor_tensor(stD[:, kk, :], stA[:, kk, :], zim[:, kk:kk + 1], stD[:, kk, :], ALU.mult, ALU.add),
             [stA, zim, stD], [stD])
    ident = self.cst[:, self.C_ID:self.C_ID + 128]
    for kk in range(8):
        for (srcb, dstb) in ((stC, self.BBre), (stD, self.BBim)):
            ps = self.bank()
            k.op('pe', lambda e, ps=ps, srcb=srcb, kk=kk: e.matmul(ps[:, 0:128], srcb[:, kk, :], ident, start=True, stop=True), [srcb, self.cst], [ps])
            k.op('act', lambda e, ps=ps, dstb=dstb, kk=kk: e.activation(dstb[:, kk, :], ps[:, 0:128], AF.Copy), [ps], [dstb])
    k.op('pool', lambda e: e.memset(stA[:], 0.0), [], [stA])
    k.op('pool', lambda e: e.memset(stB[:], 0.0), [], [stB])
    for g in range(16):
        kk, g2 = g // 2, g % 2
        off = 32 * (kk % 4) + g2 * 16
        k.dma(stA[off:off + 16, kk, g2 * 64:(g2 + 1) * 64], self.s5_C_re[g], [], [stA])
        k.dma(stB[off:off + 16, kk, g2 * 64:(g2 + 1) * 64], self.s5_C_im[g], [], [stB])
    for kk in range(8):
        ps = self.bank()
        k.op('pe', lambda e, ps=ps, kk=kk: e.matmul(ps[:, 0:128], stA[:, kk, :], ident, start=True, stop=True), [stA, self.cst], [ps])
        k.op('act', lambda e, ps=ps, kk=kk: e.activation(self.CTre[:, kk, :], ps[:, 0:128], AF.Copy), [ps], [self.CTre])
        ps = self.bank()
        k.op('pe', lambda e, ps=ps, kk=kk: e.matmul(ps[:, 0:128], stB[:, kk, :], ident, start=True, stop=True), [stB, self.cst], [ps])
        k.op('act', lambda e, ps=ps, kk=kk: e.activation(self.CTimn[:, kk, :], ps[:, 0:128], AF.Identity, scale=-1.0), [ps], [self.CTimn])


def l1_alloc(self):
    k = self.k
    off = [self.arena0]

    def al(name, shape, dt):
        b = k.sb(name, shape, dt, at=off[0])
        off[0] = (off[0] + b.nbytes + 63) // 64 * 64
        return b
    self.gc = al("gc", [128, 6, 512], F32)
    self.crbuf = al("crbuf", [128, 6, 516], F32)
    self.duf = al("duf", [128, 2, 512], F32)
    self.dub = al("dub", [128, 2, 512], BF16)
    self.g1 = al("g1", [128, 512], F32)
    self.g2 = al("g2", [128, 512], F32)
    self.yf = al("yf", [128, 2, 512], F32)
    self.ygb = al("ygb", [128, 2, 512], BF16)
    self.rt = [al("rt%d" % i, [128, 128], F32) for i in range(2)]
    self.u_ = [al("u_%d" % i, [128, 512], F32) for i in range(2)]
    self.ub_ = [al("ub_%d" % i, [128, 512], BF16) for i in range(1)]
    self.r_ = [al("r_%d" % i, [128, 512], F32) for i in range(1)]
    self.ig_ = [al("ig_%d" % i, [128, 512], F32) for i in range(1)]
    self.a_ = [al("a_%d" % i, [128, 512], F32) for i in range(1)]
    self.hs_ = [al("hs_%d" % i, [128, 512], F32) for i in range(1)]
    self.vre = [al("vre%d" % i, [128, 512], F32) for i in range(1)]
    self.vim = [al("vim%d" % i, [128, 512], F32) for i in range(1)]
    self.wre = [al("wre%d" % i, [128, 512], F32) for i in range(1)]
    self.wim = [al("wim%d" % i, [128, 512], F32) for i in range(1)]
    self.t1 = [al("t1%d" % i, [128, 512], F32) for i in range(1)]
    self.t2 = [al("t2%d" % i, [128, 512], F32) for i in range(1)]
    self.sreb = [al("sreb%d" % i, [128, 512], BF16) for i in range(1)]
    self.simb = [al("simb%d" % i, [128, 512], BF16) for i in range(1)]
    end1 = off[0]
    end = max(end1, off[0])
    assert end <= self.arena0 + self.ARENA, end - self.arena0


def gelu(self, out, xb, T, rres, wres):
    k = self.k
    g1 = self.g1
    k.op('act', lambda e: e.activation(g1[:, :T], xb, AF.Square), [rres], [g1])
    k.op('dve', lambda e: e.tensor_scalar(g1[:, :T], g1[:, :T], 0.044715, 1.0, ALU.mult, ALU.add), [g1], [g1])
    k.op('dve', lambda e: e.tensor_tensor(g1[:, :T], g1[:, :T], xb, ALU.mult), [g1, rres], [g1])
    k.op('act', lambda e: e.activation(g1[:, :T], g1[:, :T], AF.Sigmoid, scale=1.5957691216057308), [g1], [g1])
    k.op('dve', lambda e: e.tensor_tensor(out, xb, g1[:, :T], ALU.mult), [g1, rres], [wres])


def l1_mixer(self, kind, t0, T, segs, last):
    k = self.k
    xn = self.xn
    self.rmsnorm(T, 4)
    mixT = self.mixT

    def piece(c0, c1):
        w = c1 - c0
        slot = self.wload(("cd", 0 if c0 == 0 else 1))
        return slot, slot[:, 0:8 * w].rearrange("p (j c) -> p j c", j=8)

    def fm(slot, wv, col):
        ps = self.bank()
        self.mm(ps, ps[:, :T], [(wv[:, j, col:col + 128], xn[:, j, :T]) for j in range(8)], [slot, xn])
        return ps
    gc, crbuf, duf, dub = self.gc, self.crbuf, self.duf, self.dub
    cbase = [(35 * si if kind == "s" else 0) for si in range(len(segs))]
    for si, (s0, ln, sq_) in enumerate(segs):
        cb = cbase[si]
        if sq_ is None:
            k.op('dve', lambda e, cb=cb: e.tensor_copy(crbuf[:, :, cb:cb + 3], self.crprev[:]), [self.crprev], [crbuf])
        else:
            for c in range(6):
                k.dma(crbuf[:, c, cb:cb + 3], self.st_rc[sq_, :, c * 128:(c + 1) * 128].rearrange("j p -> p j"), [], [crbuf], allow_slow_non_contiguous=True)

    def cr_evac(ps, c):
        for si, (s0, ln, sq_) in enumerate(segs):
            cb = cbase[si]
            k.op('act', lambda e, cb=cb, s0=s0, ln=ln: e.activation(crbuf[:, c, cb + 3:cb + 3 + ln], ps[:, s0:s0 + ln], AF.Identity, bias=self.bC[:, 6 + c:7 + c]),
                 [ps, self.bC], [crbuf.ch(c)])
    slot, wv = piece(0, 1024)
    for c in range(6):
        ps = fm(slot, wv, c * 128)
        u = self.u_[c % 2]
        k.op('act', lambda e, ps=ps, u=u, c=c: e.activation(u[:, :T], ps[:, :T], AF.Identity, bias=self.bC[:, c:c + 1]), [ps, self.bC], [u])
        gelu(self, gc[:, c, :T], u[:, :T], T, u, gc.ch(c))
    for c in range(2):
        cr_evac(fm(slot, wv, 768 + c * 128), c)
    slot, wv = piece(1024, 1792)
    for c in range(2, 6):
        cr_evac(fm(slot, wv, (c - 2) * 128), c)
    for j in range(2):
        ps = fm(slot, wv, 512 + j * 128)
        k.op('act', lambda e, ps=ps, j=j: e.activation(duf[:, j, :T], ps[:, :T], AF.Identity, bias=self.bC[:, 12 + j:13 + j]), [ps, self.bC], [duf.ch(j)])
        k.op('dve', lambda e, j=j: e.tensor_copy(dub[:, j, :T], duf[:, j, :T]), [duf.ch(j)], [dub.ch(j)])
    allb = list(self.banks)
    self.banks = allb[0:2]
    k.begin_thread()
    for si, (s0, ln, sq_) in enumerate(segs):
        cb = cbase[si]
        sl = slice(s0, s0 + ln)
        if sq_ is not None:
            k.dma(self.hst[:], self.st_rh[sq_].rearrange("(c p) -> p c", p=128), [], [self.hst], allow_slow_non_contiguous=True)
        for c in range(6):
            u, ub, r, ig, a, hs = self.u_[c % 2], self.ub_[0], self.r_[0], self.ig_[0], self.a_[0], self.hs_[0]
            k.op('dve', lambda e, c=c, u=u, cb=cb, ln=ln: e.tensor_scalar(u[:, :ln], crbuf[:, c, cb + 3:cb + 3 + ln], self.convw[:, 3, c:c + 1], self.convb[:, c:c + 1], ALU.mult, ALU.add),
                 [crbuf.ch(c), self.convw, self.convb], [u])
            for j in range(3):
                k.op('dve', lambda e, c=c, u=u, cb=cb, ln=ln, j=j: e.scalar_tensor_tensor(u[:, :ln], crbuf[:, c, cb + j:cb + j + ln], self.convw[:, j, c:c + 1], u[:, :ln], ALU.mult, ALU.add),
                     [crbuf.ch(c), self.convw, u], [u])
            k.op('act', lambda e, u=u, ub=ub, ln=ln: e.activation(ub[:, :ln], u[:, :ln], AF.Copy), [u], [ub])
            pr, pi = self.bank(), self.bank()
            k.op('pe', lambda e, pr=pr, ub=ub, c=c, ln=ln: e.matmul(pr[:, :ln], self.rgA[:, c, :], ub[:, :ln], start=True, stop=True), [ub, self.rgA], [pr])
            k.op('pe', lambda e, pi=pi, ub=ub, c=c, ln=ln: e.matmul(pi[:, :ln], self.rgX[:, c, :], ub[:, :ln], start=True, stop=True), [ub, self.rgX], [pi])
            k.op('act', lambda e, pr=pr, r=r, c=c, ln=ln: e.activation(r[:, :ln], pr[:, :ln], AF.Sigmoid, bias=self.rgba[:, c:c + 1]), [pr, self.rgba], [r])
            k.op('act', lambda e, pi=pi, ig=ig, c=c, ln=ln: e.activation(ig[:, :ln], pi[:, :ln], AF.Sigmoid, bias=self.rgbx[:, c:c + 1]), [pi, self.rgbx], [ig])
            k.op('act', lambda e, r=r, a=a, c=c, ln=ln: e.activation(a[:, :ln], r[:, :ln], AF.Exp, scale=self.ccol[:, c:c + 1]), [r, self.ccol], [a])
            k.op('act', lambda e, r=r, c=c, ln=ln: e.activation(r[:, :ln], r[:, :ln], AF.Exp, scale=self.c2col[:, c:c + 1]), [r, self.c2col], [r])
            k.op('act', lambda e, r=r, ln=ln: e.activation(r[:, :ln], r[:, :ln], AF.Sqrt, bias=self.onec[:, 0:1], scale=-1.0), [r, self.onec], [r])
            k.op('dve', lambda e, ig=ig, u=u, ln=ln: e.tensor_tensor(ig[:, :ln], ig[:, :ln], u[:, :ln], ALU.mult), [ig, u], [ig])
            k.op('dve', lambda e, ig=ig, r=r, ln=ln: e.tensor_tensor(ig[:, :ln], ig[:, :ln], r[:, :ln], ALU.mult), [ig, r], [ig])
            k.op('dve', lambda e, a=a, ig=ig, hs=hs, c=c, ln=ln: e.tensor_tensor_scan(hs[:, :ln], a[:, :ln], ig[:, :ln], self.hst[:, c:c + 1], ALU.mult, ALU.add),
                 [a, ig, self.hst], [hs])
            k.op('dve', lambda e, hs=hs, c=c, ln=ln: e.tensor_copy(self.hst[:, c:c + 1], hs[:, ln - 1:ln]), [hs], [self.hst])
            k.op('dve', lambda e, hs=hs, c=c, sl=sl, ln=ln: e.tensor_tensor(mixT[:, c, sl], gc[:, c, sl], hs[:, :ln], ALU.mult), [gc.ch(c), hs], [mixT.ch(c)])
        if sq_ is not None:
            k.dma(self.s_rh[sq_].rearrange("(c p) -> p c", p=128), self.hst[:], [self.hst], [self.yp.d(1)], is_out=True, acc=True, allow_slow_non_contiguous=True)
            for c in range(6):
                k.dma(self.s_rc[sq_, :, c * 128:(c + 1) * 128].rearrange("j p -> p j"), crbuf[:, c, cb + ln:cb + ln + 3], [crbuf], [self.yp.d(1)],
                      is_out=True, acc=True, allow_slow_non_contiguous=True)
        else:
            k.op('dve', lambda e, cb=cb, ln=ln: e.tensor_copy(self.crprev[:], crbuf[:, :, cb + ln:cb + ln + 3]), [crbuf], [self.crprev])
            if last:
                k.dma(self.p_rh[:].rearrange("(c p) -> p c", p=128), self.hst[:], [self.hst], [self.yp.d(1)], is_out=True, acc=True, allow_slow_non_contiguous=True)
                for c in range(6):
                    k.dma(self.p_rc[:, c * 128:(c + 1) * 128].rearrange("j p -> p j"), self.crprev[:, c, :], [self.crprev], [self.yp.d(1)],
                          is_out=True, acc=True, allow_slow_non_contiguous=True)
    thr_rg = k.end_thread()
    nb_save = allb
    ybank = [allb[6], allb[7]]
    self.banks = allb[2:6]
    k.begin_thread()
    Ec, Es = self.Ec, self.Es
    blocks = []
    stre, stim = [], []
    for si, (s0, ln, sq_) in enumerate(segs):
        for a0 in range(0, ln, 128):
            blocks.append((s0 + a0, min(128, ln - a0), si, a0 == 0, a0 + 128 >= ln))
        if sq_ is None:
            stre.append(self.sre); stim.append(self.sim)
        else:
            stre.append(self.sreS[si]); stim.append(self.simS[si])
            k.dma(self.sreS[si][:], self.st_sre[sq_].rearrange("(k g) p -> (g p) k", g=2), [], [self.sreS[si]], allow_slow_non_contiguous=True)
            k.dma(self.simS[si][:], self.st_sim[sq_].rearrange("(k g) p -> (g p) k", g=2), [], [self.simS[si]], allow_slow_non_contiguous=True)
    for kk in range(8):
        kc = kk // 4
        par = kk % 2
        vre, vim, wre, wim, t1, t2, sreb, simb = self.vre[0], self.vim[0], self.wre[0], self.wim[0], self.t1[0], self.t2[0], self.sreb[0], self.simb[0]
        pre, pim = self.bank(), self.bank()
        rtile = self.rt[par]
        k.op('dve', lambda e, rtile=rtile, kk=kk: e.tensor_scalar(rtile[:, :], self.cst[:, self.C_ONES:self.C_ONES + 128], self.rho[:, kk:kk + 1], None, ALU.mult),
             [self.cst, self.rho], [rtile])
        k.op('pe', lambda e, pre=pre, kk=kk, kc=kc: e.matmul(pre[:, :T], self.BBre[:, kk, :], dub[:, kc, :T], start=True, stop=True), [self.BBre, dub.ch(kc)], [pre])
        k.op('pe', lambda e, pim=pim, kk=kk, kc=kc: e.matmul(pim[:, :T], self.BBim[:, kk, :], dub[:, kc, :T], start=True, stop=True), [self.BBim, dub.ch(kc)], [pim])
        cc = self.cc4
        for si, (s0, ln, sq_) in enumerate(segs):
            SRE, SIM = stre[si], stim[si]
            L = min(128, ln)
            nb = max(1, ln // 128)
            sl = slice(s0, s0 + ln)

            def bc(tab, kk=kk, L=L, nb=nb):
                a = tab[:, kk, 0:L]
                return bass.AP(a.tensor, a.offset, [list(a.ap[0]), [0, nb], list(a.ap[1])])

            def vw(buf, sl=sl, nb=nb):
                return buf[:, sl].rearrange("p (b l) -> p b l", b=nb)
            ecb, esb = bc(Ec), bc(Es)
            k.op('dve', lambda e, t1=t1, pre=pre, ecb=ecb, vw=vw: e.tensor_tensor(vw(t1), vw(pre), ecb, ALU.mult), [pre, Ec], [t1])
            k.op('dve', lambda e, t2=t2, pim=pim, esb=esb, vw=vw: e.tensor_tensor(vw(t2), vw(pim), esb, ALU.mult), [pim, Es], [t2])
            k.op('dve', lambda e, vre=vre, t1=t1, t2=t2, sl=sl: e.tensor_tensor(vre[:, sl], t1[:, sl], t2[:, sl], ALU.add), [t1, t2], [vre])
            k.op('dve', lambda e, t1=t1, pim=pim, ecb=ecb, vw=vw: e.tensor_tensor(vw(t1), vw(pim), ecb, ALU.mult), [pim, Ec], [t1])
            k.op('dve', lambda e, t2=t2, pre=pre, esb=esb, vw=vw: e.tensor_tensor(vw(t2), vw(pre), esb, ALU.mult), [pre, Es], [t2])
            k.op('dve', lambda e, vim=vim, t1=t1, t2=t2, sl=sl: e.tensor_tensor(vim[:, sl], t1[:, sl], t2[:, sl], ALU.subtract), [t1, t2], [vim])
            ecl, esl = Ec[:, kk, L - 1:L], Es[:, kk, L - 1:L]
            for bi_ in range(nb):
                b0 = s0 + bi_ * L
                bs = slice(b0, b0 + L)
                rtb = rtile[:, 0:L]
                k.op('dve', lambda e, wre=wre, vre=vre, bs=bs, rtb=rtb, kk=kk, SRE=SRE: e.tensor_tensor_scan(wre[:, bs], rtb, vre[:, bs], SRE[:, kk:kk + 1], ALU.mult, ALU.add),
                     [vre, rtile, SRE], [wre])
                k.op('dve', lambda e, wim=wim, vim=vim, bs=bs, rtb=rtb, kk=kk, SIM=SIM: e.tensor_tensor_scan(wim[:, bs], rtb, vim[:, bs], SIM[:, kk:kk + 1], ALU.mult, ALU.add),
                     [vim, rtile, SIM], [wim])
                wrl, wil = wre[:, b0 + L - 1:b0 + L], wim[:, b0 + L - 1:b0 + L]
                k.op('dve', lambda e, wil=wil, esl=esl, kk=kk: e.tensor_tensor(cc[:, kk, 0:1], wil, esl, ALU.mult), [wim, Es], [cc])
                k.op('dve', lambda e, wil=wil, ecl=ecl, kk=kk: e.tensor_tensor(cc[:, kk, 1:2], wil, ecl, ALU.mult), [wim, Ec], [cc])
                k.op('dve', lambda e, wrl=wrl, ecl=ecl, kk=kk, SRE=SRE: e.scalar_tensor_tensor(SRE[:, kk:kk + 1], wrl, ecl, cc[:, kk, 0:1], ALU.mult, ALU.subtract),
                     [wre, Ec, cc], [SRE])
                k.op('dve', lambda e, wrl=wrl, esl=esl, kk=kk, SIM=SIM: e.scalar_tensor_tensor(SIM[:, kk:kk + 1], wrl, esl, cc[:, kk, 1:2], ALU.mult, ALU.add),
                     [wre, Es, cc], [SIM])
            k.op('dve', lambda e, t1=t1, wre=wre, ecb=ecb, vw=vw: e.tensor_tensor(vw(t1), vw(wre), ecb, ALU.mult), [wre, Ec], [t1])
            k.op('dve', lambda e, t2=t2, wim=wim, esb=esb, vw=vw: e.tensor_tensor(vw(t2), vw(wim), esb, ALU.mult), [wim, Es], [t2])
            k.op('dve', lambda e, vre=vre, t1=t1, t2=t2, sl=sl: e.tensor_tensor(vre[:, sl], t1[:, sl], t2[:, sl], ALU.subtract), [t1, t2], [vre])
            k.op('dve', lambda e, t1=t1, wre=wre, esb=esb, vw=vw: e.tensor_tensor(vw(t1), vw(wre), esb, ALU.mult), [wre, Es], [t1])
            k.op('dve', lambda e, t2=t2, wim=wim, ecb=ecb, vw=vw: e.tensor_tensor(vw(t2), vw(wim), ecb, ALU.mult), [wim, Ec], [t2])
            k.op('dve', lambda e, vim=vim, t1=t1, t2=t2, sl=sl: e.tensor_tensor(vim[:, sl], t1[:, sl], t2[:, sl], ALU.add), [t1, t2], [vim])
        k.op('act', lambda e, vre=vre, sreb=sreb: e.activation(sreb[:, :T], vre[:, :T], AF.Copy), [vre], [sreb])
        k.op('act', lambda e, vim=vim, simb=simb: e.activation(simb[:, :T], vim[:, :T], AF.Copy), [vim], [simb])
        yb = ybank[kc]
        k4 = kk % 4
        k.op('pe', lambda e, yb=yb, kk=kk, sreb=sreb, k4=k4: e.matmul(yb[:, :T], self.CTre[:, kk, :], sreb[:, :T], start=(k4 == 0), stop=False), [sreb, self.CTre], [yb])
        k.op('pe', lambda e, yb=yb, kk=kk, simb=simb, k4=k4: e.matmul(yb[:, :T], self.CTimn[:, kk, :], simb[:, :T], start=False, stop=(k4 == 3)), [simb, self.CTimn], [yb])
    self.banks = nb_save
    for si, (s0, ln, sq_) in enumerate(segs):
        if sq_ is not None:
            k.dma(self.s_sre[sq_].rearrange("(k g) p -> (g p) k", g=2), stre[si][:], [stre[si]], [self.yp.d(1)], is_out=True, acc=True, allow_slow_non_contiguous=True)
            k.dma(self.s_sim[sq_].rearrange("(k g) p -> (g p) k", g=2), stim[si][:], [stim[si]], [self.yp.d(1)], is_out=True, acc=True, allow_slow_non_contiguous=True)
        elif last:
            k.dma(self.p_sre[:].rearrange("(k g) p -> (g p) k", g=2), self.sre[:], [self.sre], [self.yp.d(1)], is_out=True, acc=True, allow_slow_non_contiguous=True)
            k.dma(self.p_sim[:].rearrange("(k g) p -> (g p) k", g=2), self.sim[:], [self.sim], [self.yp.d(1)], is_out=True, acc=True, allow_slow_non_contiguous=True)
    thr_s5 = k.end_thread()
    k.merge(thr_rg, thr_s5)
    yf, ygb = self.yf, self.ygb
    for j in range(2):
        yb = ybank[j]
        k.op('dve', lambda e, yb=yb, j=j: e.scalar_tensor_tensor(yf[:, j, :T], duf[:, j, :T], self.dS5[:, j:j + 1], yb[:, :T], ALU.mult, ALU.add),
             [yb, duf.ch(j), self.dS5], [yf.ch(j)])
        u = self.u_[j]
        k.op('dve', lambda e, u=u, j=j: e.tensor_copy(u[:, :T], yf[:, j, :T]), [yf.ch(j)], [u])
        gelu(self, yf[:, j, :T], u[:, :T], T, u, yf.ch(j))
        k.op('act', lambda e, j=j: e.activation(ygb[:, j, :T], yf[:, j, :T], AF.Copy), [yf.ch(j)], [ygb.ch(j)])
    for o in range(2):
        ps = self.bank()
        self.mm(ps, ps[:, :T], [(self.wglu[:, j, o * 128:(o + 1) * 128], ygb[:, j, :T]) for j in range(2)], [self.wglu, ygb])
        g2 = self.g2
        k.op('act', lambda e, ps=ps, o=o: e.activation(g2[:, :T], ps[:, :T], AF.Sigmoid, bias=self.bglu[:, o:o + 1]), [ps, self.bglu], [g2])
        k.op('dve', lambda e, o=o: e.tensor_tensor(mixT[:, 6 + o, :T], yf[:, o, :T], g2[:, :T], ALU.mult), [yf.ch(o), g2], [mixT.ch(6 + o)])


def l1_zero_state(self):
    k = self.k
    for b in (self.hst, self.crprev, self.sre, self.sim):
        k.op('pool', lambda e, b=b: e.memset(b[:], 0.0), [], [b])

import numpy as np
import ml_dtypes
from concourse.bass_utils import run_bass_kernel_spmd

D = 1024
NJ = 8
FF = 2816
NF = 22
EPS = 1e-6
SLOT = 16384

C_ID = 0
C_TRI = 128
C_ONES = 256
C_SEL = 768
C_RST = 1280
C_MNEG = 1792
NCONST = 1792 + 128
CB_ID, CB_TRI, CB_ONES = 0, 128, 256


def make_consts():
    c = np.zeros((128, NCONST), np.float32)
    c[:, C_ID:C_ID + 128] = np.eye(128)
    c[:, C_ONES:C_ONES + 512] = 1.0
    s = np.arange(128)[:, None]
    t = np.arange(128)[None, :]
    c[:, C_TRI:C_TRI + 128] = (s <= t)
    c[:, C_MNEG:C_MNEG + 128] = np.where(s <= t, 0.0, -1e30)
    for h in range(4):
        c[h, C_SEL + h * 128:C_SEL + (h + 1) * 128] = 1.0
    r = np.ones(512, np.float32)
    r[::64] = 0.0
    c[:, C_RST:C_RST + 512] = r[None, :]
    return c


class Builder:
    def __init__(self, seq, do_mix=True, nlayers=2):
        self.seq = seq
        self.nc = bass.Bass("TRN2", target_bir_lowering=False)
        self.k = K(self.nc)
        self.do_mix = do_mix
        self.nlayers = nlayers
        self.bank_rr = 0

    def bank(self):
        b = self.banks[self.bank_rr % len(self.banks)]
        self.bank_rr += 1
        return b

    def din(self, name, shape, dtype=F32):
        return self.k.dram(name, shape, dtype, kind="ExternalInput")

    def dout(self, name, shape, dtype=F32):
        return self.k.dram(name, shape, dtype, kind="ExternalOutput")

    def mm(self, ps, out_ap, pairs, reads):
        n = len(pairs)
        for i, (l, r) in enumerate(pairs):
            self.k.op('pe', (lambda e, l=l, r=r, i=i: e.matmul(out_ap, l, r, start=(i == 0), stop=(i == n - 1))),
                      reads=reads if (i == 0 or i == n - 1) else (), writes=[ps])

    def build(self):
        k, nc = self.k, self.nc
        seq = self.seq
        self.xp = self.din("xp", [seq, D])
        self.xs = self.din("xs", [64, D])
        self.consts_d = self.din("consts", [128, NCONST])
        self.w_gate = self.din("ffn_w_gate", [4, D, FF])
        self.w_up = self.din("ffn_w_up", [4, D, FF])
        self.w_down = self.din("ffn_w_down", [4, FF, D])
        self.norm_g = self.din("norm_g", [7, D])
        self.yp = self.dout("yp", [seq, D])
        self.ys = self.dout("ys", [64, D])
        self.ab_w_in = self.din("ab_w_in", [D, 4104])
        self.ab_b_in = self.din("ab_b_in", [4104])
        self.mlstm_norm_g = self.din("mlstm_norm_g", [512])
        self.hgrn_norm_g = self.din("hgrn_norm_g", [512])
        self.hgrn_lb_logits = self.din("hgrn_lb_logits", [3, 512])
        self.ab_w_out = self.din("ab_w_out", [D, D])
        self.st_mC = self.din("st_mC", [2, 4, 128, 128])
        self.st_mn = self.din("st_mn", [2, 4, 128])
        self.st_mm = self.din("st_mm", [2, 4])
        self.st_hS = self.din("st_hS", [2, 4, 128, 128])
        self.p_mC = self.dout("p_mC", [4, 128, 128])
        self.p_mn = self.dout("p_mn", [4, 128])
        self.p_mm = self.dout("p_mm", [4])
        self.p_hS = self.dout("p_hS", [4, 128, 128])
        self.s_mC = self.dout("s_mC", [2, 4, 128, 128])
        self.s_mn = self.dout("s_mn", [2, 4, 128])
        self.s_mm = self.dout("s_mm", [2, 4])
        self.s_hS = self.dout("s_hS", [2, 4, 128, 128])
        self.cd_w_in = self.din("cd_w_in", [D, 1792])
        self.cd_b_in = self.din("cd_b_in", [1792])
        self.conv_w = self.din("conv_w", [4, 768])
        self.conv_b = self.din("conv_b", [768])
        self.rg_w_a = self.din("rg_w_a", [12, 64, 64])
        self.rg_b_a = self.din("rg_b_a", [768])
        self.rg_w_x = self.din("rg_w_x", [12, 64, 64])
        self.rg_b_x = self.din("rg_b_x", [768])
        self.rg_lambda = self.din("rg_lambda", [768])
        self.s5_A_re = self.din("s5_A_re", [16, 64])
        self.s5_A_im = self.din("s5_A_im", [16, 64])
        self.s5_log_dt = self.din("s5_log_dt", [16])
        self.s5_B_re = self.din("s5_B_re", [16, 64, 16])
        self.s5_B_im = self.din("s5_B_im", [16, 64, 16])
        self.s5_C_re = self.din("s5_C_re", [16, 16, 64])
        self.s5_C_im = self.din("s5_C_im", [16, 16, 64])
        self.s5_D = self.din("s5_D", [256])
        self.s5_w_glu = self.din("s5_w_glu", [256, 256])
        self.s5_b_glu = self.din("s5_b_glu", [256])
        self.cd_w_out = self.din("cd_w_out", [D, D])
        self.st_rh = self.din("st_rh", [2, 768])
        self.st_rc = self.din("st_rc", [2, 3, 768])
        self.st_sre = self.din("st_sre", [2, 16, 64])
        self.st_sim = self.din("st_sim", [2, 16, 64])
        self.p_rh = self.dout("p_rh", [768])
        self.p_rc = self.dout("p_rc", [3, 768])
        self.p_sre = self.dout("p_sre", [16, 64])
        self.p_sim = self.dout("p_sim", [16, 64])
        self.s_rh = self.dout("s_rh", [2, 768])
        self.s_rc = self.dout("s_rc", [2, 3, 768])
        self.s_sre = self.dout("s_sre", [2, 16, 64])
        self.s_sim = self.dout("s_sim", [2, 16, 64])
        self.pieces = self.piece_defs()
        self.pidx = {p[0]: i for i, p in enumerate(self.pieces)}
        self.wsc = k.dram("wsc", [len(self.pieces), 128, 8192], BF16)
        self.cst = k.sb("cst", [128, NCONST], F32)
        self.cstb = k.sb("cstb", [128, 384], BF16)
        self.gcol = k.sb("gcol", [128, 7, NJ], F32)
        self.epsc = k.sb("epsc", [128, 8], F32)
        self.x = k.sb("x", [128, NJ, 512], F32)
        self.xn = k.sb("xn", [128, NJ, 512], BF16)
        self.sq = k.sb("sq", [128, 2, 512], BF16)
        self.rstd = k.sb("rstd", [128, 512], F32)
        self.h = k.sb("h", [128, NF, 512], BF16)
        self.sg = [k.sb("sg%d" % i, [128, 512], F32) for i in range(2)]
        self.xin = [k.sb("xin%d" % i, [128, D], F32, at=self.h.addr + 4096 * i) for i in range(2)]
        self.wslot = [k.sb("wslot%d" % i, [128, SLOT // 2], BF16) for i in range(2)]
        self.wslot_rr = 0
        self.banks = [k.ps("bank%d" % i) for i in range(8)]
        self.arena0 = (k.sb_off + 63) // 64 * 64
        self.ARENA = 70 * 1024
        k.sb_off = self.arena0 + self.ARENA
        self.ident = self.cst[:, C_ID:C_ID + 128]
        self.ones_b = self.cstb[:, CB_ONES:CB_ONES + 128]
        self.ident_b = self.cstb[:, CB_ID:CB_ID + 128]
        self.C_ID, self.C_TRI, self.C_ONES, self.C_SEL, self.C_RST, self.C_MNEG = C_ID, C_TRI, C_ONES, C_SEL, C_RST, C_MNEG
        self.CB_ID, self.CB_TRI, self.CB_ONES = CB_ID, CB_TRI, CB_ONES

        k.dma(self.cst[:], self.consts_d[:], [], [self.cst])
        k.op('dve', lambda e: e.tensor_copy(self.cstb[:], self.cst[:, 0:384]), [self.cst], [self.cstb])
        k.op('dve', lambda e: e.memset(self.epsc[:], EPS), [], [self.epsc])
        k.dma(self.gcol[:], self.norm_g[:].rearrange("n (j p) -> p n j", p=128), [], [self.gcol],
              allow_slow_non_contiguous=True)
        self.cast_weights()
        l0_setup(self)
        assert self.arena0_end <= self.arena0 + self.ARENA, self.arena0_end - self.arena0
        if self.nlayers > 1:
            l1_alloc(self)
            l1_setup(self)
        tiles = [("s", 0, 64)] + [("p", t0, min(512, seq - t0)) for t0 in range(0, seq, 512)]
        for ti, (kind, t0, T) in enumerate(tiles):
            if ti == 1:
                l0_zero_state(self)
                if self.nlayers > 1:
                    l1_zero_state(self)
            self.tile(kind, t0, T, last=(ti == len(tiles) - 1))
        return k.emit()

    def rsqrt(self, out, ps, T, scale):
        k = self.k
        k.op('act', lambda e: e.activation(out[:, :T], ps[:, :T], AF.Sqrt, bias=self.epsc[:, 0:1], scale=scale), [ps, self.epsc], [out])
        k.op('dve', lambda e: e.reciprocal(out[:, :T], out[:, :T]), [out], [out])

    def piece_defs(self):
        P = []

        def ffn(li):
            f0 = 0
            pi = 0
            while f0 < NF:
                nfc = min(4, NF - f0)
                w = nfc * 128
                P.append((("gu", li, pi), [(self.w_gate, li, 8, f0 * 128, f0 * 128 + w, 0), (self.w_up, li, 8, f0 * 128, f0 * 128 + w, 8 * w)], 16 * w))
                f0 += nfc
                pi += 1
            for q4 in range(4):
                P.append((("dn", li, q4), [(self.w_down, li, NF, q4 * 256, (q4 + 1) * 256, 0)], NF * 256))
        ffn(0)
        for i, (c0, c1) in enumerate(((0, 1024), (1024, 2048), (2048, 2568), (2568, 3592), (3592, 4104))):
            P.append((("ab", i), [(self.ab_w_in, None, 8, c0, c1, 0)], 8 * (c1 - c0)))
        P.append((("abo",), [(self.ab_w_out, None, 8, 0, 1024, 0)], 8192))
        ffn(1)
        ffn(2)
        for i, (c0, c1) in enumerate(((0, 1024), (1024, 1792))):
            P.append((("cd", i), [(self.cd_w_in, None, 8, c0, c1, 0)], 8 * (c1 - c0)))
        P.append((("cdo",), [(self.cd_w_out, None, 8, 0, 1024, 0)], 8192))
        ffn(3)
        return P

    def cast_weights(self):
        k = self.k
        A = self.arena0 + 20480
        stf = [k.sb("cstf%d" % i, [128, 4096], F32, at=A + 16384 * i) for i in range(2)]
        stb = [k.sb("cstb_%d" % i, [128, 4096], BF16, at=A + 32768 + 8192 * i) for i in range(2)]
        job = 0
        pend = None
        engs = ['act', 'dve', 'pool']
        for pi, (key, parts, n) in enumerate(self.pieces):
            for (src, lead, nA, c0, c1, doff) in parts:
                w = c1 - c0
                g = max(1, 4096 // w)
                for a0 in range(0, nA, g):
                    na = min(g, nA - a0)
                    sf, sb_ = stf[job % 2], stb[job % 2]
                    sap = (src[lead, a0 * 128:(a0 + na) * 128, c0:c1] if lead is not None else src[a0 * 128:(a0 + na) * 128, c0:c1])
                    k.dma(sf[:, 0:na * w].rearrange("p (a c) -> p a c", a=na), sap.rearrange("(a p) c -> p a c", p=128), [], [sf])
                    eng = engs[job % 3]
                    if eng == 'act':
                        k.op('act', lambda e, sf=sf, sb_=sb_, m=na * w: e.activation(sb_[:, 0:m], sf[:, 0:m], AF.Copy), [sf], [sb_])
                    else:
                        k.op(eng, lambda e, sf=sf, sb_=sb_, m=na * w: e.tensor_copy(sb_[:, 0:m], sf[:, 0:m]), [sf], [sb_])
                    if pend is not None:
                        k.dma(pend[0], pend[1], [pend[2]], [pend[3]], eng='sp', acc=True)
                    pend = (self.wsc[pi, :, doff + a0 * w:doff + (a0 + na) * w], sb_[:, 0:na * w], sb_, self.wsc.d(pi))
                    job += 1

        if pend is not None:
            k.dma(pend[0], pend[1], [pend[2]], [pend[3]], eng='sp', acc=True)

    def wload(self, key):
        pi = self.pidx[key]
        n = self.pieces[pi][2]
        slot = self.wslot[self.wslot_rr % 2]
        self.wslot_rr += 1
        self.k.dma(slot[:, 0:n], self.wsc[pi, :, 0:n], [self.wsc.d(pi)], [slot])
        return slot

    def rmsnorm(self, T, gi):
        k = self.k
        x, xn, sq = self.x, self.xn, self.sq
        ps = self.bank()
        for j in range(NJ):
            k.op('act', lambda e, j=j: e.activation(sq[:, j % 2, :T], x[:, j, :T], AF.Square), [x.ch(j)], [sq.ch(j % 2)])
            k.op('pe', lambda e, j=j, ps=ps: e.matmul(ps[:, :T], self.ones_b, sq[:, j % 2, :T], start=(j == 0), stop=(j == NJ - 1)),
                 [sq.ch(j % 2), self.cstb], [ps])
        rstd = self.rstd
        self.rsqrt(rstd, ps, T, 1.0 / D)
        for j in range(NJ):
            eng = 'dve'
            k.op(eng, lambda e, j=j: e.scalar_tensor_tensor(xn[:, j, :T], x[:, j, :T], self.gcol[:, gi, j:j + 1],
                                                            rstd[:, :T], ALU.mult, ALU.mult),
                 [x.ch(j), rstd, self.gcol], [xn.ch(j)])

    def ffn(self, T, li, gi):
        k = self.k
        self.rmsnorm(T, gi)
        xn, h, x = self.xn, self.h, self.x
        f0 = 0
        while f0 < NF:
            nfc = min(4, NF - f0)
            w = nfc * 128
            slot = self.wload(("gu", li, f0 // 4))
            gv = slot[:, 0:NJ * w].rearrange("p (j c) -> p j c", j=NJ)
            uv = slot[:, NJ * w:2 * NJ * w].rearrange("p (j c) -> p j c", j=NJ)
            for fi in range(nfc):
                f = f0 + fi
                pg, pu = self.bank(), self.bank()
                self.mm(pg, pg[:, :T], [(gv[:, j, fi * 128:(fi + 1) * 128], xn[:, j, :T]) for j in range(NJ)], [slot, xn])
                self.mm(pu, pu[:, :T], [(uv[:, j, fi * 128:(fi + 1) * 128], xn[:, j, :T]) for j in range(NJ)], [slot, xn])
                sg = self.sg[f % 2]
                k.op('act', lambda e, sg=sg, pg=pg: e.activation(sg[:, :T], pg[:, :T], AF.Silu), [pg], [sg])
                k.op('dve', lambda e, sg=sg, pu=pu, f=f: e.tensor_tensor(h[:, f, :T], sg[:, :T], pu[:, :T], ALU.mult),
                     [sg, pu], [h.ch(f)])
            f0 += nfc
        for q4 in range(4):
            slot = self.wload(("dn", li, q4))
            wv = slot[:, 0:NF * 256].rearrange("p (f c) -> p f c", f=NF)
            for di in range(2):
                d = q4 * 2 + di
                po = self.bank()
                self.mm(po, po[:, :T], [(wv[:, f, di * 128:(di + 1) * 128], h[:, f, :T]) for f in range(NF)], [slot, h])
                k.op('dve', lambda e, po=po, d=d: e.scalar_tensor_tensor(x[:, d, :T], po[:, :T], 0.5, x[:, d, :T],
                                                                        ALU.mult, ALU.add),
                     [po, x.ch(d)], [x.ch(d)])

    def load_x(self, src, t0, T):
        k = self.k
        x = self.x
        nsub = (T + 127) // 128
        for s in range(nsub):
            n = min(128, T - s * 128)
            xin = self.xin[s % 2]
            k.dma(xin[:n, :], src[t0 + s * 128:t0 + s * 128 + n, :], [], [xin])
        raise NotImplementedError

    def tile(self, kind, t0, T, last=False):
        k = self.k
        x = self.x
        src = self.xs if kind == "s" else self.xp
        dst = self.ys if kind == "s" else self.yp
        nsub = (T + 127) // 128
        for s in range(nsub):
            n = min(128, T - s * 128)
            xin = self.xin[s % 2]
            k.dma(xin[:n, :], src[t0 + s * 128:t0 + s * 128 + n, :], [], [xin])
            for j in range(NJ):
                ps = self.bank()
                k.op('pe', lambda e, ps=ps, xin=xin, j=j, n=n: e.matmul(ps[:, :n], xin[:n, j * 128:(j + 1) * 128],
                                                                        self.ident[:n, :n], start=True, stop=True),
                     [xin, self.cst], [ps])
                eng = 'act' if j % 2 == 0 else 'dve'
                if eng == 'act':
                    k.op('act', lambda e, ps=ps, j=j, s=s, n=n: e.activation(x[:, j, s * 128:s * 128 + n], ps[:, :n], AF.Copy),
                         [ps], [x.ch(j)])
                else:
                    k.op('dve', lambda e, ps=ps, j=j, s=s, n=n: e.tensor_copy(x[:, j, s * 128:s * 128 + n], ps[:, :n]),
                         [ps], [x.ch(j)])
        for l in range(self.nlayers):
            self.ffn(T, 2 * l, 3 * l)
            if self.do_mix:
                self.mixer(kind, t0, T, l, last)
            self.ffn(T, 2 * l + 1, 3 * l + 2)
        self.final_out(dst, t0, T)

    def final_out(self, dst, t0, T):
        k = self.k
        x, sq = self.x, self.sq
        ps = self.bank()
        for j in range(NJ):
            k.op('act', lambda e, j=j: e.activation(sq[:, j % 2, :T], x[:, j, :T], AF.Square), [x.ch(j)], [sq.ch(j % 2)])
            k.op('pe', lambda e, j=j, ps=ps: e.matmul(ps[:, :T], self.ones_b, sq[:, j % 2, :T], start=(j == 0), stop=(j == NJ - 1)),
                 [sq.ch(j % 2), self.cstb], [ps])
        rstd = self.rstd
        self.rsqrt(rstd, ps, T, 1.0 / D)
        for j in range(NJ):
            eng = 'dve'
            k.op(eng, lambda e, j=j: e.scalar_tensor_tensor(x[:, j, :T], x[:, j, :T], self.gcol[:, 6, j:j + 1],
                                                            rstd[:, :T], ALU.mult, ALU.mult),
                 [x.ch(j), rstd, self.gcol], [x.ch(j)])
        nsub = (T + 127) // 128
        for s in range(nsub):
            n = min(128, T - s * 128)
            yo = self.xin[s % 2]
            for half in range(2):
                ps = self.bank()
                for jj in range(4):
                    j = half * 4 + jj
                    k.op('pe', lambda e, ps=ps, j=j, jj=jj, s=s, n=n: e.matmul(
                        ps[:n, jj * 128:(jj + 1) * 128], x[:, j, s * 128:s * 128 + n], self.ident, start=True, stop=True),
                        [x.ch(j), self.cst], [ps])
                if half == 0:
                    k.op('act', lambda e, ps=ps, yo=yo, n=n: e.activation(yo[:n, 0:512], ps[:n, :], AF.Copy), [ps], [yo])
                else:
                    k.op('dve', lambda e, ps=ps, yo=yo, n=n: e.tensor_copy(yo[:n, 512:1024], ps[:n, :]), [ps], [yo])
            k.dma(dst[t0 + s * 128:t0 + s * 128 + n, :], yo[:n, :], [yo], [dst.d(0)], is_out=True, acc=True)

    def mixer(self, kind, t0, T, l, last):
        segs = [(0, 32, 0), (32, 32, 1)] if kind == "s" else [(0, T, None)]
        if l == 0:
            l0_mixer(self, kind, t0, T, segs, last)
        else:
            l1_mixer(self, kind, t0, T, segs, last)
            out_proj(self, T, ("cdo",))


SEQ_FULL = 8192
_PER_CORE_IN = ('cd_w_in', 'cd_b_in', 'conv_w', 'conv_b', 'rg_w_a', 'rg_b_a', 'rg_w_x', 'rg_b_x', 'rg_lambda', 's5_A_re', 's5_A_im',
                's5_log_dt', 's5_B_re', 's5_B_im', 's5_C_re', 's5_C_im', 's5_D', 's5_w_glu', 's5_b_glu', 'cd_w_out')


def _in_map(inp, c, seq, consts):
    f = lambda a: np.ascontiguousarray(np.asarray(a, dtype=np.float32))
    ng = np.concatenate([np.asarray(inp['norm_g']).reshape(6, 1024), np.asarray(inp['final_norm_g'])[None]], 0)
    m = dict(xp=f(inp['x_prompt'][c, :seq]), xs=f(np.asarray(inp['x_sample'][2 * c:2 * c + 2]).reshape(64, 1024)), consts=consts,
             ffn_w_gate=f(np.asarray(inp['ffn_w_gate']).reshape(4, 1024, 2816)), ffn_w_up=f(np.asarray(inp['ffn_w_up']).reshape(4, 1024, 2816)),
             ffn_w_down=f(np.asarray(inp['ffn_w_down']).reshape(4, 2816, 1024)), norm_g=f(ng),
             ab_w_in=f(inp['ab_w_in'][0]), ab_b_in=f(inp['ab_b_in'][0]), mlstm_norm_g=f(inp['mlstm_norm_g'][0]), hgrn_norm_g=f(inp['hgrn_norm_g'][0]),
             hgrn_lb_logits=f(inp['hgrn_lb_logits']), ab_w_out=f(inp['ab_w_out'][0]),
             st_mC=f(inp['state_mlstm_C'][0, 2 * c:2 * c + 2]), st_mn=f(inp['state_mlstm_n'][0, 2 * c:2 * c + 2]),
             st_mm=f(inp['state_mlstm_m'][0, 2 * c:2 * c + 2]), st_hS=f(inp['state_hgrn_S'][0, 2 * c:2 * c + 2]),
             st_rh=f(inp['state_rglru_h'][0, 2 * c:2 * c + 2]), st_rc=f(inp['state_rglru_conv'][0, 2 * c:2 * c + 2]),
             st_sre=f(inp['state_s5_re'][0, 2 * c:2 * c + 2]), st_sim=f(inp['state_s5_im'][0, 2 * c:2 * c + 2]))
    for nm in _PER_CORE_IN:
        m[nm] = f(inp[nm][0])
    return m


def kernel(**inputs):
    n = 8
    seq = int(np.asarray(inputs['x_prompt']).shape[1])
    b = Builder(seq, do_mix=True, nlayers=2)
    b.build()
    consts = make_consts()
    in_maps = [_in_map(inputs, c, seq, consts) for c in range(n)]
    res = run_bass_kernel_spmd(b.nc, in_maps, core_ids=list(range(n)))
    R = res.results
    st = lambda nm: np.stack([np.asarray(R[c][nm], dtype=np.float32) for c in range(n)], 0)
    cat = lambda nm: np.concatenate([np.asarray(R[c][nm], dtype=np.float32) for c in range(n)], 0)
    y_prompt = st('yp')
    y_sample = cat('ys').reshape(16, 32, 1024)
    outs = [y_prompt, y_sample]
    for nm in ('p_mC', 'p_mn', 'p_mm', 'p_hS', 'p_rh', 'p_rc', 'p_sre', 'p_sim'):
        outs.append(st(nm)[None])
    for nm in ('s_mC', 's_mn', 's_mm', 's_hS', 's_rh', 's_rc', 's_sre', 's_sim'):
        outs.append(cat(nm)[None])
    return tuple(outs)
```

```python
import numpy as np
import concourse.bass as bass
import concourse.mybir as mybir

F32 = mybir.dt.float32
BF16 = mybir.dt.bfloat16
AF = mybir.ActivationFunctionType
ALU = mybir.AluOpType
PAGE = 64
LAT_X = 1.2
LAT_S = 0.25
NDMASEM = 24
DT_SIZE = {F32: 4, BF16: 2}


class Buf:
    def __init__(self, name, th, space, addr, nbytes, shape, dtype):
        self.name, self.th, self.space, self.addr, self.nbytes = name, th, space, addr, nbytes
        self.shape, self.dtype = shape, dtype

    def __getitem__(self, k):
        return self.th[k]

    def res(self, lo=None, hi=None):
        if self.space == 'ps':
            return [('ps', self.addr)]
        if self.space == 'dram':
            return [('dram', self.name)]
        lo = 0 if lo is None else lo
        hi = self.nbytes if hi is None else hi
        return [('sb', p) for p in range((self.addr + lo) // PAGE, (self.addr + hi - 1) // PAGE + 1)]

    def d(self, key):
        return DR(self.name, key)

    def ch(self, j, n=1):
        per = self.nbytes // self.shape[1]
        return R(self, j * per, (j + n) * per)


class BV:
    def __init__(self, buf, off):
        self.buf, self.off = buf, off

    def __getitem__(self, key):
        rows, cols = key
        a = (cols.start or 0) + self.off
        b = cols.stop + self.off
        return self.buf[rows, a:b]

    def res(self):
        return self.buf.res()


class R:
    def __init__(self, buf, lo, hi):
        self.buf, self.lo, self.hi = buf, lo, hi

    def res(self):
        return self.buf.res(self.lo, self.hi)


class DR:
    def __init__(self, name, key):
        self.k = ('dram', name, key)

    def res(self):
        return [self.k]


def _res(x):
    return x.res()


class _Rec:
    def __init__(self):
        self.call = None

    def __getattr__(self, name):
        def f(*a, **kw):
            self.call = (name, a, kw)
            return None
        return f


def _dur(call, eng, is_dma):
    try:
        name, a, kw = call
        out = kw.get('out', a[0] if a else None)
        n = 1
        for d in out.shape[1:]:
            n *= d
        if is_dma:
            return 128 * n * 3 / 250e3
        if eng == 'pe':
            rhs = a[2] if len(a) > 2 else kw.get('rhs')
            m = 1
            for d in rhs.shape[1:]:
                m *= d
            return max(0.07, m * 0.00042 * (4 if rhs.dtype == F32 else 1))
        if name == 'tensor_tensor_scan':
            return 0.07 + 2 * n * 0.00105
        if eng == 'act':
            return 0.15 + n * 0.00085
        if eng == 'pool':
            return 0.1 + n * 0.01
        return 0.07 + n * 0.00105
    except Exception:
        return 0.3


def _capture(fn):
    if fn is None or isinstance(fn, tuple):
        return fn
    p = _Rec()
    fn(p)
    assert p.call is not None
    return p.call


class Op:
    __slots__ = ('eng', 'fn', 'is_dma', 'seq', 'waits', 'signal', 'semval', 'dsem', 'gid', 't_end')

    def __init__(self, eng, fn, is_dma):
        self.eng, self.fn, self.is_dma = eng, fn, is_dma
        self.waits = []
        self.signal = False
        self.semval = None
        self.dsem = None


class K:
    ENGS = ['pe', 'act', 'dve', 'pool', 'sp']

    def __init__(self, nc):
        self.nc = nc
        self.ops = {e: [] for e in self.ENGS}
        self.writers = {}
        self.readers = {}
        self.seen = {e: {} for e in self.ENGS}
        self.seen_dma = {e: set() for e in self.ENGS}
        self.sb_off = 16640
        self.ps_banks = 0
        self.ndma = 0
        self.dma_ring = [None] * NDMASEM
        self.out_dmas = []
        self.nops = 0
        self.rec = None
        self.eng_free = {e: 0.0 for e in self.ENGS}

    def sb(self, name, shape, dtype, at=None):
        nbytes = int(np.prod(shape[1:])) * DT_SIZE[dtype]
        if at is None:
            addr = (self.sb_off + 63) // 64 * 64
            self.sb_off = addr + nbytes
        else:
            addr = at
        assert addr + nbytes <= 221000, (name, addr, nbytes)
        th = self.nc.alloc_sbuf_tensor_at(name, list(shape), dtype, offset=addr)
        return Buf(name, th, 'sb', addr, nbytes, shape, dtype)

    def ps(self, name, shape=(128, 512), dtype=F32):
        th = self.nc.alloc_psum_tensor(name, list(shape), dtype)
        b = Buf(name, th, 'ps', self.ps_banks, 2048, shape, dtype)
        self.ps_banks += 1
        return b

    def dram(self, name, shape, dtype, kind="Internal"):
        th = self.nc.dram_tensor(name, list(shape), dtype, kind=kind)
        return Buf(name, th, 'dram', 0, 0, shape, dtype)

    def _dep(self, op, d, selfsync):
        if d is op:
            return
        if d.is_dma:
            if id(d) in self.seen_dma[op.eng]:
                return
            self.seen_dma[op.eng].add(id(d))
            op.waits.append(d)
            return
        if d.eng == op.eng and not op.is_dma:
            if not selfsync or op.eng == 'pe':
                return
        s = self.seen[op.eng].get(d.eng, -1)
        if d.seq <= s:
            return
        self.seen[op.eng][d.eng] = d.seq
        d.signal = True
        op.waits.append(d)

    def begin_thread(self):
        self.rec = []

    def end_thread(self):
        r = self.rec
        self.rec = None
        return r

    def replay(self, item):
        eng, fn, reads, writes, is_dma, selfsync, acc, is_out = item
        o = self.op(eng, fn, reads, writes, is_dma=is_dma, selfsync=selfsync, acc=acc)
        if is_out:
            self.out_dmas.append(o)

    def _peek(self, item):
        eng, fn, reads, writes, is_dma, selfsync, acc, is_out = item
        t = 0.0
        W, Rd = self.writers, self.readers

        def upd(d, t):
            tt = d.t_end + (LAT_S if (d.eng == eng and not d.is_dma) else LAT_X)
            return tt if tt > t else t
        for r in reads:
            for kx in r.res():
                for d in W.get(kx, {}).values():
                    t = upd(d, t)
        if not acc:
            for w in writes:
                for kx in w.res():
                    for d in W.get(kx, {}).values():
                        t = upd(d, t)
                    for d in Rd.get(kx, {}).values():
                        t = upd(d, t)
        return max(self.eng_free[eng], t)

    def merge(self, *threads):
        threads = [t for t in threads if t]
        if len(threads) == 1:
            for it in threads[0]:
                self.replay(it)
            return
        ptr = [0] * len(threads)
        n = [len(t) for t in threads]
        total = sum(n)
        for _ in range(total):
            best, bt = None, None
            for i, t in enumerate(threads):
                if ptr[i] >= n[i]:
                    continue
                est = self._peek(t[ptr[i]])
                if bt is None or est < bt:
                    best, bt = i, est
            self.replay(threads[best][ptr[best]])
            ptr[best] += 1

    def op(self, eng, fn, reads=(), writes=(), is_dma=False, selfsync=True, acc=False, _is_out=False):
        fn = _capture(fn)
        if getattr(self, 'rec', None) is not None:
            self.rec.append((eng, fn, list(reads), list(writes), is_dma, selfsync, acc, _is_out))
            return None
        o = Op(eng, fn, is_dma)
        o.seq = len(self.ops[eng])
        o.gid = self.nops
        self.nops += 1
        deps = []
        rkeys = [k for r in reads for k in _res(r)]
        wkeys = [k for w in writes for k in _res(w)]
        for k in rkeys:
            deps.extend(self.writers.get(k, {}).values())
        raw_ids = set(id(d) for d in deps)
        for k in wkeys:
            if acc:
                continue
            deps.extend(self.writers.get(k, {}).values())
            deps.extend(self.readers.get(k, {}).values())
        if not is_dma:
            deps = [d for d in deps if id(d) in raw_ids or d.is_dma or d.eng != eng]
        t_ready = 0.0
        for d in deps:
            tt = d.t_end + (LAT_S if (d.eng == eng and not d.is_dma) else LAT_X)
            if tt > t_ready:
                t_ready = tt
        start = max(self.eng_free[eng], t_ready)
        du = _dur(fn, eng, is_dma)
        if is_dma:
            self.eng_free[eng] = start + 0.06
            o.t_end = start + 2.0 + du
        else:
            self.eng_free[eng] = start + du
            o.t_end = start + du
        seen = set()
        for d in sorted(deps, key=lambda d: -d.gid):
            if id(d) in seen:
                continue
            seen.add(id(d))
            self._dep(o, d, selfsync)
        key = ('dma', o.gid) if is_dma else eng
        for k in rkeys:
            self.readers.setdefault(k, {})[key] = o
        for k in wkeys:
            if acc:
                self.writers.setdefault(k, {})[key] = o
            else:
                self.writers[k] = {key: o}
                self.readers[k] = {}
        if is_dma:
            slot = self.ndma % NDMASEM
            prev = self.dma_ring[slot]
            if prev is not None:
                self._dep(o, prev, False)
            self.dma_ring[slot] = o
            o.dsem = (slot, 16 * (self.ndma // NDMASEM + 1))
            self.ndma += 1
        self.ops[eng].append(o)
        return o

    def dma(self, out, in_, reads, writes, eng='sp', is_out=False, acc=False, **kw):
        o = self.op(eng, lambda e: e.dma_start(out=out, in_=in_, **kw), reads, writes, is_dma=True, acc=acc, _is_out=is_out)
        if is_out and o is not None:
            self.out_dmas.append(o)
        return o

    def emit(self):
        nc = self.nc
        fin = Op('sp', None, False)
        fin.t_end = 0.0
        fin.seq = len(self.ops['sp'])
        for d in self.out_dmas:
            if id(d) not in self.seen_dma['sp']:
                fin.waits.append(d)
        self.ops['sp'].append(fin)
        sems = {e: nc.alloc_semaphore('s_' + e) for e in self.ENGS}
        dsems = [nc.alloc_semaphore('d%d' % i) for i in range(NDMASEM)]
        for e in self.ENGS:
            c = 0
            for o in self.ops[e]:
                if o.is_dma:
                    o.semval = (dsems[o.dsem[0]], o.dsem[1])
                elif o.signal:
                    c += 1
                    o.semval = (sems[e], c)
        ops = self.ops

        def run(eng, e):
            for o in ops[e]:
                for d in o.waits:
                    eng.wait_ge(d.semval[0], d.semval[1])
                if o.fn is None:
                    continue
                name, a_, kw_ = o.fn
                ins = getattr(eng, name)(*a_, **kw_)
                if o.is_dma:
                    ins.then_inc(o.semval[0], 16)
                elif o.signal:
                    ins.then_inc(o.semval[0], 1)

        with nc.Block() as block:
            @block.tensor
            def _(t):
                run(t, 'pe')

            @block.scalar
            def _(t):
                run(t, 'act')

            @block.vector
            def _(t):
                run(t, 'dve')

            @block.gpsimd
            def _(t):
                run(t, 'pool')

            @block.sync
            def _(t):
                run(t, 'sp')
        return {e: len(self.ops[e]) for e in self.ENGS}


DKS = 128 ** -0.5


def l0_setup(self):
    k = self.k
    A = self.arena0
    off = [A]

    def al(name, shape, dt):
        b = k.sb(name, shape, dt, at=off[0])
        off[0] = (off[0] + b.nbytes + 63) // 64 * 64
        return b
    self.qT = al("qT", [128, 4, 512], BF16)
    self.kT = al("kT", [128, 4, 512], BF16)
    self.Ktok = al("Ktok", [128, 4, 512], BF16)
    self.Vtok = al("Vtok", [128, 4, 512], BF16)
    self.rows = al("rows", [128, 4, 512], F32)
    self.Rr = al("Rr", [128, 3, 512], F32)
    self.hT = al("hT", [128, 4, 512], F32)
    self.qdec = al("qdec", [128, 4, 512], BF16)
    self.kinv = al("kinv", [128, 4, 512], BF16)
    self.tG = al("tG", [128, 512], F32)
    self.tK = al("tK", [128, 512], F32)
    self.eG = al("eG", [128, 512], F32)
    self.bcol = al("bcol", [128, 4, 4], F32)
    self.eGend = al("eGend", [128, 4, 8], F32)
    self.Em = [al("Em%d" % i, [128, 128], F32) for i in range(4)]
    self.Dm = [al("Dm%d" % i, [128, 128], F32) for i in range(4)]
    self.PT = [al("PT%d" % i, [128, 128], BF16) for i in range(4)]
    self.qw = [al("qw%d" % i, [128, 128], BF16) for i in range(4)]
    self.aden = [al("aden%d" % i, [128, 128], F32) for i in range(4)]
    self.Kw = [al("Kw%d" % i, [128, 128], BF16) for i in range(4)]
    self.attm = [al("attm%d" % i, [128, 64], BF16) for i in range(4)]
    self.ktokb = [al("ktokb%d" % i, [128, 128], BF16) for i in range(4)]
    self.tmpS = [al("tmpS%d" % i, [128, 128], F32) for i in range(4)]
    self.dcol = [al("dcol%d" % i, [128, 2], F32) for i in range(4)]
    self.sqh = al("sqh", [128, 512], BF16)
    self.rstdh = al("rstdh", [128, 512], F32)
    self.arena0_end = off[0]
    self.og = k.sb("og", [128, 4, 512], F32, at=self.h.addr)
    self.sig = k.sb("sig", [128, 4, 512], F32, at=self.h.addr)
    self.qh = k.sb("qh", [128, 4, 512], F32, at=self.h.addr + 8192)
    self.mixT = k.sb("mixT", [128, 8, 512], BF16, at=self.h.addr + 16384)
    self.gate_b = k.sb("gate_b", [128, 4, 512], F32, at=self.qT.addr)
    self.IVtok = k.sb("IVtok", [128, 8, 512], BF16, at=self.Ktok.addr)
    assert self.kT.addr == self.qT.addr + 4096 and self.Vtok.addr == self.Ktok.addr + 4096
    self.CN = k.sb("CN", [128, 4, 256], F32)
    self.CNb = k.sb("CNb", [128, 4, 256], BF16)
    self.S = k.sb("S", [128, 4, 128], F32)
    self.Sb = k.sb("Sb", [128, 4, 128], BF16)
    self.mcol = k.sb("mcol", [128, 2], F32)
    self.ncol = k.sb("ncol", [128, 4], F32)
    self.bA = k.sb("bA", [128, 16], F32)
    self.bB = k.sb("bB", [128, 16], F32)
    self.bqs = k.sb("bqs", [128, 4], F32)
    self.bg4 = k.sb("bg4", [128, 4], F32)
    self.brow_f = k.sb("brow_f", [1, 1536], F32, at=self.hT.addr)
    self.brow_b = k.sb("brow_b", [1, 1536], BF16)
    self.gm = k.sb("gm", [128, 4], F32)
    self.gh = k.sb("gh", [128, 4], F32)
    self.lg = k.sb("lg", [128, 3, 4], F32)
    self.lb = k.sb("lb", [128, 4], F32)
    self.oml = k.sb("oml", [128, 4], F32)
    self.noml = k.sb("noml", [128, 4], F32)
    self.lsum = k.sb("lsum", [128, 4], F32)
    bi = self.ab_b_in
    k.dma(self.bA[:], bi[0:2048].rearrange("(c p) -> p c", p=128), [], [self.bA], allow_slow_non_contiguous=True)
    k.dma(self.bB[:], bi[2056:4104].rearrange("(c p) -> p c", p=128), [], [self.bB], allow_slow_non_contiguous=True)
    k.dma(self.bg4[0:4, 0:1], bi[2048:2052].rearrange("(p o) -> p o", o=1), [], [self.bg4])
    k.dma(self.bg4[0:4, 1:2], bi[2052:2056].rearrange("(p o) -> p o", o=1), [], [self.bg4])
    k.dma(self.brow_f[0:1, 0:1024], bi[512:1536].rearrange("(o c) -> o c", o=1), [], [self.brow_f])
    k.dma(self.brow_f[0:1, 1024:1536], bi[3080:3592].rearrange("(o c) -> o c", o=1), [], [self.brow_f])
    k.dma(self.gm[:], self.mlstm_norm_g[:].rearrange("(c p) -> p c", p=128), [], [self.gm], allow_slow_non_contiguous=True)
    k.dma(self.gh[:], self.hgrn_norm_g[:].rearrange("(c p) -> p c", p=128), [], [self.gh], allow_slow_non_contiguous=True)
    k.dma(self.lg[:], self.hgrn_lb_logits[:].rearrange("s (h p) -> p s h", p=128), [], [self.lg], allow_slow_non_contiguous=True)
    k.op('dve', lambda e: e.tensor_copy(self.brow_b[:], self.brow_f[:]), [self.brow_f], [self.brow_b])
    k.op('dve', lambda e: e.tensor_scalar(self.bqs[:], self.bA[:, 0:4], DKS, None, ALU.mult), [self.bA], [self.bqs])
    k.op('dve', lambda e: e.tensor_scalar(self.bg4[0:4, 1:2], self.bg4[0:4, 1:2], -1.0, None, ALU.mult), [self.bg4], [self.bg4])
    k.op('act', lambda e: e.activation(self.lg[:], self.lg[:], AF.Exp), [self.lg], [self.lg])
    k.op('dve', lambda e: e.tensor_tensor(self.lsum[:], self.lg[:, 0, :], self.lg[:, 1, :], ALU.add), [self.lg], [self.lsum])
    k.op('dve', lambda e: e.tensor_tensor(self.lsum[:], self.lsum[:], self.lg[:, 2, :], ALU.add), [self.lg, self.lsum], [self.lsum])
    k.op('dve', lambda e: e.reciprocal(self.lsum[:], self.lsum[:]), [self.lsum], [self.lsum])
    k.op('dve', lambda e: e.tensor_tensor(self.lb[:], self.lg[:, 0, :], self.lsum[:], ALU.mult), [self.lg, self.lsum], [self.lb])
    k.op('dve', lambda e: e.tensor_scalar(self.oml[:], self.lb[:], -1.0, 1.0, ALU.mult, ALU.add), [self.lb], [self.oml])
    k.op('dve', lambda e: e.tensor_scalar(self.noml[:], self.oml[:], -1.0, None, ALU.mult), [self.oml], [self.noml])
    k.op('pool', lambda e: e.memset(self.rows[:], 0.0), [], [self.rows])
    k.op('pool', lambda e: e.memset(self.Rr[:], 0.0), [], [self.Rr])
    self.onec = k.sb("onec", [128, 2], F32)
    k.op('pool', lambda e: e.memset(self.onec[:], 1.0), [], [self.onec])


def l0_zero_state(self):
    k = self.k
    k.op('pool', lambda e: e.memset(self.CN[:], 0.0), [], [self.CN])
    k.op('pool', lambda e: e.memset(self.CNb[:], 0.0), [], [self.CNb])
    k.op('pool', lambda e: e.memset(self.S[:], 0.0), [], [self.S])
    k.op('pool', lambda e: e.memset(self.Sb[:], 0.0), [], [self.Sb])
    k.op('pool', lambda e: e.memset(self.mcol[:], 0.0), [], [self.mcol])


def l0_load_m(self, sq_):
    k = self.k
    for h in range(4):
        k.dma(self.CN[:, h, 0:128], self.st_mC[sq_, h], [], [self.CN])
    k.dma(self.ncol[:], self.st_mn[sq_].rearrange("h d -> d h"), [], [self.ncol], allow_slow_non_contiguous=True)
    k.dma(self.mcol[0:4, 0:1], self.st_mm[sq_].rearrange("(h o) -> h o", o=1), [], [self.mcol])
    ones = self.cst[:, self.C_ONES:self.C_ONES + 128]
    for h in range(4):
        k.op('dve', lambda e, h=h: e.tensor_scalar(self.CN[:, h, 128:256], ones, self.ncol[:, h:h + 1], None, ALU.mult),
             [self.cst, self.ncol], [self.CN])
    k.op('act', lambda e: e.activation(self.CNb[:], self.CN[:], AF.Copy), [self.CN], [self.CNb])


def l0_load_h(self, sq_):
    k = self.k
    for h in range(4):
        k.dma(self.S[:, h, :], self.st_hS[sq_, h], [], [self.S])
    k.op('act', lambda e: e.activation(self.Sb[:], self.S[:], AF.Copy), [self.S], [self.Sb])


def l0_store_m(self, dC, dn, dm):
    k = self.k
    for h in range(4):
        k.dma(dC[h], self.CN[:, h, 0:128], [self.CN], [self.yp.d(1)], is_out=True, acc=True)
        k.dma(dn[h].rearrange("(d o) -> d o", o=1), self.CN[:, h, 128:129], [self.CN], [self.yp.d(1)], is_out=True, acc=True)
    k.dma(dm.rearrange("(h o) -> h o", o=1), self.mcol[0:4, 0:1], [self.mcol], [self.yp.d(1)], is_out=True, acc=True)


def l0_store_h(self, dS):
    k = self.k
    for h in range(4):
        k.dma(dS[h], self.S[:, h, :], [self.S], [self.yp.d(1)], is_out=True, acc=True)


def l0_mixer(self, kind, t0, T, segs, last):
    k = self.k
    x, xn = self.x, self.xn
    k.op('act', lambda e: e.activation(self.rows[:], self.x[:, 0:4, :], AF.Copy, scale=0.0), [self.x], [self.rows])
    k.op('act', lambda e: e.activation(self.Rr[:], self.x[:, 4:7, :], AF.Copy, scale=0.0), [self.x], [self.Rr])
    self.rmsnorm(T, 1)
    mch = []
    hch = []
    for si, (c0, ln, _) in enumerate(segs):
        for a in range(0, ln, 128):
            mch.append((c0 + a, min(128, ln - a), si))
        for a in range(0, ln, 64):
            hch.append((c0 + a, min(64, ln - a), si))
    ones1 = self.cstb[0:1, self.CB_ONES:self.CB_ONES + 128]

    PIDX = {0: 0, 1024: 1, 2048: 2, 2568: 3, 3592: 4}

    def piece(c0, c1):
        w = c1 - c0
        slot = self.wload(("ab", PIDX[c0]))
        return slot, slot[:, 0:NJW * w].rearrange("p (j c) -> p j c", j=NJW)
    NJW = 8

    def fm(slot, wv, col, M=128):
        ps = self.bank()
        self.mm(ps, ps[:M, :T], [(wv[:, j, col:col + M], xn[:, j, :T]) for j in range(8)], [slot, xn])
        return ps

    slot, wv = piece(0, 1024)
    for h in range(4):
        ps = fm(slot, wv, h * 128)
        k.op('act', lambda e, ps=ps, h=h: e.activation(self.qT[:, h, :T], ps[:, :T], AF.Identity, bias=self.bqs[:, h:h + 1], scale=DKS),
             [ps, self.bqs], [self.qT.ch(h)])
        ps = fm(slot, wv, 512 + h * 128)
        k.op('dve', lambda e, ps=ps, h=h: e.tensor_scalar(self.kT[:, h, :T], ps[:, :T], self.bA[:, 4 + h:5 + h], None, ALU.add),
             [ps, self.bA], [self.kT.ch(h)])

    def tokproj(slot, wv, wcol, brcol, chunks, dst, eng):
        for ci, (c0, L, si) in enumerate(chunks):
            ps = self.bank()
            pairs = [(xn[:, j, c0:c0 + L], wv[:, j, wcol:wcol + 512]) for j in range(8)]
            pairs.append((ones1[:, 0:L], self.brow_b[0:1, brcol:brcol + 512]))
            self.mm(ps, ps[:L, :], pairs, [slot, xn, self.brow_b, self.cstb])
            if eng == 'act':
                k.op('act', lambda e, ps=ps, ci=ci, L=L: e.activation(dst[:L, ci, :], ps[:L, :], AF.Copy), [ps], [dst.ch(ci)])
            else:
                k.op('dve', lambda e, ps=ps, ci=ci, L=L: e.tensor_copy(dst[:L, ci, :], ps[:L, :]), [ps], [dst.ch(ci)])
    tokproj(slot, wv, 512, 0, mch, self.Ktok, 'act')
    slot, wv = piece(1024, 2048)
    tokproj(slot, wv, 0, 512, mch, self.Vtok, 'dve')
    for h in range(4):
        ps = fm(slot, wv, 512 + h * 128)
        k.op('act', lambda e, ps=ps, h=h: e.activation(self.og[:, h, :T], ps[:, :T], AF.Sigmoid, bias=self.bA[:, 12 + h:13 + h]),
             [ps, self.bA], [self.og.ch(h)])
    slot, wv = piece(2048, 2568)
    rows, Rr = self.rows, self.Rr
    ps_i = fm(slot, wv, 0, M=4)
    k.op('act', lambda e: e.activation(rows[0:4, 0, :T], ps_i[0:4, :T], AF.Identity, bias=self.bg4[0:4, 0:1]),
         [ps_i, self.bg4], [rows.ch(0)])
    ps_f = fm(slot, wv, 4, M=4)
    k.op('act', lambda e: e.activation(rows[0:4, 1, :T], ps_f[0:4, :T], AF.Exp, bias=self.bg4[0:4, 1:2], scale=-1.0),
         [ps_f, self.bg4], [rows.ch(1)])
    k.op('act', lambda e: e.activation(rows[0:4, 1, :T], rows[0:4, 1, :T], AF.Ln, bias=self.onec[0:4, 0:1]), [rows.ch(1), self.onec], [rows.ch(1)])
    for h in range(4):
        ps = fm(slot, wv, 8 + h * 128)
        k.op('act', lambda e, ps=ps, h=h: e.activation(self.qh[:, h, :T], ps[:, :T], AF.Silu, bias=self.bB[:, h:h + 1]),
             [ps, self.bB], [self.qh.ch(h)])

    def hgrn_proj():
        slot, wv = piece(2568, 3592)
        for h in range(4):
            ps = fm(slot, wv, h * 128)
            k.op('act', lambda e, ps=ps, h=h: e.activation(self.sig[:, h, :T], ps[:, :T], AF.Sigmoid, bias=self.bB[:, 4 + h:5 + h]),
                 [ps, self.bB], [self.sig.ch(h)])
        tokproj(slot, wv, 512, 1024, hch, self.IVtok, 'dve')
        slot, wv = piece(3592, 4104)
        for h in range(4):
            ps = fm(slot, wv, h * 128)
            k.op('act', lambda e, ps=ps, h=h: e.activation(self.gate_b[:, h, :T], ps[:, :T], AF.Silu, bias=self.bB[:, 12 + h:13 + h]),
                 [ps, self.bB], [self.gate_b.ch(h)])

    ones4 = self.cst[0:4, self.C_ONES:self.C_ONES + 512]
    for si, (s0, ln, sq_) in enumerate(segs):
        if sq_ is not None:
            l0_load_m(self, sq_)
        sl = slice(s0, s0 + ln)
        k.op('dve', lambda e, sl=sl, ln=ln: e.tensor_tensor_scan(rows[0:4, 2, sl], ones4[:, 0:ln], rows[0:4, 1, sl], 0.0, ALU.mult, ALU.add),
             [rows.ch(1), self.cst], [rows.ch(2)])
        k.op('dve', lambda e, sl=sl: e.tensor_tensor(rows[0:4, 0, sl], rows[0:4, 0, sl], rows[0:4, 2, sl], ALU.add),
             [rows.ch(0), rows.ch(2)], [rows.ch(0)])
        k.op('dve', lambda e, sl=sl, ln=ln: e.tensor_tensor_scan(rows[0:4, 3, sl], ones4[:, 0:ln], rows[0:4, 0, sl], self.mcol[0:4, 0:1], ALU.mult, ALU.max),
             [rows.ch(0), self.cst, self.mcol], [rows.ch(3)])
        k.op('dve', lambda e, sl=sl: e.tensor_scalar(Rr[0:4, 0, sl], rows[0:4, 3, sl], -1.0, None, ALU.mult), [rows.ch(3)], [Rr.ch(0)])
        k.op('dve', lambda e, sl=sl: e.tensor_tensor(rows[0:4, 1, sl], rows[0:4, 2, sl], rows[0:4, 3, sl], ALU.subtract),
             [rows.ch(2), rows.ch(3)], [rows.ch(1)])
        k.op('act', lambda e, sl=sl: e.activation(Rr[0:4, 2, sl], rows[0:4, 1, sl], AF.Exp), [rows.ch(1)], [Rr.ch(2)])
        my_m = [(ci, c) for ci, c in enumerate(mch) if c[2] == si]
        for (ci, (c0, L, _)) in my_m:
            if c0 == s0:
                k.op('act', lambda e, c0=c0, L=L: e.activation(Rr[0:4, 1, c0:c0 + L], Rr[0:4, 0, c0:c0 + L], AF.Exp, bias=self.mcol[0:4, 0:1]),
                     [Rr.ch(0), self.mcol], [Rr.ch(1)])
            else:
                k.op('act', lambda e, c0=c0, L=L: e.activation(Rr[0:4, 1, c0:c0 + L], Rr[0:4, 0, c0:c0 + L], AF.Exp, bias=rows[0:4, 3, c0 - 1:c0]),
                     [Rr.ch(0), rows.ch(3)], [Rr.ch(1)])
        k.op('dve', lambda e, s0=s0, ln=ln: e.tensor_tensor(self.mcol[0:4, 0:1], rows[0:4, 3, s0 + ln - 1:s0 + ln], rows[0:4, 2, s0 + ln - 1:s0 + ln], ALU.subtract),
             [rows.ch(3), rows.ch(2), Rr.ch(1)], [self.mcol])
        for (ci, (c0, L, _)) in my_m:
            cs = slice(c0, c0 + L)
            pb = self.bank()
            k.op('pe', lambda e, pb=pb, cs=cs, L=L: e.matmul(pb[:L, 0:4], rows[:, 0, cs], self.cst[:, self.C_ID:self.C_ID + 4], start=True, stop=True),
                 [rows.ch(0), self.cst], [pb])
            k.op('dve', lambda e, pb=pb, L=L, ci=ci: e.tensor_copy(self.bcol[:L, ci % 4, :], pb[:L, 0:4]), [pb], [self.bcol])
            allb = list(self.banks)
            thr = []
            for h in range(4):
                k.begin_thread()
                par = h
                Em, Dm, PT, qw, aden, Kw, dcol = self.Em[par], self.Dm[par], self.PT[par], self.qw[par], self.aden[par], self.Kw[par], self.dcol[par]
                pS, pB, pC, pD = BV(allb[2 * h], 0), BV(allb[2 * h], 128), BV(allb[2 * h + 1], 0), BV(allb[2 * h + 1], 256)
                k.op('pe', lambda e, pS=pS, h=h, cs=cs, L=L: e.matmul(pS[:L, :L], self.kT[:, h, cs], self.qT[:, h, cs], start=True, stop=True),
                     [self.kT.ch(h), self.qT.ch(h)], [pS])
                k.op('pe', lambda e, pB=pB, h=h, cs=cs, L=L: e.matmul(pB[:, 0:3 * L].rearrange("p (r l) -> p r l", r=3),
                                                                      self.cst[:, self.C_SEL + h * 128:self.C_SEL + (h + 1) * 128],
                                                                      Rr[:, :, cs], start=True, stop=True),
                     [Rr, self.cst], [pB])
                k.op('dve', lambda e, pB=pB, Em=Em, h=h, L=L, ci=ci: e.scalar_tensor_tensor(Em[:L, :L], pB[:L, 0:L], self.bcol[:L, ci % 4, h:h + 1],
                                                                                             self.cst[:L, self.C_MNEG:self.C_MNEG + L], ALU.add, ALU.add),
                     [pB, self.bcol, self.cst], [Em])
                k.op('act', lambda e, Em=Em, Dm=Dm, L=L: e.activation(Dm[:L, :L], Em[:L, :L], AF.Exp), [Em], [Dm])
                k.op('dve', lambda e, pS=pS, Dm=Dm, PT=PT, L=L: e.tensor_tensor(PT[:L, :L], pS[:L, :L], Dm[:L, :L], ALU.mult), [pS, Dm], [PT])
                k.op('dve', lambda e, pB=pB, qw=qw, h=h, cs=cs, L=L: e.tensor_tensor(qw[:, :L], self.qT[:, h, cs], pB[:, L:2 * L], ALU.mult),
                     [pB, self.qT.ch(h)], [qw])
                k.op('dve', lambda e, pB=pB, dcol=dcol, L=L: e.tensor_copy(dcol[:, 0:1], pB[:, 2 * L - 1:2 * L]), [pB], [dcol])
                self.mm(pC, pC[:, 0:L], [(self.Vtok[:L, ci, h * 128:(h + 1) * 128], PT[:L, :L]), (self.CNb[:, h, 0:128], qw[:, :L])],
                        [self.Vtok.ch(ci), PT, self.CNb, qw])
                self.mm(pC, pC[:, 128:128 + L], [(self.cstb[:L, self.CB_ONES:self.CB_ONES + 128], PT[:L, :L]), (self.CNb[:, h, 128:256], qw[:, :L])],
                        [PT, self.CNb, qw, self.cstb])
                k.op('act', lambda e, pC=pC, aden=aden, L=L: e.activation(aden[:, :L], pC[:, 128:128 + L], AF.Abs), [pC], [aden])
                k.op('dve', lambda e, pB=pB, aden=aden, L=L: e.tensor_tensor(aden[:, :L], aden[:, :L], pB[:, 2 * L:3 * L], ALU.max), [aden, pB], [aden])
                k.op('dve', lambda e, aden=aden, L=L: e.reciprocal(aden[:, :L], aden[:, :L]), [aden], [aden])
                k.op('dve', lambda e, pC=pC, aden=aden, h=h, cs=cs, L=L: e.tensor_tensor(self.hT[:, h, cs], pC[:, 0:L], aden[:, :L], ALU.mult),
                     [pC, aden], [self.hT.ch(h)])
                k.op('dve', lambda e, Kw=Kw, Dm=Dm, h=h, L=L, ci=ci: e.tensor_scalar(Kw[:L, :], self.Ktok[:L, ci, h * 128:(h + 1) * 128], Dm[:L, L - 1:L], None, ALU.mult),
                     [self.Ktok.ch(ci), Dm], [Kw])
                k.op('pe', lambda e, pD=pD, Kw=Kw, h=h, L=L, ci=ci: e.matmul(pD[:, 0:128], Kw[:L, :], self.Vtok[:L, ci, h * 128:(h + 1) * 128], start=True, stop=True),
                     [Kw, self.Vtok.ch(ci)], [pD])
                k.op('pe', lambda e, pD=pD, Kw=Kw, L=L: e.matmul(pD[:, 128:256], Kw[:L, :], self.cstb[:L, self.CB_ONES:self.CB_ONES + 128], start=True, stop=True),
                     [Kw, self.cstb], [pD])
                k.op('dve', lambda e, pD=pD, dcol=dcol, h=h: e.scalar_tensor_tensor(self.CN[:, h, :], self.CN[:, h, :], dcol[:, 0:1], pD[:, 0:256], ALU.mult, ALU.add),
                     [pD, dcol, self.CN.ch(h)], [self.CN.ch(h)])
                k.op('act', lambda e, h=h: e.activation(self.CNb[:, h, :], self.CN[:, h, :], AF.Copy), [self.CN.ch(h)], [self.CNb.ch(h)])
                thr.append(k.end_thread())
            k.merge(*thr)
        if sq_ is not None:
            l0_store_m(self, self.s_mC[sq_], self.s_mn[sq_], self.s_mm[sq_])
        elif last:
            l0_store_m(self, self.p_mC[:], self.p_mn[:], self.p_mm[:])
    head_epilogue(self, T, self.gm, self.og, 0)
    hgrn_proj()
    for si, (s0, ln, sq_) in enumerate(segs):
        sl = slice(s0, s0 + ln)
        if sq_ is not None:
            l0_load_h(self, sq_)
        my_h = [(ci, c) for ci, c in enumerate(hch) if c[2] == si]
        tG, tK, eG = self.tG, self.tK, self.eG
        rst = self.cst[:, self.C_RST:self.C_RST + 512]
        for h in range(4):
            k.op('dve', lambda e, h=h, sl=sl: e.tensor_scalar(tG[:, sl], self.sig[:, h, sl], self.oml[:, h:h + 1], self.lb[:, h:h + 1], ALU.mult, ALU.add),
                 [self.sig.ch(h), self.oml, self.lb], [tG])
            k.op('act', lambda e, sl=sl: e.activation(tG[:, sl], tG[:, sl], AF.Ln), [tG], [tG])
            k.op('dve', lambda e, h=h, sl=sl: e.tensor_scalar(tK[:, sl], self.sig[:, h, sl], self.noml[:, h:h + 1], self.oml[:, h:h + 1], ALU.mult, ALU.add),
                 [self.sig.ch(h), self.oml, self.noml], [tK])
            k.op('dve', lambda e, sl=sl, ln=ln: e.tensor_tensor_scan(eG[:, sl], rst[:, 0:ln], tG[:, sl], 0.0, ALU.mult, ALU.add), [tG, self.cst], [eG])
            k.op('act', lambda e, sl=sl: e.activation(tG[:, sl], eG[:, sl], AF.Exp, scale=-1.0), [eG], [tG])
            k.op('act', lambda e, sl=sl: e.activation(eG[:, sl], eG[:, sl], AF.Exp), [eG], [eG])
            k.op('dve', lambda e, h=h, sl=sl: e.tensor_tensor(self.kinv[:, h, sl], tK[:, sl], tG[:, sl], ALU.mult), [tK, tG], [self.kinv.ch(h)])
            k.op('dve', lambda e, h=h, sl=sl: e.tensor_tensor(self.qdec[:, h, sl], self.qh[:, h, sl], eG[:, sl], ALU.mult), [self.qh.ch(h), eG], [self.qdec.ch(h)])
            for (ci, (c0, L, _)) in my_h:
                k.op('dve', lambda e, h=h, ci=ci, c0=c0, L=L: e.tensor_copy(self.eGend[:, h, ci:ci + 1], eG[:, c0 + L - 1:c0 + L]), [eG], [self.eGend])
        for (ci, (c0, L, _)) in my_h:
            cs = slice(c0, c0 + L)
            allb = list(self.banks)
            thr = []
            for h in range(4):
                k.begin_thread()
                par = h
                attm, ktokb, tmpS = self.attm[par], self.ktokb[par], self.tmpS[par]
                pA, pB, pC, pD = BV(allb[2 * h], 0), BV(allb[2 * h], 128), BV(allb[2 * h + 1], 0), BV(allb[2 * h + 1], 128)
                k.op('pe', lambda e, pA=pA, h=h, cs=cs, L=L: e.matmul(pA[:L, :L], self.kinv[:, h, cs], self.qdec[:, h, cs], start=True, stop=True),
                     [self.kinv.ch(h), self.qdec.ch(h)], [pA])
                k.op('pe', lambda e, pB=pB, h=h, cs=cs, L=L: e.matmul(pB[:L, 0:128], self.kinv[:, h, cs], self.cstb[:, self.CB_ID:self.CB_ID + 128], start=True, stop=True),
                     [self.kinv.ch(h), self.cstb], [pB])
                k.op('dve', lambda e, pA=pA, attm=attm, L=L: e.tensor_tensor(attm[:L, :L], pA[:L, :L], self.cst[:L, self.C_TRI:self.C_TRI + L], ALU.mult),
                     [pA, self.cst], [attm])
                k.op('act', lambda e, pB=pB, ktokb=ktokb, L=L: e.activation(ktokb[:L, :], pB[:L, 0:128], AF.Copy), [pB], [ktokb])
                self.mm(pC, pC[:, 0:L], [(self.IVtok[:L, ci, h * 128:(h + 1) * 128], attm[:L, :L]), (self.Sb[:, h, :], self.qdec[:, h, cs])],
                        [self.IVtok.ch(ci), attm, self.Sb.ch(h), self.qdec.ch(h)])
                k.op('dve', lambda e, pC=pC, h=h, cs=cs, L=L: e.tensor_copy(self.hT[:, h, cs], pC[:, 0:L]), [pC], [self.hT.ch(h)])
                k.op('pe', lambda e, pD=pD, ktokb=ktokb, h=h, L=L, ci=ci: e.matmul(pD[:, 0:128], ktokb[:L, :], self.IVtok[:L, ci, h * 128:(h + 1) * 128], start=True, stop=True),
                     [ktokb, self.IVtok.ch(ci)], [pD])
                k.op('dve', lambda e, pD=pD, tmpS=tmpS, h=h: e.tensor_tensor(tmpS[:, :], self.S[:, h, :], pD[:, 0:128], ALU.add), [pD, self.S.ch(h)], [tmpS])
                k.op('act', lambda e, tmpS=tmpS, h=h, ci=ci: e.activation(self.S[:, h, :], tmpS[:, :], AF.Identity, scale=self.eGend[:, h, ci:ci + 1]),
                     [tmpS, self.eGend], [self.S.ch(h)])
                k.op('act', lambda e, tmpS=tmpS, h=h, ci=ci: e.activation(self.Sb[:, h, :], tmpS[:, :], AF.Identity, scale=self.eGend[:, h, ci:ci + 1]),
                     [tmpS, self.eGend], [self.Sb.ch(h)])
                thr.append(k.end_thread())
            for t_ in thr:
                k.merge(t_)
        if sq_ is not None:
            l0_store_h(self, self.s_hS[sq_])
        elif last:
            l0_store_h(self, self.p_hS[:])
    head_epilogue(self, T, self.gh, self.gate_b, 4)
    out_proj(self, T, ("abo",))


def head_epilogue(self, T, gcols, gate, mix_off):
    k = self.k
    for h in range(4):
        k.op('act', lambda e, h=h: e.activation(self.sqh[:, :T], self.hT[:, h, :T], AF.Square), [self.hT.ch(h)], [self.sqh])
        ps = self.bank()
        self.mm(ps, ps[:, :T], [(self.cstb[:, self.CB_ONES:self.CB_ONES + 128], self.sqh[:, :T])], [self.sqh, self.cstb])
        self.rsqrt(self.rstdh, ps, T, 1.0 / 128)
        k.op('dve', lambda e, h=h: e.scalar_tensor_tensor(self.hT[:, h, :T], self.hT[:, h, :T], gcols[:, h:h + 1], self.rstdh[:, :T], ALU.mult, ALU.mult),
             [self.hT.ch(h), self.rstdh], [self.hT.ch(h)])
        k.op('dve', lambda e, h=h: e.tensor_tensor(self.mixT[:, mix_off + h, :T], self.hT[:, h, :T], gate[:, h, :T], ALU.mult),
             [self.hT.ch(h), gate.ch(h)], [self.mixT.ch(mix_off + h)])


def out_proj(self, T, Wb):
    k = self.k
    slot = self.wload(Wb)
    wv = slot[:, 0:8 * 1024].rearrange("p (j c) -> p j c", j=8)
    for d in range(8):
        po = self.bank()
        self.mm(po, po[:, :T], [(wv[:, j, d * 128:(d + 1) * 128], self.mixT[:, j, :T]) for j in range(8)], [slot, self.mixT])
        k.op('dve', lambda e, po=po, d=d: e.tensor_tensor(self.x[:, d, :T], self.x[:, d, :T], po[:, :T], ALU.add), [po, self.x.ch(d)], [self.x.ch(d)])

import math


def l1_setup(self):
    k = self.k
    P = lambda name, shape, dt: k.sb(name, shape, dt)
    self.bC = P("bC", [128, 14], F32)
    self.convw = P("convw", [128, 4, 6], F32)
    self.convb = P("convb", [128, 6], F32)
    self.rgA = P("rgA", [128, 6, 128], BF16)
    self.rgX = P("rgX", [128, 6, 128], BF16)
    self.rgba = P("rgba", [128, 6], F32)
    self.rgbx = P("rgbx", [128, 6], F32)
    self.ccol = P("ccol", [128, 6], F32)
    self.c2col = P("c2col", [128, 6], F32)
    self.Are = P("Are", [128, 8], F32)
    self.Aim = P("Aim", [128, 8], F32)
    self.dtc = P("dtc", [128, 8], F32)
    self.rho = P("rho", [128, 8], F32)
    self.Ec = P("Ec", [128, 8, 128], F32)
    self.Es = P("Es", [128, 8, 128], F32)
    self.BBre = P("BBre", [128, 8, 128], BF16)
    self.BBim = P("BBim", [128, 8, 128], BF16)
    self.CTre = P("CTre", [128, 8, 128], BF16)
    self.CTimn = P("CTimn", [128, 8, 128], BF16)
    self.dS5 = P("dS5", [128, 2], F32)
    self.wglu = P("wglu", [128, 2, 256], BF16)
    self.bglu = P("bglu", [128, 2], F32)
    self.hst = P("hst", [128, 6], F32)
    self.crprev = P("crprev", [128, 6, 3], F32)
    self.sre = P("sre", [128, 8], F32)
    self.sim = P("sim", [128, 8], F32)
    self.hpi = P("hpi", [128, 2], F32)
    self.cc4 = P("cc4", [128, 8, 2], F32)
    self.sreS = [P("sreS%d" % i, [128, 8], F32) for i in range(2)]
    self.simS = [P("simS%d" % i, [128, 8], F32) for i in range(2)]
    self.tz = [P("tz%d" % i, [128, 8], F32) for i in range(8)]
    A = self.arena0
    stA = k.sb("stA", [128, 8, 128], F32, at=A)
    stB = k.sb("stB", [128, 8, 128], F32, at=A + 4096)
    stC = k.sb("stC", [128, 8, 128], F32, at=A + 8192)
    stD = k.sb("stD", [128, 8, 128], F32, at=A + 12288)
    stW = k.sb("stW", [128, 2, 256], F32, at=A + 16384)
    nz = lambda ap: ap
    k.dma(self.bC[:], self.cd_b_in[:].rearrange("(c p) -> p c", p=128), [], [self.bC], allow_slow_non_contiguous=True)
    k.dma(self.convw[:], self.conv_w[:].rearrange("j (c p) -> p j c", p=128), [], [self.convw], allow_slow_non_contiguous=True)
    k.dma(self.convb[:], self.conv_b[:].rearrange("(c p) -> p c", p=128), [], [self.convb], allow_slow_non_contiguous=True)
    k.dma(self.rgba[:], self.rg_b_a[:].rearrange("(c p) -> p c", p=128), [], [self.rgba], allow_slow_non_contiguous=True)
    k.dma(self.rgbx[:], self.rg_b_x[:].rearrange("(c p) -> p c", p=128), [], [self.rgbx], allow_slow_non_contiguous=True)
    k.dma(self.ccol[:], self.rg_lambda[:].rearrange("(c p) -> p c", p=128), [], [self.ccol], allow_slow_non_contiguous=True)
    k.dma(self.dS5[:], self.s5_D[:].rearrange("(c p) -> p c", p=128), [], [self.dS5], allow_slow_non_contiguous=True)
    k.dma(self.bglu[:], self.s5_b_glu[:].rearrange("(c p) -> p c", p=128), [], [self.bglu], allow_slow_non_contiguous=True)
    k.dma(self.Are[:], self.s5_A_re[:].rearrange("(k g) p -> (g p) k", g=2), [], [self.Are], allow_slow_non_contiguous=True)
    k.dma(self.Aim[:], self.s5_A_im[:].rearrange("(k g) p -> (g p) k", g=2), [], [self.Aim], allow_slow_non_contiguous=True)
    ld = self.s5_log_dt[:]
    for g2 in range(2):
        src = bass.AP(ld.tensor, g2, [[0, 64], [2, 8]])
        k.dma(self.dtc[g2 * 64:(g2 + 1) * 64, :], src, [], [self.dtc], allow_slow_non_contiguous=True)
    k.op('pool', lambda e: e.memset(self.hpi[:], math.pi / 2), [], [self.hpi])
    k.op('act', lambda e: e.activation(self.ccol[:], self.ccol[:], AF.Exp, scale=-1.0), [self.ccol], [self.ccol])
    k.op('act', lambda e: e.activation(self.ccol[:], self.ccol[:], AF.Ln, bias=self.onec[:, 0:1]), [self.ccol, self.onec], [self.ccol])
    k.op('dve', lambda e: e.tensor_scalar(self.c2col[:], self.ccol[:], -16.0, None, ALU.mult), [self.ccol], [self.c2col])
    k.op('dve', lambda e: e.tensor_scalar(self.ccol[:], self.ccol[:], -8.0, None, ALU.mult), [self.ccol], [self.ccol])
    for (src, dst) in ((self.rg_w_a, self.rgA), (self.rg_w_x, self.rgX)):
        k.op('pool', lambda e: e.memset(stA[:, 0:6, :], 0.0), [], [stA])
        for n in range(12):
            c, hf = n // 2, n % 2
            k.dma(stA[hf * 64:(hf + 1) * 64, c, hf * 64:(hf + 1) * 64], src[n], [], [stA])
        k.op('dve', lambda e, dst=dst: e.tensor_copy(dst[:], stA[:, 0:6, :]), [stA], [dst])
    k.dma(stW[:], self.s5_w_glu[:].rearrange("(j p) c -> p j c", p=128), [], [stW])
    k.op('dve', lambda e: e.tensor_copy(self.wglu[:], stW[:]), [stW], [self.wglu])
    t = self.tz
    TT = lambda out, a, b, op: k.op('dve', lambda e: e.tensor_tensor(out[:], a[:], b[:], op), [a, b], [out])
    TS = lambda out, a, s1, s2, op0, op1=None: k.op('dve', (lambda e: e.tensor_scalar(out[:], a[:], s1, s2, op0, op1)) if op1 is not None else
                                                    (lambda e: e.tensor_scalar(out[:], a[:], s1, None, op0)), [a], [out])
    dt = self.dtc
    k.op('act', lambda e: e.activation(dt[:], dt[:], AF.Exp, scale=0.125), [dt], [dt])
    for _ in range(3):
        TT(dt, dt, dt, ALU.mult)
    TT(t[0], dt, self.Are, ALU.mult)
    TT(t[1], dt, self.Aim, ALU.mult)
    k.op('act', lambda e: e.activation(self.rho[:], t[0][:], AF.Exp), [t[0]], [self.rho])
    k.op('act', lambda e: e.activation(t[2][:], t[1][:], AF.Sin, bias=self.hpi[:, 0:1], scale=1.0 / 16), [t[1], self.hpi], [t[2]])
    k.op('act', lambda e: e.activation(t[3][:], t[1][:], AF.Sin, scale=1.0 / 16), [t[1]], [t[3]])
    for _ in range(4):
        TT(t[4], t[2], t[2], ALU.mult)
        TT(t[5], t[3], t[3], ALU.mult)
        TT(t[6], t[2], t[3], ALU.mult)
        TT(t[2], t[4], t[5], ALU.subtract)
        TS(t[3], t[6], 2.0, None, ALU.mult)
    cth, sth = t[2], t[3]
    TT(t[4], self.rho, cth, ALU.mult)
    TT(t[5], self.rho, sth, ALU.mult)
    TT(t[0], self.Are, self.Are, ALU.mult)
    TT(t[1], self.Aim, self.Aim, ALU.mult)
    TT(t[0], t[0], t[1], ALU.add)
    k.op('dve', lambda e: e.reciprocal(t[0][:], t[0][:]), [t[0]], [t[0]])
    TS(t[6], t[4], -1.0, None, ALU.add)
    TT(t[1], t[6], self.Are, ALU.mult)
    TT(t[7], t[5], self.Aim, ALU.mult)
    TT(t[1], t[1], t[7], ALU.add)
    TT(t[1], t[1], t[0], ALU.mult)
    TT(t[7], t[5], self.Are, ALU.mult)
    TT(t[6], t[6], self.Aim, ALU.mult)
    TT(t[7], t[7], t[6], ALU.subtract)
    TT(t[7], t[7], t[0], ALU.mult)
    TS(t[6], t[7], -1.0, None, ALU.mult)
    zre, zim, nzim = t[1], t[7], t[6]
    Ec, Es = self.Ec, self.Es
    k.op('dve', lambda e: e.tensor_copy(Ec[:, :, 0], cth[:]), [cth], [Ec])
    k.op('dve', lambda e: e.tensor_copy(Es[:, :, 0], sth[:]), [sth], [Es])
    tmpT = stD
    m = 1
    while m < 128:
        k.op('dve', lambda e, m=m: e.tensor_scalar(t[0][:], Es[:, :, m - 1], -1.0, None, ALU.mult), [Es], [t[0]])
        for kk in range(8):
            cm = Ec[:, kk, m - 1:m]
            sm = Es[:, kk, m - 1:m]
            nsm = t[0][:, kk:kk + 1]
            k.op('dve', lambda e, kk=kk, m=m, cm=cm: e.tensor_scalar(tmpT[:, kk, 0:m], Ec[:, kk, 0:m], cm, None, ALU.mult), [Ec], [tmpT])
            k.op('dve', lambda e, kk=kk, m=m, sm=sm: e.tensor_scalar(tmpT[:, kk, 64:64 + m], Ec[:, kk, 0:m], sm, None, ALU.mult), [Ec], [tmpT])
            k.op('dve', lambda e, kk=kk, m=m, nsm=nsm: e.scalar_tensor_tensor(Ec[:, kk, m:2 * m], Es[:, kk, 0:m], nsm, tmpT[:, kk, 0:m], ALU.mult, ALU.add),
                 [Es, tmpT, t[0]], [Ec])
            k.op('dve', lambda e, kk=kk, m=m, cm=cm: e.scalar_tensor_tensor(Es[:, kk, m:2 * m], Es[:, kk, 0:m], cm, tmpT[:, kk, 64:64 + m], ALU.mult, ALU.add),
                 [Es, tmpT, Ec], [Es])
        m *= 2
    k.op('pool', lambda e: e.memset(stA[:], 0.0), [], [stA])
    k.op('pool', lambda e: e.memset(stB[:], 0.0), [], [stB])
    for g in range(16):
        kk, g2 = g // 2, g % 2
        off = 32 * (kk % 4) + g2 * 16
        k.dma(stA[g2 * 64:(g2 + 1) * 64, kk, off:off + 16], self.s5_B_re[g], [], [stA])
        k.dma(stB[g2 * 64:(g2 + 1) * 64, kk, off:off + 16], self.s5_B_im[g], [], [stB])
    for kk in range(8):
        k.op('dve', lambda e, kk=kk: e.tensor_scalar(stC[:, kk, :], stA[:, kk, :], zre[:, kk:kk + 1], None, ALU.mult), [stA, zre], [stC])
        k.op('dve', lambda e, kk=kk: e.scalar_tensor_tensor(stC[:, kk, :], stB[:, kk, :], nzim[:, kk:kk + 1], stC[:, kk, :], ALU.mult, ALU.add),
             [stB, nzim, stC], [stC])
        k.op('dve', lambda e, kk=kk: e.tensor_scalar(stD[:, kk, :], stB[:, kk, :], zre[:, kk:kk + 1], None, ALU.mult), [stB, zre], [stD])
        k.op('dve', lambda e, kk=kk: e.scalar_tensor_tensor(stD[:, kk, :], stA[:, kk, :], zim[:, kk:kk + 1], stD[:, kk, :], ALU.mult, ALU.add),
             [stA, zim, stD], [stD])
    ident = self.cst[:, self.C_ID:self.C_ID + 128]
    for kk in range(8):
        for (srcb, dstb) in ((stC, self.BBre), (stD, self.BBim)):
            ps = self.bank()
            k.op('pe', lambda e, ps=ps, srcb=srcb, kk=kk: e.matmul(ps[:, 0:128], srcb[:, kk, :], ident, start=True, stop=True), [srcb, self.cst], [ps])
            k.op('act', lambda e, ps=ps, dstb=dstb, kk=kk: e.activation(dstb[:, kk, :], ps[:, 0:128], AF.Copy), [ps], [dstb])
    k.op('pool', lambda e: e.memset(stA[:], 0.0), [], [stA])
    k.op('pool', lambda e: e.memset(stB[:], 0.0), [], [stB])
    for g in range(16):
        kk, g2 = g // 2, g % 2
        off = 32 * (kk % 4) + g2 * 16
        k.dma(stA[off:off + 16, kk, g2 * 64:(g2 + 1) * 64], self.s5_C_re[g], [], [stA])
        k.dma(stB[off:off + 16, kk, g2 * 64:(g2 + 1) * 64], self.s5_C_im[g], [], [stB])
    for kk in range(8):
        ps = self.bank()
        k.op('pe', lambda e, ps=ps, kk=kk: e.matmul(ps[:, 0:128], stA[:, kk, :], ident, start=True, stop=True), [stA, self.cst], [ps])
        k.op('act', lambda e, ps=ps, kk=kk: e.activation(self.CTre[:, kk, :], ps[:, 0:128], AF.Copy), [ps], [self.CTre])
        ps = self.bank()
        k.op('pe', lambda e, ps=ps, kk=kk: e.matmul(ps[:, 0:128], stB[:, kk, :], ident, start=True, stop=True), [stB, self.cst], [ps])
        k.op('act', lambda e, ps=ps, kk=kk: e.activation(self.CTimn[:, kk, :], ps[:, 0:128], AF.Identity, scale=-1.0), [ps], [self.CTimn])


def l1_alloc(self):
    k = self.k
    off = [self.arena0]

    def al(name, shape, dt):
        b = k.sb(name, shape, dt, at=off[0])
        off[0] = (off[0] + b.nbytes + 63) // 64 * 64
        return b
    self.gc = al("gc", [128, 6, 512], F32)
    self.crbuf = al("crbuf", [128, 6, 516], F32)
    self.duf = al("duf", [128, 2, 512], F32)
    self.dub = al("dub", [128, 2, 512], BF16)
    self.g1 = al("g1", [128, 512], F32)
    self.g2 = al("g2", [128, 512], F32)
    self.yf = al("yf", [128, 2, 512], F32)
    self.ygb = al("ygb", [128, 2, 512], BF16)
    self.rt = [al("rt%d" % i, [128, 128], F32) for i in range(2)]
    self.u_ = [al("u_%d" % i, [128, 512], F32) for i in range(2)]
    self.ub_ = [al("ub_%d" % i, [128, 512], BF16) for i in range(1)]
    self.r_ = [al("r_%d" % i, [128, 512], F32) for i in range(1)]
    self.ig_ = [al("ig_%d" % i, [128, 512], F32) for i in range(1)]
    self.a_ = [al("a_%d" % i, [128, 512], F32) for i in range(1)]
    self.hs_ = [al("hs_%d" % i, [128, 512], F32) for i in range(1)]
    self.vre = [al("vre%d" % i, [128, 512], F32) for i in range(1)]
    self.vim = [al("vim%d" % i, [128, 512], F32) for i in range(1)]
    self.wre = [al("wre%d" % i, [128, 512], F32) for i in range(1)]
    self.wim = [al("wim%d" % i, [128, 512], F32) for i in range(1)]
    self.t1 = [al("t1%d" % i, [128, 512], F32) for i in range(1)]
    self.t2 = [al("t2%d" % i, [128, 512], F32) for i in range(1)]
    self.sreb = [al("sreb%d" % i, [128, 512], BF16) for i in range(1)]
    self.simb = [al("simb%d" % i, [128, 512], BF16) for i in range(1)]
    end1 = off[0]
    end = max(end1, off[0])
    assert end <= self.arena0 + self.ARENA, end - self.arena0


def gelu(self, out, xb, T, rres, wres):
    k = self.k
    g1 = self.g1
    k.op('act', lambda e: e.activation(g1[:, :T], xb, AF.Square), [rres], [g1])
    k.op('dve', lambda e: e.tensor_scalar(g1[:, :T], g1[:, :T], 0.044715, 1.0, ALU.mult, ALU.add), [g1], [g1])
    k.op('dve', lambda e: e.tensor_tensor(g1[:, :T], g1[:, :T], xb, ALU.mult), [g1, rres], [g1])
    k.op('act', lambda e: e.activation(g1[:, :T], g1[:, :T], AF.Sigmoid, scale=1.5957691216057308), [g1], [g1])
    k.op('dve', lambda e: e.tensor_tensor(out, xb, g1[:, :T], ALU.mult), [g1, rres], [wres])


def l1_mixer(self, kind, t0, T, segs, last):
    k = self.k
    xn = self.xn
    self.rmsnorm(T, 4)
    mixT = self.mixT

    def piece(c0, c1):
        w = c1 - c0
        slot = self.wload(("cd", 0 if c0 == 0 else 1))
        return slot, slot[:, 0:8 * w].rearrange("p (j c) -> p j c", j=8)

    def fm(slot, wv, col):
        ps = self.bank()
        self.mm(ps, ps[:, :T], [(wv[:, j, col:col + 128], xn[:, j, :T]) for j in range(8)], [slot, xn])
        return ps
    gc, crbuf, duf, dub = self.gc, self.crbuf, self.duf, self.dub
    cbase = [(35 * si if kind == "s" else 0) for si in range(len(segs))]
    for si, (s0, ln, sq_) in enumerate(segs):
        cb = cbase[si]
        if sq_ is None:
            k.op('dve', lambda e, cb=cb: e.tensor_copy(crbuf[:, :, cb:cb + 3], self.crprev[:]), [self.crprev], [crbuf])
        else:
            for c in range(6):
                k.dma(crbuf[:, c, cb:cb + 3], self.st_rc[sq_, :, c * 128:(c + 1) * 128].rearrange("j p -> p j"), [], [crbuf], allow_slow_non_contiguous=True)

    def cr_evac(ps, c):
        for si, (s0, ln, sq_) in enumerate(segs):
            cb = cbase[si]
            k.op('act', lambda e, cb=cb, s0=s0, ln=ln: e.activation(crbuf[:, c, cb + 3:cb + 3 + ln], ps[:, s0:s0 + ln], AF.Identity, bias=self.bC[:, 6 + c:7 + c]),
                 [ps, self.bC], [crbuf.ch(c)])
    slot, wv = piece(0, 1024)
    for c in range(6):
        ps = fm(slot, wv, c * 128)
        u = self.u_[c % 2]
        k.op('act', lambda e, ps=ps, u=u, c=c: e.activation(u[:, :T], ps[:, :T], AF.Identity, bias=self.bC[:, c:c + 1]), [ps, self.bC], [u])
        gelu(self, gc[:, c, :T], u[:, :T], T, u, gc.ch(c))
    for c in range(2):
        cr_evac(fm(slot, wv, 768 + c * 128), c)
    slot, wv = piece(1024, 1792)
    for c in range(2, 6):
        cr_evac(fm(slot, wv, (c - 2) * 128), c)
    for j in range(2):
        ps = fm(slot, wv, 512 + j * 128)
        k.op('act', lambda e, ps=ps, j=j: e.activation(duf[:, j, :T], ps[:, :T], AF.Identity, bias=self.bC[:, 12 + j:13 + j]), [ps, self.bC], [duf.ch(j)])
        k.op('dve', lambda e, j=j: e.tensor_copy(dub[:, j, :T], duf[:, j, :T]), [duf.ch(j)], [dub.ch(j)])
    allb = list(self.banks)
    self.banks = allb[0:2]
    k.begin_thread()
    for si, (s0, ln, sq_) in enumerate(segs):
        cb = cbase[si]
        sl = slice(s0, s0 + ln)
        if sq_ is not None:
            k.dma(self.hst[:], self.st_rh[sq_].rearrange("(c p) -> p c", p=128), [], [self.hst], allow_slow_non_contiguous=True)
        for c in range(6):
            u, ub, r, ig, a, hs = self.u_[c % 2], self.ub_[0], self.r_[0], self.ig_[0], self.a_[0], self.hs_[0]
            k.op('dve', lambda e, c=c, u=u, cb=cb, ln=ln: e.tensor_scalar(u[:, :ln], crbuf[:, c, cb + 3:cb + 3 + ln], self.convw[:, 3, c:c + 1], self.convb[:, c:c + 1], ALU.mult, ALU.add),
                 [crbuf.ch(c), self.convw, self.convb], [u])
            for j in range(3):
                k.op('dve', lambda e, c=c, u=u, cb=cb, ln=ln, j=j: e.scalar_tensor_tensor(u[:, :ln], crbuf[:, c, cb + j:cb + j + ln], self.convw[:, j, c:c + 1], u[:, :ln], ALU.mult, ALU.add),
                     [crbuf.ch(c), self.convw, u], [u])
            k.op('act', lambda e, u=u, ub=ub, ln=ln: e.activation(ub[:, :ln], u[:, :ln], AF.Copy), [u], [ub])
            pr, pi = self.bank(), self.bank()
            k.op('pe', lambda e, pr=pr, ub=ub, c=c, ln=ln: e.matmul(pr[:, :ln], self.rgA[:, c, :], ub[:, :ln], start=True, stop=True), [ub, self.rgA], [pr])
            k.op('pe', lambda e, pi=pi, ub=ub, c=c, ln=ln: e.matmul(pi[:, :ln], self.rgX[:, c, :], ub[:, :ln], start=True, stop=True), [ub, self.rgX], [pi])
            k.op('act', lambda e, pr=pr, r=r, c=c, ln=ln: e.activation(r[:, :ln], pr[:, :ln], AF.Sigmoid, bias=self.rgba[:, c:c + 1]), [pr, self.rgba], [r])
            k.op('act', lambda e, pi=pi, ig=ig, c=c, ln=ln: e.activation(ig[:, :ln], pi[:, :ln], AF.Sigmoid, bias=self.rgbx[:, c:c + 1]), [pi, self.rgbx], [ig])
            k.op('act', lambda e, r=r, a=a, c=c, ln=ln: e.activation(a[:, :ln], r[:, :ln], AF.Exp, scale=self.ccol[:, c:c + 1]), [r, self.ccol], [a])
            k.op('act', lambda e, r=r, c=c, ln=ln: e.activation(r[:, :ln], r[:, :ln], AF.Exp, scale=self.c2col[:, c:c + 1]), [r, self.c2col], [r])
            k.op('act', lambda e, r=r, ln=ln: e.activation(r[:, :ln], r[:, :ln], AF.Sqrt, bias=self.onec[:, 0:1], scale=-1.0), [r, self.onec], [r])
            k.op('dve', lambda e, ig=ig, u=u, ln=ln: e.tensor_tensor(ig[:, :ln], ig[:, :ln], u[:, :ln], ALU.mult), [ig, u], [ig])
            k.op('dve', lambda e, ig=ig, r=r, ln=ln: e.tensor_tensor(ig[:, :ln], ig[:, :ln], r[:, :ln], ALU.mult), [ig, r], [ig])
            k.op('dve', lambda e, a=a, ig=ig, hs=hs, c=c, ln=ln: e.tensor_tensor_scan(hs[:, :ln], a[:, :ln], ig[:, :ln], self.hst[:, c:c + 1], ALU.mult, ALU.add),
                 [a, ig, self.hst], [hs])
            k.op('dve', lambda e, hs=hs, c=c, ln=ln: e.tensor_copy(self.hst[:, c:c + 1], hs[:, ln - 1:ln]), [hs], [self.hst])
            k.op('dve', lambda e, hs=hs, c=c, sl=sl, ln=ln: e.tensor_tensor(mixT[:, c, sl], gc[:, c, sl], hs[:, :ln], ALU.mult), [gc.ch(c), hs], [mixT.ch(c)])
        if sq_ is not None:
            k.dma(self.s_rh[sq_].rearrange("(c p) -> p c", p=128), self.hst[:], [self.hst], [self.yp.d(1)], is_out=True, acc=True, allow_slow_non_contiguous=True)
            for c in range(6):
                k.dma(self.s_rc[sq_, :, c * 128:(c + 1) * 128].rearrange("j p -> p j"), crbuf[:, c, cb + ln:cb + ln + 3], [crbuf], [self.yp.d(1)],
                      is_out=True, acc=True, allow_slow_non_contiguous=True)
        else:
            k.op('dve', lambda e, cb=cb, ln=ln: e.tensor_copy(self.crprev[:], crbuf[:, :, cb + ln:cb + ln + 3]), [crbuf], [self.crprev])
            if last:
                k.dma(self.p_rh[:].rearrange("(c p) -> p c", p=128), self.hst[:], [self.hst], [self.yp.d(1)], is_out=True, acc=True, allow_slow_non_contiguous=True)
                for c in range(6):
                    k.dma(self.p_rc[:, c * 128:(c + 1) * 128].rearrange("j p -> p j"), self.crprev[:, c, :], [self.crprev], [self.yp.d(1)],
                          is_out=True, acc=True, allow_slow_non_contiguous=True)
    thr_rg = k.end_thread()
    nb_save = allb
    ybank = [allb[6], allb[7]]
    self.banks = allb[2:6]
    k.begin_thread()
    Ec, Es = self.Ec, self.Es
    blocks = []
    stre, stim = [], []
    for si, (s0, ln, sq_) in enumerate(segs):
        for a0 in range(0, ln, 128):
            blocks.append((s0 + a0, min(128, ln - a0), si, a0 == 0, a0 + 128 >= ln))
        if sq_ is None:
            stre.append(self.sre); stim.append(self.sim)
        else:
            stre.append(self.sreS[si]); stim.append(self.simS[si])
            k.dma(self.sreS[si][:], self.st_sre[sq_].rearrange("(k g) p -> (g p) k", g=2), [], [self.sreS[si]], allow_slow_non_contiguous=True)
            k.dma(self.simS[si][:], self.st_sim[sq_].rearrange("(k g) p -> (g p) k", g=2), [], [self.simS[si]], allow_slow_non_contiguous=True)
    for kk in range(8):
        kc = kk // 4
        par = kk % 2
        vre, vim, wre, wim, t1, t2, sreb, simb = self.vre[0], self.vim[0], self.wre[0], self.wim[0], self.t1[0], self.t2[0], self.sreb[0], self.simb[0]
        pre, pim = self.bank(), self.bank()
        rtile = self.rt[par]
        k.op('dve', lambda e, rtile=rtile, kk=kk: e.tensor_scalar(rtile[:, :], self.cst[:, self.C_ONES:self.C_ONES + 128], self.rho[:, kk:kk + 1], None, ALU.mult),
             [self.cst, self.rho], [rtile])
        k.op('pe', lambda e, pre=pre, kk=kk, kc=kc: e.matmul(pre[:, :T], self.BBre[:, kk, :], dub[:, kc, :T], start=True, stop=True), [self.BBre, dub.ch(kc)], [pre])
        k.op('pe', lambda e, pim=pim, kk=kk, kc=kc: e.matmul(pim[:, :T], self.BBim[:, kk, :], dub[:, kc, :T], start=True, stop=True), [self.BBim, dub.ch(kc)], [pim])
        cc = self.cc4
        for si, (s0, ln, sq_) in enumerate(segs):
            SRE, SIM = stre[si], stim[si]
            L = min(128, ln)
            nb = max(1, ln // 128)
            sl = slice(s0, s0 + ln)

            def bc(tab, kk=kk, L=L, nb=nb):
                a = tab[:, kk, 0:L]
                return bass.AP(a.tensor, a.offset, [list(a.ap[0]), [0, nb], list(a.ap[1])])

            def vw(buf, sl=sl, nb=nb):
                return buf[:, sl].rearrange("p (b l) -> p b l", b=nb)
            ecb, esb = bc(Ec), bc(Es)
            k.op('dve', lambda e, t1=t1, pre=pre, ecb=ecb, vw=vw: e.tensor_tensor(vw(t1), vw(pre), ecb, ALU.mult), [pre, Ec], [t1])
            k.op('dve', lambda e, t2=t2, pim=pim, esb=esb, vw=vw: e.tensor_tensor(vw(t2), vw(pim), esb, ALU.mult), [pim, Es], [t2])
            k.op('dve', lambda e, vre=vre, t1=t1, t2=t2, sl=sl: e.tensor_tensor(vre[:, sl], t1[:, sl], t2[:, sl], ALU.add), [t1, t2], [vre])
            k.op('dve', lambda e, t1=t1, pim=pim, ecb=ecb, vw=vw: e.tensor_tensor(vw(t1), vw(pim), ecb, ALU.mult), [pim, Ec], [t1])
            k.op('dve', lambda e, t2=t2, pre=pre, esb=esb, vw=vw: e.tensor_tensor(vw(t2), vw(pre), esb, ALU.mult), [pre, Es], [t2])
            k.op('dve', lambda e, vim=vim, t1=t1, t2=t2, sl=sl: e.tensor_tensor(vim[:, sl], t1[:, sl], t2[:, sl], ALU.subtract), [t1, t2], [vim])
            ecl, esl = Ec[:, kk, L - 1:L], Es[:, kk, L - 1:L]
            for bi_ in range(nb):
                b0 = s0 + bi_ * L
                bs = slice(b0, b0 + L)
                rtb = rtile[:, 0:L]
                k.op('dve', lambda e, wre=wre, vre=vre, bs=bs, rtb=rtb, kk=kk, SRE=SRE: e.tensor_tensor_scan(wre[:, bs], rtb, vre[:, bs], SRE[:, kk:kk + 1], ALU.mult, ALU.add),
                     [vre, rtile, SRE], [wre])
                k.op('dve', lambda e, wim=wim, vim=vim, bs=bs, rtb=rtb, kk=kk, SIM=SIM: e.tensor_tensor_scan(wim[:, bs], rtb, vim[:, bs], SIM[:, kk:kk + 1], ALU.mult, ALU.add),
                     [vim, rtile, SIM], [wim])
                wrl, wil = wre[:, b0 + L - 1:b0 + L], wim[:, b0 + L - 1:b0 + L]
                k.op('dve', lambda e, wil=wil, esl=esl, kk=kk: e.tensor_tensor(cc[:, kk, 0:1], wil, esl, ALU.mult), [wim, Es], [cc])
                k.op('dve', lambda e, wil=wil, ecl=ecl, kk=kk: e.tensor_tensor(cc[:, kk, 1:2], wil, ecl, ALU.mult), [wim, Ec], [cc])
                k.op('dve', lambda e, wrl=wrl, ecl=ecl, kk=kk, SRE=SRE: e.scalar_tensor_tensor(SRE[:, kk:kk + 1], wrl, ecl, cc[:, kk, 0:1], ALU.mult, ALU.subtract),
                     [wre, Ec, cc], [SRE])
                k.op('dve', lambda e, wrl=wrl, esl=esl, kk=kk, SIM=SIM: e.scalar_tensor_tensor(SIM[:, kk:kk + 1], wrl, esl, cc[:, kk, 1:2], ALU.mult, ALU.add),
                     [wre, Es, cc], [SIM])
            k.op('dve', lambda e, t1=t1, wre=wre, ecb=ecb, vw=vw: e.tensor_tensor(vw(t1), vw(wre), ecb, ALU.mult), [wre, Ec], [t1])
            k.op('dve', lambda e, t2=t2, wim=wim, esb=esb, vw=vw: e.tensor_tensor(vw(t2), vw(wim), esb, ALU.mult), [wim, Es], [t2])
            k.op('dve', lambda e, vre=vre, t1=t1, t2=t2, sl=sl: e.tensor_tensor(vre[:, sl], t1[:, sl], t2[:, sl], ALU.subtract), [t1, t2], [vre])
            k.op('dve', lambda e, t1=t1, wre=wre, esb=esb, vw=vw: e.tensor_tensor(vw(t1), vw(wre), esb, ALU.mult), [wre, Es], [t1])
            k.op('dve', lambda e, t2=t2, wim=wim, ecb=ecb, vw=vw: e.tensor_tensor(vw(t2), vw(wim), ecb, ALU.mult), [wim, Ec], [t2])
            k.op('dve', lambda e, vim=vim, t1=t1, t2=t2, sl=sl: e.tensor_tensor(vim[:, sl], t1[:, sl], t2[:, sl], ALU.add), [t1, t2], [vim])
        k.op('act', lambda e, vre=vre, sreb=sreb: e.activation(sreb[:, :T], vre[:, :T], AF.Copy), [vre], [sreb])
        k.op('act', lambda e, vim=vim, simb=simb: e.activation(simb[:, :T], vim[:, :T], AF.Copy), [vim], [simb])
        yb = ybank[kc]
        k4 = kk % 4
        k.op('pe', lambda e, yb=yb, kk=kk, sreb=sreb, k4=k4: e.matmul(yb[:, :T], self.CTre[:, kk, :], sreb[:, :T], start=(k4 == 0), stop=False), [sreb, self.CTre], [yb])
        k.op('pe', lambda e, yb=yb, kk=kk, simb=simb, k4=k4: e.matmul(yb[:, :T], self.CTimn[:, kk, :], simb[:, :T], start=False, stop=(k4 == 3)), [simb, self.CTimn], [yb])
    self.banks = nb_save
    for si, (s0, ln, sq_) in enumerate(segs):
        if sq_ is not None:
            k.dma(self.s_sre[sq_].rearrange("(k g) p -> (g p) k", g=2), stre[si][:], [stre[si]], [self.yp.d(1)], is_out=True, acc=True, allow_slow_non_contiguous=True)
            k.dma(self.s_sim[sq_].rearrange("(k g) p -> (g p) k", g=2), stim[si][:], [stim[si]], [self.yp.d(1)], is_out=True, acc=True, allow_slow_non_contiguous=True)
        elif last:
            k.dma(self.p_sre[:].rearrange("(k g) p -> (g p) k", g=2), self.sre[:], [self.sre], [self.yp.d(1)], is_out=True, acc=True, allow_slow_non_contiguous=True)
            k.dma(self.p_sim[:].rearrange("(k g) p -> (g p) k", g=2), self.sim[:], [self.sim], [self.yp.d(1)], is_out=True, acc=True, allow_slow_non_contiguous=True)
    thr_s5 = k.end_thread()
    k.merge(thr_rg, thr_s5)
    yf, ygb = self.yf, self.ygb
    for j in range(2):
        yb = ybank[j]
        k.op('dve', lambda e, yb=yb, j=j: e.scalar_tensor_tensor(yf[:, j, :T], duf[:, j, :T], self.dS5[:, j:j + 1], yb[:, :T], ALU.mult, ALU.add),
             [yb, duf.ch(j), self.dS5], [yf.ch(j)])
        u = self.u_[j]
        k.op('dve', lambda e, u=u, j=j: e.tensor_copy(u[:, :T], yf[:, j, :T]), [yf.ch(j)], [u])
        gelu(self, yf[:, j, :T], u[:, :T], T, u, yf.ch(j))
        k.op('act', lambda e, j=j: e.activation(ygb[:, j, :T], yf[:, j, :T], AF.Copy), [yf.ch(j)], [ygb.ch(j)])
    for o in range(2):
        ps = self.bank()
        self.mm(ps, ps[:, :T], [(self.wglu[:, j, o * 128:(o + 1) * 128], ygb[:, j, :T]) for j in range(2)], [self.wglu, ygb])
        g2 = self.g2
        k.op('act', lambda e, ps=ps, o=o: e.activation(g2[:, :T], ps[:, :T], AF.Sigmoid, bias=self.bglu[:, o:o + 1]), [ps, self.bglu], [g2])
        k.op('dve', lambda e, o=o: e.tensor_tensor(mixT[:, 6 + o, :T], yf[:, o, :T], g2[:, :T], ALU.mult), [yf.ch(o), g2], [mixT.ch(6 + o)])


def l1_zero_state(self):
    k = self.k
    for b in (self.hst, self.crprev, self.sre, self.sim):
        k.op('pool', lambda e, b=b: e.memset(b[:], 0.0), [], [b])

import numpy as np
import ml_dtypes
from concourse.bass_utils import run_bass_kernel_spmd

D = 1024
NJ = 8
FF = 2816
NF = 22
EPS = 1e-6
SLOT = 16384

C_ID = 0
C_TRI = 128
C_ONES = 256
C_SEL = 768
C_RST = 1280
C_MNEG = 1792
NCONST = 1792 + 128
CB_ID, CB_TRI, CB_ONES = 0, 128, 256


def make_consts():
    c = np.zeros((128, NCONST), np.float32)
    c[:, C_ID:C_ID + 128] = np.eye(128)
    c[:, C_ONES:C_ONES + 512] = 1.0
    s = np.arange(128)[:, None]
    t = np.arange(128)[None, :]
    c[:, C_TRI:C_TRI + 128] = (s <= t)
    c[:, C_MNEG:C_MNEG + 128] = np.where(s <= t, 0.0, -1e30)
    for h in range(4):
        c[h, C_SEL + h * 128:C_SEL + (h + 1) * 128] = 1.0
    r = np.ones(512, np.float32)
    r[::64] = 0.0
    c[:, C_RST:C_RST + 512] = r[None, :]
    return c


class Builder:
    def __init__(self, seq, do_mix=True, nlayers=2):
        self.seq = seq
        self.nc = bass.Bass("TRN2", target_bir_lowering=False)
        self.k = K(self.nc)
        self.do_mix = do_mix
        self.nlayers = nlayers
        self.bank_rr = 0

    def bank(self):
        b = self.banks[self.bank_rr % len(self.banks)]
        self.bank_rr += 1
        return b

    def din(self, name, shape, dtype=F32):
        return self.k.dram(name, shape, dtype, kind="ExternalInput")

    def dout(self, name, shape, dtype=F32):
        return self.k.dram(name, shape, dtype, kind="ExternalOutput")

    def mm(self, ps, out_ap, pairs, reads):
        n = len(pairs)
        for i, (l, r) in enumerate(pairs):
            self.k.op('pe', (lambda e, l=l, r=r, i=i: e.matmul(out_ap, l, r, start=(i == 0), stop=(i == n - 1))),
                      reads=reads if (i == 0 or i == n - 1) else (), writes=[ps])

    def build(self):
        k, nc = self.k, self.nc
        seq = self.seq
        self.xp = self.din("xp", [seq, D])
        self.xs = self.din("xs", [64, D])
        self.consts_d = self.din("consts", [128, NCONST])
        self.w_gate = self.din("ffn_w_gate", [4, D, FF])
        self.w_up = self.din("ffn_w_up", [4, D, FF])
        self.w_down = self.din("ffn_w_down", [4, FF, D])
        self.norm_g = self.din("norm_g", [7, D])
        self.yp = self.dout("yp", [seq, D])
        self.ys = self.dout("ys", [64, D])
        self.ab_w_in = self.din("ab_w_in", [D, 4104])
        self.ab_b_in = self.din("ab_b_in", [4104])
        self.mlstm_norm_g = self.din("mlstm_norm_g", [512])
        self.hgrn_norm_g = self.din("hgrn_norm_g", [512])
        self.hgrn_lb_logits = self.din("hgrn_lb_logits", [3, 512])
        self.ab_w_out = self.din("ab_w_out", [D, D])
        self.st_mC = self.din("st_mC", [2, 4, 128, 128])
        self.st_mn = self.din("st_mn", [2, 4, 128])
        self.st_mm = self.din("st_mm", [2, 4])
        self.st_hS = self.din("st_hS", [2, 4, 128, 128])
        self.p_mC = self.dout("p_mC", [4, 128, 128])
        self.p_mn = self.dout("p_mn", [4, 128])
        self.p_mm = self.dout("p_mm", [4])
        self.p_hS = self.dout("p_hS", [4, 128, 128])
        self.s_mC = self.dout("s_mC", [2, 4, 128, 128])
        self.s_mn = self.dout("s_mn", [2, 4, 128])
        self.s_mm = self.dout("s_mm", [2, 4])
        self.s_hS = self.dout("s_hS", [2, 4, 128, 128])
        self.cd_w_in = self.din("cd_w_in", [D, 1792])
        self.cd_b_in = self.din("cd_b_in", [1792])
        self.conv_w = self.din("conv_w", [4, 768])
        self.conv_b = self.din("conv_b", [768])
        self.rg_w_a = self.din("rg_w_a", [12, 64, 64])
        self.rg_b_a = self.din("rg_b_a", [768])
        self.rg_w_x = self.din("rg_w_x", [12, 64, 64])
        self.rg_b_x = self.din("rg_b_x", [768])
        self.rg_lambda = self.din("rg_lambda", [768])
        self.s5_A_re = self.din("s5_A_re", [16, 64])
        self.s5_A_im = self.din("s5_A_im", [16, 64])
        self.s5_log_dt = self.din("s5_log_dt", [16])
        self.s5_B_re = self.din("s5_B_re", [16, 64, 16])
        self.s5_B_im = self.din("s5_B_im", [16, 64, 16])
        self.s5_C_re = self.din("s5_C_re", [16, 16, 64])
        self.s5_C_im = self.din("s5_C_im", [16, 16, 64])
        self.s5_D = self.din("s5_D", [256])
        self.s5_w_glu = self.din("s5_w_glu", [256, 256])
        self.s5_b_glu = self.din("s5_b_glu", [256])
        self.cd_w_out = self.din("cd_w_out", [D, D])
        self.st_rh = self.din("st_rh", [2, 768])
        self.st_rc = self.din("st_rc", [2, 3, 768])
        self.st_sre = self.din("st_sre", [2, 16, 64])
        self.st_sim = self.din("st_sim", [2, 16, 64])
        self.p_rh = self.dout("p_rh", [768])
        self.p_rc = self.dout("p_rc", [3, 768])
        self.p_sre = self.dout("p_sre", [16, 64])
        self.p_sim = self.dout("p_sim", [16, 64])
        self.s_rh = self.dout("s_rh", [2, 768])
        self.s_rc = self.dout("s_rc", [2, 3, 768])
        self.s_sre = self.dout("s_sre", [2, 16, 64])
        self.s_sim = self.dout("s_sim", [2, 16, 64])
        self.pieces = self.piece_defs()
        self.pidx = {p[0]: i for i, p in enumerate(self.pieces)}
        self.wsc = k.dram("wsc", [len(self.pieces), 128, 8192], BF16)
        self.cst = k.sb("cst", [128, NCONST], F32)
        self.cstb = k.sb("cstb", [128, 384], BF16)
        self.gcol = k.sb("gcol", [128, 7, NJ], F32)
        self.epsc = k.sb("epsc", [128, 8], F32)
        self.x = k.sb("x", [128, NJ, 512], F32)
        self.xn = k.sb("xn", [128, NJ, 512], BF16)
        self.sq = k.sb("sq", [128, 2, 512], BF16)
        self.rstd = k.sb("rstd", [128, 512], F32)
        self.h = k.sb("h", [128, NF, 512], BF16)
        self.sg = [k.sb("sg%d" % i, [128, 512], F32) for i in range(2)]
        self.xin = [k.sb("xin%d" % i, [128, D], F32, at=self.h.addr + 4096 * i) for i in range(2)]
        self.wslot = [k.sb("wslot%d" % i, [128, SLOT // 2], BF16) for i in range(2)]
        self.wslot_rr = 0
        self.banks = [k.ps("bank%d" % i) for i in range(8)]
        self.arena0 = (k.sb_off + 63) // 64 * 64
        self.ARENA = 70 * 1024
        k.sb_off = self.arena0 + self.ARENA
        self.ident = self.cst[:, C_ID:C_ID + 128]
        self.ones_b = self.cstb[:, CB_ONES:CB_ONES + 128]
        self.ident_b = self.cstb[:, CB_ID:CB_ID + 128]
        self.C_ID, self.C_TRI, self.C_ONES, self.C_SEL, self.C_RST, self.C_MNEG = C_ID, C_TRI, C_ONES, C_SEL, C_RST, C_MNEG
        self.CB_ID, self.CB_TRI, self.CB_ONES = CB_ID, CB_TRI, CB_ONES

        k.dma(self.cst[:], self.consts_d[:], [], [self.cst])
        k.op('dve', lambda e: e.tensor_copy(self.cstb[:], self.cst[:, 0:384]), [self.cst], [self.cstb])
        k.op('dve', lambda e: e.memset(self.epsc[:], EPS), [], [self.epsc])
        k.dma(self.gcol[:], self.norm_g[:].rearrange("n (j p) -> p n j", p=128), [], [self.gcol],
              allow_slow_non_contiguous=True)
        self.cast_weights()
        l0_setup(self)
        assert self.arena0_end <= self.arena0 + self.ARENA, self.arena0_end - self.arena0
        if self.nlayers > 1:
            l1_alloc(self)
            l1_setup(self)
        tiles = [("s", 0, 64)] + [("p", t0, min(512, seq - t0)) for t0 in range(0, seq, 512)]
        for ti, (kind, t0, T) in enumerate(tiles):
            if ti == 1:
                l0_zero_state(self)
                if self.nlayers > 1:
                    l1_zero_state(self)
            self.tile(kind, t0, T, last=(ti == len(tiles) - 1))
        return k.emit()

    def rsqrt(self, out, ps, T, scale):
        k = self.k
        k.op('act', lambda e: e.activation(out[:, :T], ps[:, :T], AF.Sqrt, bias=self.epsc[:, 0:1], scale=scale), [ps, self.epsc], [out])
        k.op('dve', lambda e: e.reciprocal(out[:, :T], out[:, :T]), [out], [out])

    def piece_defs(self):
        P = []

        def ffn(li):
            f0 = 0
            pi = 0
            while f0 < NF:
                nfc = min(4, NF - f0)
                w = nfc * 128
                P.append((("gu", li, pi), [(self.w_gate, li, 8, f0 * 128, f0 * 128 + w, 0), (self.w_up, li, 8, f0 * 128, f0 * 128 + w, 8 * w)], 16 * w))
                f0 += nfc
                pi += 1
            for q4 in range(4):
                P.append((("dn", li, q4), [(self.w_down, li, NF, q4 * 256, (q4 + 1) * 256, 0)], NF * 256))
        ffn(0)
        for i, (c0, c1) in enumerate(((0, 1024), (1024, 2048), (2048, 2568), (2568, 3592), (3592, 4104))):
            P.append((("ab", i), [(self.ab_w_in, None, 8, c0, c1, 0)], 8 * (c1 - c0)))
        P.append((("abo",), [(self.ab_w_out, None, 8, 0, 1024, 0)], 8192))
        ffn(1)
        ffn(2)
        for i, (c0, c1) in enumerate(((0, 1024), (1024, 1792))):
            P.append((("cd", i), [(self.cd_w_in, None, 8, c0, c1, 0)], 8 * (c1 - c0)))
        P.append((("cdo",), [(self.cd_w_out, None, 8, 0, 1024, 0)], 8192))
        ffn(3)
        return P

    def cast_weights(self):
        k = self.k
        A = self.arena0 + 20480
        stf = [k.sb("cstf%d" % i, [128, 4096], F32, at=A + 16384 * i) for i in range(2)]
        stb = [k.sb("cstb_%d" % i, [128, 4096], BF16, at=A + 32768 + 8192 * i) for i in range(2)]
        job = 0
        pend = None
        engs = ['act', 'dve']
        for pi, (key, parts, n) in enumerate(self.pieces):
            for (src, lead, nA, c0, c1, doff) in parts:
                w = c1 - c0
                g = max(1, 4096 // w)
                for a0 in range(0, nA, g):
                    na = min(g, nA - a0)
                    sf, sb_ = stf[job % 2], stb[job % 2]
                    sap = (src[lead, a0 * 128:(a0 + na) * 128, c0:c1] if lead is not None else src[a0 * 128:(a0 + na) * 128, c0:c1])
                    k.dma(sf[:, 0:na * w].rearrange("p (a c) -> p a c", a=na), sap.rearrange("(a p) c -> p a c", p=128), [], [sf])
                    eng = engs[job % 2]
                    if eng == 'act':
                        k.op('act', lambda e, sf=sf, sb_=sb_, m=na * w: e.activation(sb_[:, 0:m], sf[:, 0:m], AF.Copy), [sf], [sb_])
                    else:
                        k.op(eng, lambda e, sf=sf, sb_=sb_, m=na * w: e.tensor_copy(sb_[:, 0:m], sf[:, 0:m]), [sf], [sb_])
                    if pend is not None:
                        k.dma(pend[0], pend[1], [pend[2]], [pend[3]], eng='sp', acc=True)
                    pend = (self.wsc[pi, :, doff + a0 * w:doff + (a0 + na) * w], sb_[:, 0:na * w], sb_, self.wsc.d(pi))
                    job += 1

        if pend is not None:
            k.dma(pend[0], pend[1], [pend[2]], [pend[3]], eng='sp', acc=True)

    def wload(self, key):
        pi = self.pidx[key]
        n = self.pieces[pi][2]
        slot = self.wslot[self.wslot_rr % 2]
        self.wslot_rr += 1
        self.k.dma(slot[:, 0:n], self.wsc[pi, :, 0:n], [self.wsc.d(pi)], [slot])
        return slot

    def rmsnorm(self, T, gi):
        k = self.k
        x, xn, sq = self.x, self.xn, self.sq
        ps = self.bank()
        for j in range(NJ):
            k.op('act', lambda e, j=j: e.activation(sq[:, j % 2, :T], x[:, j, :T], AF.Square), [x.ch(j)], [sq.ch(j % 2)])
            k.op('pe', lambda e, j=j, ps=ps: e.matmul(ps[:, :T], self.ones_b, sq[:, j % 2, :T], start=(j == 0), stop=(j == NJ - 1)),
                 [sq.ch(j % 2), self.cstb], [ps])
        rstd = self.rstd
        self.rsqrt(rstd, ps, T, 1.0 / D)
        for j in range(NJ):
            eng = 'dve'
            k.op(eng, lambda e, j=j: e.scalar_tensor_tensor(xn[:, j, :T], x[:, j, :T], self.gcol[:, gi, j:j + 1],
                                                            rstd[:, :T], ALU.mult, ALU.mult),
                 [x.ch(j), rstd, self.gcol], [xn.ch(j)])

    def ffn(self, T, li, gi):
        k = self.k
        self.rmsnorm(T, gi)
        xn, h, x = self.xn, self.h, self.x
        f0 = 0
        while f0 < NF:
            nfc = min(4, NF - f0)
            w = nfc * 128
            slot = self.wload(("gu", li, f0 // 4))
            gv = slot[:, 0:NJ * w].rearrange("p (j c) -> p j c", j=NJ)
            uv = slot[:, NJ * w:2 * NJ * w].rearrange("p (j c) -> p j c", j=NJ)
            for fi in range(nfc):
                f = f0 + fi
                pg, pu = self.bank(), self.bank()
                self.mm(pg, pg[:, :T], [(gv[:, j, fi * 128:(fi + 1) * 128], xn[:, j, :T]) for j in range(NJ)], [slot, xn])
                self.mm(pu, pu[:, :T], [(uv[:, j, fi * 128:(fi + 1) * 128], xn[:, j, :T]) for j in range(NJ)], [slot, xn])
                sg = self.sg[f % 2]
                k.op('act', lambda e, sg=sg, pg=pg: e.activation(sg[:, :T], pg[:, :T], AF.Silu), [pg], [sg])
                k.op('dve', lambda e, sg=sg, pu=pu, f=f: e.tensor_tensor(h[:, f, :T], sg[:, :T], pu[:, :T], ALU.mult),
                     [sg, pu], [h.ch(f)])
            f0 += nfc
        for q4 in range(4):
            slot = self.wload(("dn", li, q4))
            wv = slot[:, 0:NF * 256].rearrange("p (f c) -> p f c", f=NF)
            for di in range(2):
                d = q4 * 2 + di
                po = self.bank()
                self.mm(po, po[:, :T], [(wv[:, f, di * 128:(di + 1) * 128], h[:, f, :T]) for f in range(NF)], [slot, h])
                k.op('dve', lambda e, po=po, d=d: e.scalar_tensor_tensor(x[:, d, :T], po[:, :T], 0.5, x[:, d, :T],
                                                                        ALU.mult, ALU.add),
                     [po, x.ch(d)], [x.ch(d)])

    def load_x(self, src, t0, T):
        k = self.k
        x = self.x
        nsub = (T + 127) // 128
        for s in range(nsub):
            n = min(128, T - s * 128)
            xin = self.xin[s % 2]
            k.dma(xin[:n, :], src[t0 + s * 128:t0 + s * 128 + n, :], [], [xin])
        raise NotImplementedError

    def tile(self, kind, t0, T, last=False):
        k = self.k
        x = self.x
        src = self.xs if kind == "s" else self.xp
        dst = self.ys if kind == "s" else self.yp
        nsub = (T + 127) // 128
        for s in range(nsub):
            n = min(128, T - s * 128)
            xin = self.xin[s % 2]
            k.dma(xin[:n, :], src[t0 + s * 128:t0 + s * 128 + n, :], [], [xin])
            for j in range(NJ):
                ps = self.bank()
                k.op('pe', lambda e, ps=ps, xin=xin, j=j, n=n: e.matmul(ps[:, :n], xin[:n, j * 128:(j + 1) * 128],
                                                                        self.ident[:n, :n], start=True, stop=True),
                     [xin, self.cst], [ps])
                eng = 'act' if j % 2 == 0 else 'dve'
                if eng == 'act':
                    k.op('act', lambda e, ps=ps, j=j, s=s, n=n: e.activation(x[:, j, s * 128:s * 128 + n], ps[:, :n], AF.Copy),
                         [ps], [x.ch(j)])
                else:
                    k.op('dve', lambda e, ps=ps, j=j, s=s, n=n: e.tensor_copy(x[:, j, s * 128:s * 128 + n], ps[:, :n]),
                         [ps], [x.ch(j)])
        for l in range(self.nlayers):
            self.ffn(T, 2 * l, 3 * l)
            if self.do_mix:
                self.mixer(kind, t0, T, l, last)
            self.ffn(T, 2 * l + 1, 3 * l + 2)
        self.final_out(dst, t0, T)

    def final_out(self, dst, t0, T):
        k = self.k
        x, sq = self.x, self.sq
        ps = self.bank()
        for j in range(NJ):
            k.op('act', lambda e, j=j: e.activation(sq[:, j % 2, :T], x[:, j, :T], AF.Square), [x.ch(j)], [sq.ch(j % 2)])
            k.op('pe', lambda e, j=j, ps=ps: e.matmul(ps[:, :T], self.ones_b, sq[:, j % 2, :T], start=(j == 0), stop=(j == NJ - 1)),
                 [sq.ch(j % 2), self.cstb], [ps])
        rstd = self.rstd
        self.rsqrt(rstd, ps, T, 1.0 / D)
        for j in range(NJ):
            eng = 'dve'
            k.op(eng, lambda e, j=j: e.scalar_tensor_tensor(x[:, j, :T], x[:, j, :T], self.gcol[:, 6, j:j + 1],
                                                            rstd[:, :T], ALU.mult, ALU.mult),
                 [x.ch(j), rstd, self.gcol], [x.ch(j)])
        nsub = (T + 127) // 128
        for s in range(nsub):
            n = min(128, T - s * 128)
            yo = self.xin[s % 2]
            for half in range(2):
                ps = self.bank()
                for jj in range(4):
                    j = half * 4 + jj
                    k.op('pe', lambda e, ps=ps, j=j, jj=jj, s=s, n=n: e.matmul(
                        ps[:n, jj * 128:(jj + 1) * 128], x[:, j, s * 128:s * 128 + n], self.ident, start=True, stop=True),
                        [x.ch(j), self.cst], [ps])
                if half == 0:
                    k.op('act', lambda e, ps=ps, yo=yo, n=n: e.activation(yo[:n, 0:512], ps[:n, :], AF.Copy), [ps], [yo])
                else:
                    k.op('dve', lambda e, ps=ps, yo=yo, n=n: e.tensor_copy(yo[:n, 512:1024], ps[:n, :]), [ps], [yo])
            k.dma(dst[t0 + s * 128:t0 + s * 128 + n, :], yo[:n, :], [yo], [dst.d(0)], is_out=True, acc=True)

    def mixer(self, kind, t0, T, l, last):
        segs = [(0, 32, 0), (32, 32, 1)] if kind == "s" else [(0, T, None)]
        if l == 0:
            l0_mixer(self, kind, t0, T, segs, last)
        else:
            l1_mixer(self, kind, t0, T, segs, last)
            out_proj(self, T, ("cdo",))


SEQ_FULL = 8192
_PER_CORE_IN = ('cd_w_in', 'cd_b_in', 'conv_w', 'conv_b', 'rg_w_a', 'rg_b_a', 'rg_w_x', 'rg_b_x', 'rg_lambda', 's5_A_re', 's5_A_im',
                's5_log_dt', 's5_B_re', 's5_B_im', 's5_C_re', 's5_C_im', 's5_D', 's5_w_glu', 's5_b_glu', 'cd_w_out')


def _in_map(inp, c, seq, consts):
    f = lambda a: np.ascontiguousarray(np.asarray(a, dtype=np.float32))
    ng = np.concatenate([np.asarray(inp['norm_g']).reshape(6, 1024), np.asarray(inp['final_norm_g'])[None]], 0)
    m = dict(xp=f(inp['x_prompt'][c, :seq]), xs=f(np.asarray(inp['x_sample'][2 * c:2 * c + 2]).reshape(64, 1024)), consts=consts,
             ffn_w_gate=f(np.asarray(inp['ffn_w_gate']).reshape(4, 1024, 2816)), ffn_w_up=f(np.asarray(inp['ffn_w_up']).reshape(4, 1024, 2816)),
             ffn_w_down=f(np.asarray(inp['ffn_w_down']).reshape(4, 2816, 1024)), norm_g=f(ng),
             ab_w_in=f(inp['ab_w_in'][0]), ab_b_in=f(inp['ab_b_in'][0]), mlstm_norm_g=f(inp['mlstm_norm_g'][0]), hgrn_norm_g=f(inp['hgrn_norm_g'][0]),
             hgrn_lb_logits=f(inp['hgrn_lb_logits']), ab_w_out=f(inp['ab_w_out'][0]),
             st_mC=f(inp['state_mlstm_C'][0, 2 * c:2 * c + 2]), st_mn=f(inp['state_mlstm_n'][0, 2 * c:2 * c + 2]),
             st_mm=f(inp['state_mlstm_m'][0, 2 * c:2 * c + 2]), st_hS=f(inp['state_hgrn_S'][0, 2 * c:2 * c + 2]),
             st_rh=f(inp['state_rglru_h'][0, 2 * c:2 * c + 2]), st_rc=f(inp['state_rglru_conv'][0, 2 * c:2 * c + 2]),
             st_sre=f(inp['state_s5_re'][0, 2 * c:2 * c + 2]), st_sim=f(inp['state_s5_im'][0, 2 * c:2 * c + 2]))
    for nm in _PER_CORE_IN:
        m[nm] = f(inp[nm][0])
    return m


def kernel(**inputs):
    n = 8
    seq = int(np.asarray(inputs['x_prompt']).shape[1])
    b = Builder(seq, do_mix=True, nlayers=2)
    b.build()
    consts = make_consts()
    in_maps = [_in_map(inputs, c, seq, consts) for c in range(n)]
    res = run_bass_kernel_spmd(b.nc, in_maps, core_ids=list(range(n)))
    R = res.results
    st = lambda nm: np.stack([np.asarray(R[c][nm], dtype=np.float32) for c in range(n)], 0)
    cat = lambda nm: np.concatenate([np.asarray(R[c][nm], dtype=np.float32) for c in range(n)], 0)
    y_prompt = st('yp')
    y_sample = cat('ys').reshape(16, 32, 1024)
    outs = [y_prompt, y_sample]
    for nm in ('p_mC', 'p_mn', 'p_mm', 'p_hS', 'p_rh', 'p_rc', 'p_sre', 'p_sim'):
        outs.append(st(nm)[None])
    for nm in ('s_mC', 's_mn', 's_mm', 's_hS', 's_rh', 's_rc', 's_sre', 's_sim'):
        outs.append(cat(nm)[None])
    return tuple(outs)
```
